# Optimizing a Trainium2 kernel written in Bass

```python
import jax, jax.numpy as jnp
from jax import lax
import numpy as np

D_MODEL = 1024
BATCH = 8
SEQ = 2048
DEPTH = 2
DEC_BATCH = 128
DEC_SEQ = 4
PAST_LEN = 16384
PAGE_SIZE = 128

CONV_DIM = D_MODEL // 2
CONV_WIDTH = 31
RET_DIM = D_MODEL - CONV_DIM
RET_HEADS = 4
RET_HEAD_DIM = RET_DIM // RET_HEADS
RET_CHUNK = 128
D_FF = 4 * D_MODEL
ROPE_BASE = 10000.0
EPS = 1e-6
IN_COLS = 2 * CONV_DIM + 4 * RET_DIM

kernel_name = "hymba_conformerconv_retention_decode_step"


def rmsnorm(x, g):
    xf = x.astype(jnp.float32)
    y = xf * lax.rsqrt(jnp.mean(xf * xf, axis=-1, keepdims=True) + EPS)
    return (y * g.astype(jnp.float32)).astype(x.dtype)


def rotary(t, pos):
    half = t.shape[-1] // 2
    inv = ROPE_BASE ** (-jnp.arange(half, dtype=jnp.float32) / half)
    ang = pos[:, None] * inv[None, :]
    cos, sin = jnp.cos(ang), jnp.sin(ang)
    t1, t2 = t[..., :half], t[..., half:]
    return jnp.concatenate([t1 * cos - t2 * sin, t2 * cos + t1 * sin], axis=-1)


def retention_log_gamma():
    return jnp.log(1.0 - jnp.exp2(-5.0 - jnp.arange(RET_HEADS, dtype=jnp.float32)))


def retention_chunk(state, q, k, v, log_gamma):
    L = q.shape[2]
    idx = jnp.arange(L, dtype=jnp.float32)
    diff = idx[:, None] - idx[None, :]
    decay = jnp.where(diff[None] >= 0,
                      jnp.exp(log_gamma[:, None, None] * jnp.maximum(diff, 0.0)[None]), 0.0)
    scores = jnp.einsum('bhid,bhjd->bhij', q, k) * decay
    inner = jnp.einsum('bhij,bhjv->bhiv', scores, v)
    cross = jnp.einsum('bhid,bhdv->bhiv', q, state) * jnp.exp(log_gamma[:, None] * (idx + 1.0))[None, :, :, None]
    k_dec = k * jnp.exp(log_gamma[:, None] * (L - 1.0 - idx))[None, :, :, None]
    new_state = state * jnp.exp(log_gamma * L)[None, :, None, None] + jnp.einsum('bhjd,bhjv->bhdv', k_dec, v)
    return new_state, inner + cross


def retention(q, k, v, state):
    B, H, T, d = q.shape
    C = RET_CHUNK if T % RET_CHUNK == 0 else T
    n = T // C
    log_gamma = retention_log_gamma()

    def to_chunks(t):
        return t.reshape(B, H, n, C, t.shape[-1]).transpose(2, 0, 1, 3, 4)

    def step(s, qkv):
        return retention_chunk(s, qkv[0], qkv[1], qkv[2], log_gamma)

    final, outs = lax.scan(step, state, (to_chunks(q), to_chunks(k), to_chunks(v)))
    o = outs.transpose(1, 2, 0, 3, 4).reshape(B, H, T, v.shape[-1])
    return o, final


def mixer(h, conv_buf, ret_state, pos0, w_in, conv_w, conv_b, conv_ln_g, conv_ln_b, ret_gn_g, w_out):
    B, T, _ = h.shape
    proj = h @ w_in
    c0 = 2 * CONV_DIM
    a, b, q, k, v, g = jnp.split(proj, [CONV_DIM, c0, c0 + RET_DIM, c0 + 2 * RET_DIM, c0 + 3 * RET_DIM], axis=-1)

    glu = a * jax.nn.sigmoid(b)
    xp = jnp.concatenate([conv_buf.astype(glu.dtype), glu], axis=1)
    c = lax.conv_general_dilated(xp, conv_w[:, None, :].astype(xp.dtype), (1,), 'VALID',
                                 dimension_numbers=('NWC', 'WIO', 'NWC'),
                                 feature_group_count=CONV_DIM) + conv_b.astype(xp.dtype)
    new_buf = xp[:, -(CONV_WIDTH - 1):]
    cf = c.astype(jnp.float32)
    mu = jnp.mean(cf, axis=-1, keepdims=True)
    var = jnp.mean(jnp.square(cf - mu), axis=-1, keepdims=True)
    cn = (cf - mu) * lax.rsqrt(var + EPS) * conv_ln_g.astype(jnp.float32) + conv_ln_b.astype(jnp.float32)
    conv_out = jax.nn.silu(cn).astype(h.dtype)

    pos = pos0 + jnp.arange(T, dtype=jnp.float32)

    def heads(t):
        return t.reshape(B, T, RET_HEADS, RET_HEAD_DIM).transpose(0, 2, 1, 3).astype(jnp.float32)

    qh = rotary(heads(q), pos)
    kh = rotary(heads(k), pos) * (RET_HEAD_DIM ** -0.5)
    vh = heads(v)
    o, new_state = retention(qh, kh, vh, ret_state.astype(jnp.float32))
    omu = jnp.mean(o, axis=-1, keepdims=True)
    ovar = jnp.mean(jnp.square(o - omu), axis=-1, keepdims=True)
    on = ((o - omu) * lax.rsqrt(ovar + EPS)).transpose(0, 2, 1, 3).reshape(B, T, RET_DIM)
    ret_out = (on * ret_gn_g.astype(jnp.float32) * jax.nn.silu(g.astype(jnp.float32))).astype(h.dtype)

    y = jnp.concatenate([conv_out, ret_out], axis=-1) @ w_out
    return y, new_buf, new_state.astype(ret_state.dtype)


def trunk(x, conv_bufs, ret_states, pos0, norm1_g, w_in, conv_w, conv_b, conv_ln_g, conv_ln_b,
          ret_gn_g, w_out, norm2_g, w_up, w_down, final_norm_g):
    new_bufs, new_states = [], []
    for l in range(DEPTH):
        y, nb, ns = mixer(rmsnorm(x, norm1_g[l]), conv_bufs[l], ret_states[l], pos0, w_in[l], conv_w[l],
                          conv_b[l], conv_ln_g[l], conv_ln_b[l], ret_gn_g[l], w_out[l])
        x = x + y
        hf = rmsnorm(x, norm2_g[l]) @ w_up[l]
        x = x + jnp.square(jax.nn.relu(hf)) @ w_down[l]
        new_bufs.append(nb)
        new_states.append(ns)
    return rmsnorm(x, final_norm_g), jnp.stack(new_bufs), jnp.stack(new_states)


def setup_inputs(seed: int = 0) -> dict:
    key = jax.random.key(seed)
    ks = jax.random.split(key, 20)
    f32 = jnp.float32
    nrm = lambda k, s, sc: jax.random.normal(k, s, f32) * sc
    return {
        "x_prompt": nrm(ks[0], (BATCH, SEQ, D_MODEL), 1.0),
        "x_sample": nrm(ks[1], (DEC_BATCH, DEC_SEQ, D_MODEL), 1.0),
        "cache_conv": nrm(ks[2], (DEPTH, DEC_BATCH, CONV_WIDTH - 1, CONV_DIM), 0.5),
        "state_ret": nrm(ks[3], (DEPTH, DEC_BATCH, RET_HEADS, RET_HEAD_DIM, RET_HEAD_DIM), 0.5),
        "norm1_g": 1.0 + nrm(ks[4], (DEPTH, D_MODEL), 0.02),
        "w_in": nrm(ks[5], (DEPTH, D_MODEL, IN_COLS), D_MODEL ** -0.5),
        "conv_w": nrm(ks[6], (DEPTH, CONV_WIDTH, CONV_DIM), CONV_WIDTH ** -0.5),
        "conv_b": nrm(ks[7], (DEPTH, CONV_DIM), 0.02),
        "conv_ln_g": 1.0 + nrm(ks[8], (DEPTH, CONV_DIM), 0.02),
        "conv_ln_b": nrm(ks[9], (DEPTH, CONV_DIM), 0.02),
        "ret_gn_g": 1.0 + nrm(ks[10], (DEPTH, RET_DIM), 0.02),
        "w_out": nrm(ks[11], (DEPTH, D_MODEL, D_MODEL), D_MODEL ** -0.5),
        "norm2_g": 1.0 + nrm(ks[12], (DEPTH, D_MODEL), 0.02),
        "w_up": nrm(ks[13], (DEPTH, D_MODEL, D_FF), D_MODEL ** -0.5),
        "w_down": nrm(ks[14], (DEPTH, D_FF, D_MODEL), D_FF ** -0.5),
        "final_norm_g": 1.0 + nrm(ks[15], (D_MODEL,), 0.02),
    }


def reference(x_prompt, x_sample, cache_conv, state_ret, norm1_g, w_in, conv_w, conv_b, conv_ln_g,
              conv_ln_b, ret_gn_g, w_out, norm2_g, w_up, w_down, final_norm_g):
    conv0 = jnp.zeros((DEPTH, x_prompt.shape[0], CONV_WIDTH - 1, CONV_DIM), x_prompt.dtype)
    ret0 = jnp.zeros((DEPTH, x_prompt.shape[0], RET_HEADS, RET_HEAD_DIM, RET_HEAD_DIM), state_ret.dtype)
    y_prompt, new_conv_prompt, new_ret_prompt = trunk(
        x_prompt, conv0, ret0, 0.0, norm1_g, w_in, conv_w, conv_b, conv_ln_g, conv_ln_b,
        ret_gn_g, w_out, norm2_g, w_up, w_down, final_norm_g)
    y_sample, new_conv_sample, new_ret_sample = trunk(
        x_sample, cache_conv, state_ret, float(PAST_LEN), norm1_g, w_in, conv_w, conv_b, conv_ln_g,
        conv_ln_b, ret_gn_g, w_out, norm2_g, w_up, w_down, final_norm_g)
    return (y_prompt, y_sample, new_conv_prompt, new_ret_prompt, new_conv_sample, new_ret_sample)
```

```python
import numpy as np
from contextlib import ExitStack

import concourse.bass as bass
import concourse.mybir as mybir
from concourse.bass_utils import run_bass_kernel_spmd

F32 = mybir.dt.float32
BF16 = mybir.dt.bfloat16
ALU = mybir.AluOpType
AF = mybir.ActivationFunctionType

NCORES = 8
P = 128
D = 1024
KC = 8
DEPTH = 2
TP = 2048
NSEQ = 16
TS = 64
T = TP + TS
CW = 31
CD = 512
HD = 128
NH = 4
DFF = 4096
EPS = 1e-6
PAST = 16384
TTS = [(0, 512), (512, 512), (1024, 512), (1536, 512), (2048, 64)]
NCHK = 17
NSLOT = 4
GAM = [1.0 - 2.0 ** (-5.0 - h) for h in range(NH)]
SAME_ENGINE_WAR = True


def chunks_of(t0, n):
    return range(t0 // 128, (t0 + n + 127) // 128)


class Op:
    __slots__ = ("eng", "fn", "deps", "dma", "sem", "val", "has_dep", "idx")


class Prog:
    ENGS = ("pe", "dve", "act", "pool", "sp")

    def __init__(self):
        self.ops = []
        self.last_w = {}
        self.readers = {}
        self.dma_last = {}
        self.ranges = {}
        self.tops = {}
        self.pending = {}
        self.pdone = {}

    def register(self, name, lo, hi):
        self.ranges[name] = (lo, hi)
        self.tops[name] = {}

    def phase_begin(self, names):
        for nm in names:
            lo, hi = self.ranges[nm]
            pend = {}
            for other, (l2, h2) in self.ranges.items():
                if other in names or other == nm:
                    continue
                if l2 < hi and lo < h2:
                    for o in self.tops[other].values():
                        pend[o.idx] = o
            self.pending[nm] = list(pend.values())
            self.pdone[nm] = set()

    def add(self, eng, fn, R=(), W=(), dma=None):
        o = Op()
        o.eng = eng
        o.fn = fn
        o.dma = dma
        o.sem = None
        o.val = 0
        o.has_dep = False
        o.idx = len(self.ops)
        deps = {}
        me = "dma" if dma else eng

        def consider(d, kind):
            if d is None:
                return
            de = "dma" if d.dma else d.eng
            if de == me and me != "dma":
                if me == "pe":
                    return
                if kind == "war" and not SAME_ENGINE_WAR:
                    return
            deps[d.idx] = d

        for k in tuple(R) + tuple(W):
            nm = k[0] if isinstance(k, tuple) else k
            if nm in self.pending and me not in self.pdone[nm]:
                if me != "dma":
                    self.pdone[nm].add(me)
                for d in self.pending[nm]:
                    consider(d, "raw")
            if nm in self.tops:
                self.tops[nm][dma if dma else eng] = o
        for k in R:
            consider(self.last_w.get(k), "raw")
        for k in W:
            consider(self.last_w.get(k), "waw")
            for r in self.readers.get(k, ()):
                consider(r, "war")
        if dma:
            consider(self.dma_last.get(dma), "waw")
            self.dma_last[dma] = o
        for k in R:
            self.readers.setdefault(k, []).append(o)
        for k in W:
            self.last_w[k] = o
            self.readers[k] = []
        o.deps = list(deps.values())
        for d in o.deps:
            d.has_dep = True
        if dma:
            o.has_dep = True
        self.ops.append(o)
        return o

    def emit(self, nc, es, final_waits_groups):
        eng_sem = {e: es.enter_context(nc.semaphore("sem_" + e)) for e in ("pe", "dve", "act", "pool")}
        cnt = {e: 0 for e in eng_sem}
        dma_sem = {}
        dma_cnt = {}
        for o in self.ops:
            if o.dma:
                if o.dma not in dma_sem:
                    dma_sem[o.dma] = es.enter_context(nc.semaphore("dsem_" + o.dma))
                    dma_cnt[o.dma] = 0
                dma_cnt[o.dma] += 16
                o.sem = dma_sem[o.dma]
                o.val = dma_cnt[o.dma]
            elif o.has_dep:
                cnt[o.eng] += 1
                o.sem = eng_sem[o.eng]
                o.val = cnt[o.eng]
        block = es.enter_context(nc.Block())
        handles = {"pe": block.tensor, "dve": block.vector, "act": block.scalar, "pool": block.gpsimd, "sp": block.sync}
        for e in self.ENGS:
            ops = [o for o in self.ops if o.eng == e]

            def body(eh, ops=ops, e=e):
                waited = {}
                for o in ops:
                    for d in o.deps:
                        key = d.sem.num
                        if waited.get(key, 0) < d.val:
                            eh.wait_ge(d.sem, d.val)
                            waited[key] = d.val
                    ins = o.fn(eh)
                    if o.sem is not None:
                        ins.then_inc(o.sem, 16 if o.dma else 1)
                if e == "sp":
                    for g in final_waits_groups:
                        if g in dma_sem:
                            eh.wait_ge(dma_sem[g], dma_cnt[g])

            handles[e](body)
        self.stats = {e: sum(1 for o in self.ops if o.eng == e) for e in self.ENGS}


def host_consts():
    f32 = np.float32
    i = np.arange(128)
    cols = {}
    cst = []

    def put(name, arr):
        arr = np.asarray(arr, dtype=f32)
        assert arr.shape[0] == 128
        cols[name] = (sum(a.shape[1] for a in cst), arr.shape[1])
        cst.append(arr)

    put("ident", np.eye(128))
    sc = f32(HD ** -0.5)
    for h in range(NH):
        g = np.float64(f32(np.log(f32(1.0) - f32(2.0) ** f32(-5.0 - h))))
        diff = i[None, :] - i[:, None]
        decT = np.where(diff >= 0, np.exp(g * np.maximum(diff, 0)), 0.0) * sc
        put(f"decT{h}", decT)
        i64 = np.arange(64)
        same = (i64[None, :] // 4) == (i64[:, None] // 4)
        dd = (i64[None, :] % 4) - (i64[:, None] % 4)
        decS = np.where(same & (dd >= 0), np.exp(g * np.maximum(dd, 0)), 0.0) * sc
        a = np.zeros((128, 64)); a[:64] = decS
        put(f"decS{h}", a)
        put(f"gq{h}", np.broadcast_to(np.exp(g * (i + 1.0))[None, :], (128, 128)))
        put(f"gqs{h}", np.broadcast_to(np.exp(g * ((i64 % 4) + 1.0))[None, :], (128, 64)))
        put(f"gdec{h}", (np.exp(g * (127.0 - i)) * sc)[:, None])
        md = np.zeros((128, 16))
        for j in range(64):
            md[j, j // 4] = np.exp(g * (3.0 - (j % 4))) * sc
        put(f"maskdec{h}", md)
    cstn = np.concatenate(cst, axis=1).astype(f32)
    half = HD // 2
    inv = (np.float32(10000.0) ** (-(np.arange(half, dtype=f32)) / f32(half))).astype(f32)
    pos = np.concatenate([np.arange(TP, dtype=f32), np.tile(f32(PAST) + np.arange(4, dtype=f32), NSEQ)]).astype(f32)
    ang = (pos[None, :] * inv[:, None]).astype(f32)
    cos = np.cos(ang.astype(np.float64)).astype(f32)
    sin = np.sin(ang.astype(np.float64)).astype(f32)
    rot = np.zeros((128, 2, T), f32)
    rot[:64, 0] = cos; rot[64:, 0] = cos
    rot[:64, 1] = sin; rot[64:, 1] = -sin
    return cstn, cols, rot


def host_vecs(inp):
    cols = {}
    parts = []

    def put(name, arr):
        cols[name] = (sum(a.shape[1] for a in parts), arr.shape[1])
        parts.append(np.ascontiguousarray(arr, dtype=np.float32))

    for l in range(DEPTH):
        put(f"n1g{l}", inp["norm1_g"][l].reshape(KC, 128).T)
        put(f"n2g{l}", inp["norm2_g"][l].reshape(KC, 128).T)
        cw = inp["conv_w"][l].reshape(CW, 4, 128).transpose(2, 1, 0).reshape(128, 4 * CW)
        put(f"cw{l}", cw)
        put(f"cb{l}", inp["conv_b"][l].reshape(4, 128).T)
        put(f"lng{l}", inp["conv_ln_g"][l].reshape(4, 128).T)
        put(f"lnb{l}", inp["conv_ln_b"][l].reshape(4, 128).T)
        put(f"gng{l}", inp["ret_gn_g"][l].reshape(4, 128).T)
    put("fng", inp["final_norm_g"].reshape(KC, 128).T)
    return np.concatenate(parts, axis=1), cols


def build_program(ccols, vcols, NCST, NVEC, dbg=False):
    import os
    STAGE = int(os.environ.get("KSTAGE", "9"))
    NL = int(os.environ.get("KLAYERS", str(DEPTH)))
    KSUB = int(os.environ.get("KSUB", "9"))
    KNH = int(os.environ.get("KHEADS", str(NH)))
    KPART = int(os.environ.get("KPART", "9"))
    KNC = int(os.environ.get("KNC", "99"))
    nc = bass.Bass("TRN2", target_bir_lowering=False)
    pg = Prog()
    es = ExitStack()

    def dram(name, shape, kind):
        return nc.dram_tensor(name, list(shape), F32, kind=kind).ap()

    xin = dram("xin", [T, D], "ExternalInput")
    cache = dram("cache", [DEPTH, NSEQ * 30, CD], "ExternalInput")
    state = dram("state", [DEPTH, NSEQ, NH, HD, HD], "ExternalInput")
    w_in = dram("w_in", [DEPTH, D, 3072], "ExternalInput")
    w_out = dram("w_out", [DEPTH, D, D], "ExternalInput")
    w_up = dram("w_up", [DEPTH, D, DFF], "ExternalInput")
    w_down = dram("w_down", [DEPTH, DFF, D], "ExternalInput")
    cst_d = dram("cst", [128, NCST], "ExternalInput")
    vec_d = dram("vecs", [128, NVEC], "ExternalInput")
    rot_d = dram("rot", [128, 2, T], "ExternalInput")
    y_d = dram("y", [T, D], "ExternalOutput")
    ncp_d = dram("ncp", [DEPTH, 30, CD], "ExternalOutput")
    nrp_d = dram("nrp", [DEPTH, NH, HD, HD], "ExternalOutput")
    ncs_d = dram("ncs", [DEPTH, NSEQ, 30, CD], "ExternalOutput")
    nrs_d = dram("nrs", [DEPTH, NSEQ, NH, HD, HD], "ExternalOutput")

    SB_END = 229376
    off = [24576]

    def sb(name, shape, dt, at=None):
        nbytes = int(np.prod(shape[1:])) * (4 if dt == F32 else 2)
        nbytes = (nbytes + 31) // 32 * 32
        if at is None:
            o = off[0]
            off[0] += nbytes
        else:
            o = at
        assert o + nbytes <= SB_END, (name, o, nbytes, o + nbytes - SB_END)
        t = nc.alloc_sbuf_tensor_at(name, list(shape), dt, offset=o)
        pg.register(name, o, o + nbytes)
        return t, o + nbytes

    xT, _ = sb("xT", [128, KC, T], F32)
    hb_off = off[0]
    hb, _ = sb("hb", [128, KC, T], BF16)
    ring = [sb(f"ring{i}", [128, KC, 128], BF16)[0] for i in range(NSLOT)]
    cst, _ = sb("cst", [128, NCST], F32)
    vec, _ = sb("vec", [128, NVEC], F32)
    identb, _ = sb("identb", [128, 128], BF16)
    onesb, _ = sb("onesb", [128, 128], BF16)
    A0 = off[0]

    sq = [sb(f"sq{i}", [128, KC, 512], BF16, at=A0 + i * 8192)[0] for i in range(2)]
    rstd, _ = sb("rstd", [128, T], F32, at=A0 + 16384)
    NXT = 8
    xtok = [sb(f"xtok{i}", [128, D], F32, at=A0 + 25600 + i * 4096)[0] for i in range(NXT)]
    XTOK_NAMES = [f"xtok{i}" for i in range(NXT)]
    a = A0
    qT, a = sb("qT", [128, T], BF16, at=a)
    kT, a = sb("kT", [128, T], BF16, at=a)
    vT, a = sb("vT", [128, T], BF16, at=a)
    gs, a = sb("gs", [128, T], BF16, at=a)
    retout, a = sb("retout", [128, 4, T], BF16, at=a)
    R_END = a
    qdT, a = sb("qdT", [128, T], BF16, at=a)
    rott, a = sb("rott", [128, 2, T], F32, at=a)
    st32, a = sb("st32", [128, NSEQ, 128], F32, at=a)
    stsb, a = sb("stsb", [128, NSEQ, 128], BF16, at=a)
    qmask, a = sb("qmask", [128, NSEQ, 64], BF16, at=a)
    kmask, a = sb("kmask", [128, NSEQ, 128], BF16, at=a)
    k32a, _ = sb("k32a", [128, 512], F32, at=a - 4096)
    tmpKB, _ = sb("tmpKB", [128, 512], F32, at=a - 2048)
    k32b, a = sb("k32b", [128, 512], F32, at=a)
    k32 = [k32a, k32b]
    tmpA, a = sb("tmpA", [128, 512], F32, at=a)
    tmpB, a = sb("tmpB", [128, 512], F32, at=a)
    s32, a = sb("s32", [128, 128], F32, at=a)
    stb, a = sb("stb", [128, 512], BF16, at=a)
    ktok, a = sb("ktok", [128, 512], BF16, at=a)
    vtok, a = sb("vtok", [128, 512], BF16, at=a)
    vdec, a = sb("vdec", [128, 512], BF16, at=a)
    STb, a = sb("STb", [128, 512], BF16, at=a)
    onb, a = sb("onb", [128, 512], BF16, at=a)
    bnst, a = sb("bnst", [128, 32], F32, at=a)
    mv, a = sb("mv", [128, 16], F32, at=a)
    RET_NAMES = ["qT", "kT", "vT", "gs", "retout", "rott", "st32", "stsb", "qmask", "kmask", "tmpA", "tmpB", "s32", "stb",
                 "qdT", "ktok", "vtok", "vdec", "STb", "onb", "bnst", "mv", "k32a", "k32b", "tmpKB"]
    a = R_END
    glub_p, a = sb("glub_p", [128, 4, TP + 32], BF16, at=a)
    glub_s, a = sb("glub_s", [128, 4, NSEQ, 34], BF16, at=a)
    sig_off = a
    sig, a = sb("sig", [128, T], F32, at=a)
    g32p, a = sb("g32p", [128, 4, 32], F32, at=a)
    g32s, a = sb("g32s", [128, 4, 64], F32, at=a)
    ctok_off = a
    ctok, a = sb("ctok", [128, CD], F32, at=a)
    otok, a = sb("otok", [128, CD], F32, at=a)
    cbb, a = sb("cbb", [128, 4, 512], BF16, at=a)
    mean, a = sb("mean", [128, 512], F32, at=a)
    msq, a = sb("msq", [128, 512], F32, at=a)
    crs, a = sb("crs", [128, 512], F32, at=a)
    CONVA_NAMES = ["glub_p", "glub_s", "sig", "g32p", "g32s", "ctok", "otok"]
    convout, _ = sb("convout", [128, 4, T], BF16, at=A0)
    c32, _ = sb("c32", [128, 4, 512], F32, at=sig_off)
    csq, _ = sb("csq", [128, 4, 512], BF16, at=ctok_off)
    dg, _ = sb("dg", [128, 4 * CW, 128], BF16, at=hb_off)
    CONVB_NAMES = ["convout", "c32", "csq", "dg", "cbb", "mean", "msq", "crs"]
    hid = [sb(f"hid{i}", [128, 8, T], BF16, at=A0 + i * 8 * T * 2)[0] for i in range(2)]
    rtmp = [sb(f"rtmp{i}", [128, 512], BF16, at=A0 + 2 * 8 * T * 2 + i * 1024)[0] for i in range(2)]
    FFN_NAMES = ["hid0", "hid1", "rtmp0", "rtmp1"]
    ring2 = [sb(f"ringB{i}", [128, KC, 128], BF16, at=A0 + 2 * 8 * T * 2 + 2048 + i * 2048)[0] for i in range(8)]
    NORM_NAMES = ["sq0", "sq1", "rstd"]

    def cs(name, r0=0, r1=128):
        c0, n = ccols[name]
        return cst[r0:r1, c0:c0 + n]

    def vc(name, j=None):
        c0, n = vcols[name]
        if j is None:
            return vec[:, c0:c0 + n]
        return vec[:, c0 + j:c0 + j + 1]

    ident = cs("ident")

    psb = [es.enter_context(nc.psum_tensor(f"ps{i}", [128, 512], F32)) for i in range(8)]
    NB = [0, 1, 2, 4, 5]

    def PK(b):
        return ("ps", b)

    PK6 = [("ps6", i) for i in range(4)]

    def act(out, in_, func, R, W, bias=None, scale=None):
        kw = {}
        if bias is not None:
            kw["bias"] = bias
        if scale is not None:
            kw["scale"] = scale
        return pg.add("act", lambda e: e.activation(out=out, in_=in_, func=func, **kw), R, W)

    def dve(fn, R, W):
        return pg.add("dve", fn, R, W)

    def mm(out, lhsT, rhs, start, stop, R, W):
        return pg.add("pe", lambda e: e.matmul(out, lhsT, rhs, start=start, stop=stop), R, W)

    def tr(out, in_, idn, R, W):
        return pg.add("pe", lambda e: e.transpose(out, in_, idn), R, W)

    def dma(q, out, in_, R, W, group):
        return pg.add(q, lambda e: e.dma_start(out=out, in_=in_), R, W, dma=group)

    ogrp = [0]

    def out_dma(out, in_, R):
        g = f"o{ogrp[0] % 8}"
        ogrp[0] += 1
        return dma("sp", out, in_, R, (), g)

    mmb = [0]

    def next_bank(banks=(0, 1, 2)):
        b = banks[mmb[0] % len(banks)]
        mmb[0] += 1
        return b

    units = []
    for l in range(NL):
        if STAGE >= 1:
            for h in range(KNH):
                for grp in range(4):
                    c0 = 1024 + grp * 512 + h * 128
                    units.append((w_in[l, :, c0:c0 + 128], 8))
        if STAGE >= 2:
            for cc in range(4):
                units.append((w_in[l, :, 512 + cc * 128:512 + (cc + 1) * 128], 8))
                units.append((w_in[l, :, cc * 128:(cc + 1) * 128], 8))
        if STAGE >= 4:
            for m in range(8):
                units.append((w_out[l, :, m * 128:(m + 1) * 128], 8))
        for g in range(4 if STAGE >= 5 else 0):
            for f in range(8):
                c0 = (g * 8 + f) * 128
                units.append((w_up[l, :, c0:c0 + 128], 8))
            for m in range(8):
                units.append((w_down[l, g * 1024:(g + 1) * 1024, m * 128:(m + 1) * 128], 8))
    TAIL = STAGE >= 5 and NL == DEPTH
    if TAIL:
        tail_units = units[-8:]
        units = units[:-8]
    ucur = [0]
    uload = [0]

    def load_units_upto(i):
        while uload[0] <= i and uload[0] < len(units):
            u = uload[0]
            ap, kcn = units[u]
            s = u % NSLOT
            dma("pool", ring[s][:, 0:kcn, :], ap.rearrange("(kc p) n -> p kc n", p=128), (), [(f"ring{s}",)], f"w{s}")
            uload[0] += 1

    def next_unit(look=NSLOT - 1):
        u = ucur[0]
        ucur[0] += 1
        load_units_upto(u + look)
        return u % NSLOT, units[u][1]

    def proj(rhs_fn, rhs_keys, evac, banks=(0, 1, 2)):
        s, kcn = next_unit()
        for tt, (t0, n) in enumerate(TTS):
            b = next_bank(banks)
            for kc in range(kcn):
                mm(psb[b][:, 0:n], ring[s][:, kc, :], rhs_fn(kc, t0, n), kc == 0, kc == kcn - 1,
                   [(f"ring{s}",)] + rhs_keys(kc, t0, n), [PK(b)])
            evac(tt, t0, n, b)

    def proj2(rhs_fn, rhs_keys, evac_a, evac_b, banks=(0, 1, 2)):
        sa, kcn = next_unit(NSLOT - 1)
        sb_, _ = next_unit(NSLOT - 2)
        for tt, (t0, n) in enumerate(TTS):
            for s_, ev in ((sa, evac_a), (sb_, evac_b)):
                b = next_bank(banks)
                for kc in range(kcn):
                    mm(psb[b][:, 0:n], ring[s_][:, kc, :], rhs_fn(kc, t0, n), kc == 0, kc == kcn - 1,
                       [(f"ring{s_}",)] + rhs_keys(kc, t0, n), [PK(b)])
                ev(tt, t0, n, b)

    def tkeys(name, idx, t0, n):
        return [(name, idx, c) for c in chunks_of(t0, n)]

    dma("sp", cst[:, :], cst_d, (), ["cst"], "c0")
    dma("sp", vec[:, :], vec_d, (), ["vec"], "c1")
    dve(lambda e: e.tensor_copy(out=identb[:, :], in_=ident), ["cst"], ["identb"])
    dve(lambda e: e.memset(onesb[:, :], 1.0), [], ["onesb"])

    pg.phase_begin(XTOK_NAMES)
    for i in range(NCHK):
        r0 = i * 128
        rows = 128 if i < 16 else 64
        s = i % NXT
        dma("sp", xtok[s][0:rows, :], xin[r0:r0 + rows, :], (), [(f"xtok{s}",)], f"x{s}")
        for half in range(2):
            b = next_bank((0, 1, 2, 4))
            for cq in range(4):
                c = half * 4 + cq
                tr(psb[b][:, cq * 128:cq * 128 + rows], xtok[s][0:rows, c * 128:(c + 1) * 128], ident[0:rows, 0:rows],
                   [(f"xtok{s}",), "cst"], [PK(b)])
            src = psb[b][:, :].rearrange("p (c t) -> p c t", c=4)[:, :, 0:rows]
            dst = xT[:, half * 4:half * 4 + 4, r0:r0 + rows]
            W = [("xT", c, i) for c in range(half * 4, half * 4 + 4)]
            if (i + half) % 2 == 0:
                act(dst, src, AF.Copy, [PK(b)], W)
            else:
                dve(lambda e, dst=dst, src=src: e.tensor_copy(out=dst, in_=src), [PK(b)], W)

    def rmsnorm_tile(gname, to_hb, tt):
        t0, n = TTS[tt]
        s = tt % 2
        act(sq[s][:, :, 0:n], xT[:, :, t0:t0 + n], AF.Square, [k for c in range(KC) for k in tkeys("xT", c, t0, n)], [(f"sq{s}",)])
        for c in range(KC):
            mm(psb[NB[tt]][:, 0:n], onesb[:, :], sq[s][:, c, 0:n], c == 0, c == KC - 1, [(f"sq{s}",), "onesb"], [PK(NB[tt])])
        r = rstd[:, t0:t0 + n]
        dve(lambda e: e.tensor_scalar(out=r, in0=psb[NB[tt]][:, 0:n], scalar1=1.0 / D, scalar2=EPS,
                                      op0=ALU.mult, op1=ALU.add), [PK(NB[tt])], [("rstd", tt)])
        act(r, r, AF.Sqrt, [("rstd", tt)], [("rstd", tt)])
        dve(lambda e: e.reciprocal(out=r, in_=r), [("rstd", tt)], [("rstd", tt)])
        for c in range(KC):
            src = xT[:, c, t0:t0 + n]
            if to_hb:
                dst = hb[:, c, t0:t0 + n]
                W = tkeys("hb", c, t0, n)
            else:
                dst = src
                W = tkeys("xT", c, t0, n)
            dve(lambda e, dst=dst, src=src, c=c: e.scalar_tensor_tensor(
                out=dst, in0=src, scalar=vc(gname, c), in1=rstd[:, t0:t0 + n], op0=ALU.mult, op1=ALU.mult),
                tkeys("xT", c, t0, n) + [("rstd", tt), "vec"], W)

    def rmsnorm(gname, to_hb):
        pg.phase_begin(NORM_NAMES + (["hb"] if to_hb else []))
        for tt in range(len(TTS)):
            rmsnorm_tile(gname, to_hb, tt)

    def hb_rhs(kc, t0, n):
        return hb[:, kc, t0:t0 + n]

    def hb_keys(kc, t0, n):
        return tkeys("hb", kc, t0, n)

    for l in range(NL):
        if STAGE < 1:
            break
        rmsnorm(f"n1g{l}", True)

        pg.phase_begin(RET_NAMES)
        dma("sp", rott[:, :, :], rot_d, (), ["rott"], "c2")
        dve(lambda e: e.memset(qmask[:, :, :], 0.0), [], ["qmask"])

        def make_head(h):
            g128 = float(np.float32(GAM[h]) ** 128)
            g4 = float(np.float32(GAM[h]) ** 4)
            dma("sp", st32[:, :, :], state[l, :, h, :, :].rearrange("b d v -> d b v"), (), ["st32"] + [("st32", i) for i in range(4)], "c3")
            u0 = ucur[0]
            ucur[0] += 4
            load_units_upto(u0 + 3)
            slots = [(u0 + i) % NSLOT for i in range(4)]
            pool = lambda fn, R, W: pg.add("pool", fn, R, W)

            def p_mm(ui, tt):
                t0, n = TTS[tt]
                b = next_bank((0, 1))
                s_ = slots[ui]
                for kc in range(KC):
                    mm(psb[b][:, 0:n], ring[s_][:, kc, :], hb[:, kc, t0:t0 + n], kc == 0, kc == KC - 1,
                       [(f"ring{s_}",)] + tkeys("hb", kc, t0, n), [PK(b)])
                return b

            def P_q(tt, h=h):
                t0, n = TTS[tt]
                b = p_mm(0, tt)
                ps = psb[b]
                dve(lambda e: e.tensor_tensor(out=tmpA[:, 0:n], in0=ps[:, 0:n], in1=rott[:, 0, t0:t0 + n], op=ALU.mult),
                    [PK(b), "rott"], ["tmpA"])
                dve(lambda e: e.tensor_tensor(out=tmpB[0:64, 0:n], in0=ps[64:128, 0:n], in1=rott[64:128, 1, t0:t0 + n], op=ALU.mult),
                    [PK(b), "rott"], [("tmpB", 0)])
                dve(lambda e: e.tensor_tensor(out=tmpB[64:128, 0:n], in0=ps[0:64, 0:n], in1=rott[0:64, 1, t0:t0 + n], op=ALU.mult),
                    [PK(b), "rott"], [("tmpB", 1)])
                dve(lambda e: e.tensor_tensor(out=tmpA[:, 0:n], in0=tmpA[:, 0:n], in1=tmpB[:, 0:n], op=ALU.add),
                    ["tmpA", ("tmpB", 0), ("tmpB", 1)], ["tmpA"])
                act(qT[:, t0:t0 + n], tmpA[:, 0:n], AF.Copy, ["tmpA"], tkeys("qT", 0, t0, n))
                if tt < 4:
                    g_ap = cs(f"gq{h}")
                    in1 = bass.AP(g_ap.tensor, g_ap.offset, [[g_ap.ap[0][0], 128], [0, 4], [1, 128]])
                    dve(lambda e: e.tensor_tensor(out=qdT[:, t0:t0 + n].rearrange("p (c i) -> p c i", c=4),
                                                  in0=tmpA[:, 0:n].rearrange("p (c i) -> p c i", c=4), in1=in1, op=ALU.mult),
                        ["tmpA", "cst"], tkeys("qdT", 0, t0, n))
                else:
                    dve(lambda e: e.tensor_tensor(out=qdT[:, t0:t0 + n], in0=tmpA[:, 0:n], in1=cs(f"gqs{h}"), op=ALU.mult),
                        ["tmpA", "cst"], tkeys("qdT", 0, t0, n))

            def P_k(tt):
                t0, n = TTS[tt]
                b = p_mm(1, tt)
                ps = psb[b]
                dve(lambda e: e.tensor_tensor(out=k32b[:, 0:n], in0=ps[:, 0:n], in1=rott[:, 0, t0:t0 + n], op=ALU.mult),
                    [PK(b), "rott"], [("k32b",)])
                dve(lambda e: e.tensor_tensor(out=tmpKB[0:64, 0:n], in0=ps[64:128, 0:n], in1=rott[64:128, 1, t0:t0 + n], op=ALU.mult),
                    [PK(b), "rott"], [("tmpKB", 0)])
                dve(lambda e: e.tensor_tensor(out=tmpKB[64:128, 0:n], in0=ps[0:64, 0:n], in1=rott[0:64, 1, t0:t0 + n], op=ALU.mult),
                    [PK(b), "rott"], [("tmpKB", 1)])
                dve(lambda e: e.tensor_tensor(out=kT[:, t0:t0 + n], in0=k32b[:, 0:n], in1=tmpKB[:, 0:n], op=ALU.add),
                    [("k32b",), ("tmpKB", 0), ("tmpKB", 1)], tkeys("kT", 0, t0, n))

            def P_v(tt):
                t0, n = TTS[tt]
                b = p_mm(2, tt)
                act(vT[:, t0:t0 + n], psb[b][:, 0:n], AF.Copy, [PK(b)], tkeys("vT", 0, t0, n))

            def P_g(tt):
                t0, n = TTS[tt]
                b = p_mm(3, tt)
                act(gs[:, t0:t0 + n], psb[b][:, 0:n], AF.Silu, [PK(b)], tkeys("gs", 0, t0, n))

            def chunk_steps(tt, h=h, g128=g128, g4=g4):
                t0, n = TTS[tt]
                smp = tt == 4
                nck = 1 if smp else 4
                w = 64 if smp else 128
                cl = [4 * tt + ci for ci in range(nck)]
                kq = [("qT", 0, c) for c in cl]
                kk = [("kT", 0, c) for c in cl]
                kv = [("vT", 0, c) for c in cl]
                nn = nck * 128
                ob = 5 if tt % 2 == 0 else 2

                def C1():
                    for ci in range(nck):
                        cols = slice(t0 + ci * 128, t0 + ci * 128 + w)
                        mm(psb[3][0:w, ci * 128:(ci + 1) * 128], kT[:, cols], identb[:, :], True, True, kk + ["identb"], [PK(3)])
                    for ci in range(nck):
                        cols = slice(t0 + ci * 128, t0 + ci * 128 + w)
                        mm(psb[7][0:w, ci * 128:(ci + 1) * 128], vT[:, cols], identb[:, :], True, True, kv + ["identb"], [PK(7)])
                    act(ktok[0:w, 0:nn], psb[3][0:w, 0:nn], AF.Copy, [PK(3)], ["ktok"])
                    if not smp:
                        act(vdec[:, 0:nn], psb[7][:, 0:nn], AF.Identity, [PK(7), "cst"], ["vdec"], scale=cs(f"gdec{h}"))
                    act(vtok[0:w, 0:nn], psb[7][0:w, 0:nn], AF.Copy, [PK(7)], ["vtok"])

                def C2():
                    for ci in range(nck):
                        cols = slice(t0 + ci * 128, t0 + ci * 128 + w)
                        mm(psb[4][0:w, ci * 128:ci * 128 + w], kT[:, cols], qT[:, cols], True, True, kk + kq, [PK(4)])
                    if not smp:
                        d_ap = cs(f"decT{h}")
                        dec4 = bass.AP(d_ap.tensor, d_ap.offset, [[d_ap.ap[0][0], 128], [0, 4], [1, 128]])
                        dve(lambda e: e.tensor_tensor(out=STb[:, 0:512].rearrange("p (c i) -> p c i", c=4),
                                                      in0=psb[4][:, 0:512].rearrange("p (c i) -> p c i", c=4), in1=dec4, op=ALU.mult),
                            [PK(4), "cst"], ["STb"])
                    else:
                        dve(lambda e: e.tensor_tensor(out=STb[0:64, 0:64], in0=psb[4][0:64, 0:64], in1=cs(f"decS{h}", 0, 64), op=ALU.mult),
                            [PK(4), "cst"], ["STb"])

                def o_mms(ci):
                    c = cl[ci]
                    cols = slice(t0 + ci * 128, t0 + ci * 128 + 128)
                    oreg = psb[ob][:, ci * 128:(ci + 1) * 128]
                    mm(oreg, STb[:, ci * 128:(ci + 1) * 128], vtok[:, ci * 128:(ci + 1) * 128], True, c == 0, ["STb", "vtok"], [PK(ob)])
                    if c > 0:
                        pv = (ci - 1) % 4
                        mm(oreg, qdT[:, cols], stb[:, pv * 128:(pv + 1) * 128], False, True, [("qdT", 0, c), ("stb", pv)], [PK(ob)])

                def C3():
                    if not smp:
                        for ci in range(nck):
                            mm(psb[6][:, ci * 128:(ci + 1) * 128], ktok[:, ci * 128:(ci + 1) * 128], vdec[:, ci * 128:(ci + 1) * 128], True, True,
                               ["ktok", "vdec"], PK6)
                        o_mms(0)
                        for ci in range(nck):
                            c = cl[ci]
                            ureg = psb[6][:, ci * 128:(ci + 1) * 128]
                            sreg = stb[:, ci * 128:(ci + 1) * 128]
                            if c == 0:
                                dve(lambda e, ureg=ureg, sreg=sreg: e.tensor_copy(out=sreg, in_=ureg), PK6, [("stb", ci)])
                                dve(lambda e, ureg=ureg: e.tensor_copy(out=s32[:, :], in_=ureg), PK6, ["s32"])
                            else:
                                if c < 15:
                                    dve(lambda e, ureg=ureg, sreg=sreg: e.scalar_tensor_tensor(
                                        out=sreg, in0=s32[:, :], scalar=g128, in1=ureg, op0=ALU.mult, op1=ALU.add), PK6 + ["s32"], [("stb", ci)])
                                dve(lambda e, ureg=ureg: e.scalar_tensor_tensor(out=s32[:, :], in0=s32[:, :], scalar=g128, in1=ureg,
                                                                                op0=ALU.mult, op1=ALU.add), PK6 + ["s32"], ["s32"])
                            if c == 15:
                                out_dma(nrp_d[l, h, :, :], s32[:, :], ["s32"])
                    else:
                        mm(psb[ob][0:64, 0:128], STb[0:64, 0:64], vtok[0:64, 0:128], True, False, ["STb", "vtok"], [PK(ob)])
                        qm_dst = bass.AP(qmask, 0, [[NSEQ * 64, 128], [68, NSEQ], [1, 4]])
                        dve(lambda e: e.tensor_copy(out=qm_dst, in_=qdT[:, t0:t0 + 64].rearrange("p (b i) -> p b i", i=4)),
                            [("qdT", 0, 16), "qmask"], ["qmask"])
                        act(stsb[:, :, :], st32[:, :, :], AF.Copy, ["st32"] + [("st32", i) for i in range(4)], ["stsb"])
                        for bq in range(NSEQ):
                            mm(psb[ob][0:64, 0:128], qmask[:, bq, :], stsb[:, bq, :], False, bq == NSEQ - 1, ["qmask", "stsb"], [PK(ob)])

                def C4():
                    if not smp:
                        for ci in range(1, nck):
                            o_mms(ci)
                    else:
                        pg.phase_begin(["kmask"])
                        km_in0 = bass.AP(ktok, 0, [[512, 64], [0, NSEQ], [1, 128]])
                        md = cs(f"maskdec{h}", 0, 64)
                        km_in1 = bass.AP(md.tensor, md.offset, [[md.ap[0][0], 64], [1, NSEQ], [0, 128]])
                        dve(lambda e: e.tensor_tensor(out=kmask[0:64, :, :], in0=km_in0, in1=km_in1, op=ALU.mult), ["ktok", "cst"], ["kmask"])
                        for bq4 in range(4):
                            ub = 6 if bq4 % 2 == 0 else 3
                            ukeys = PK6 if ub == 6 else [PK(3)]
                            for bi in range(4):
                                bq = bq4 * 4 + bi
                                mm(psb[ub][:, bi * 128:(bi + 1) * 128], kmask[0:64, bq, :], vtok[0:64, 0:128], True, True, ["kmask", "vtok"], ukeys)
                            dve(lambda e, bq4=bq4, ub=ub: e.scalar_tensor_tensor(
                                out=st32[:, bq4 * 4:bq4 * 4 + 4, :], in0=st32[:, bq4 * 4:bq4 * 4 + 4, :], scalar=g4,
                                in1=psb[ub][:, :].rearrange("p (b v) -> p b v", b=4), op0=ALU.mult, op1=ALU.add),
                                ukeys + [("st32", bq4)], [("st32", bq4)])
                        out_dma(nrs_d[l, :, h, :, :].rearrange("b d v -> d b v"), st32[:, :, :], ["st32"] + [("st32", i) for i in range(4)])

                def G1():
                    for ci in range(nck):
                        dve(lambda e, ci=ci: e.bn_stats(out=bnst[0:w, ci * 6:ci * 6 + 6], in_=psb[ob][0:w, ci * 128:(ci + 1) * 128]), [PK(ob)], [("bnst", ci)])
                    for ci in range(nck):
                        dve(lambda e, ci=ci: e.bn_aggr(out=mv[0:w, ci * 2:ci * 2 + 2], in_=bnst[0:w, ci * 6:ci * 6 + 6]), [("bnst", ci)], [("mv", ci)])
                    mvk = [("mv", ci) for ci in range(nck)]
                    var_ap = bass.AP(mv, 1, [[16, w], [2, nck]])
                    dve(lambda e: e.tensor_scalar(out=mv[0:w, 8:8 + nck], in0=var_ap, scalar1=EPS, scalar2=1.0, op0=ALU.add, op1=ALU.mult), mvk, ["rs"])
                    act(mv[0:w, 8:8 + nck], mv[0:w, 8:8 + nck], AF.Sqrt, ["rs"], ["rs"])

                def G2():
                    dve(lambda e: e.reciprocal(out=mv[0:w, 12:12 + nck], in_=mv[0:w, 8:8 + nck]), ["rs"], ["rs2"])
                    for ci in range(nck):
                        dve(lambda e, ci=ci: e.tensor_scalar(out=onb[0:w, ci * 128:(ci + 1) * 128], in0=psb[ob][0:w, ci * 128:(ci + 1) * 128],
                                                             scalar1=mv[0:w, 2 * ci:2 * ci + 1], scalar2=mv[0:w, 12 + ci:13 + ci],
                                                             op0=ALU.subtract, op1=ALU.mult), [PK(ob), ("mv", ci), "rs2"], [("onb", ci)])

                def C5():
                    for ci in range(nck):
                        mm(psb[4][:, ci * 128:ci * 128 + w], onb[0:w, ci * 128:(ci + 1) * 128], identb[0:w, 0:w], True, True, [("onb", ci), "identb"], [PK(4)])
                    dve(lambda e, gcol=vc(f"gng{l}", h): e.scalar_tensor_tensor(out=retout[:, h, t0:t0 + n], in0=psb[4][:, 0:n], scalar=gcol,
                                                                                in1=gs[:, t0:t0 + n], op0=ALU.mult, op1=ALU.mult),
                        [PK(4), "vec"] + [("gs", 0, c) for c in cl], [("retout", h, c) for c in cl])

                return [C1, C2, C3, C4, G1, G2, C5]

            return dict(P_q=P_q, P_k=P_k, P_v=P_v, P_g=P_g, chunk_steps=chunk_steps, u0=u0)

        nothing = lambda: None
        ctx = None
        for h in range(KNH):
            if ctx is None:
                ctx = make_head(h)
                pg.phase_begin(["k32a", "tmpKB"])
                for fn in ("P_q", "P_k", "P_v", "P_g"):
                    ctx[fn](0)
                ctx["gdone"] = {0}
            nxt = None
            prev = None
            for tt in range(6):
                cur = ctx["chunk_steps"](tt) if tt < 5 else None
                C1, C2, C3, C4 = cur[0:4] if cur else [nothing] * 4
                G1, G2, G3 = prev[4:7] if prev else [nothing] * 3
                if tt + 1 < 5:
                    pc, pt = ctx, tt + 1
                elif tt == 5 and h + 1 < KNH:
                    G1()
                    G1 = nothing
                    nxt = make_head(h + 1)
                    pg.phase_begin(["k32a", "tmpKB"])
                    pc, pt = nxt, 0
                else:
                    pc = None
                G1()
                C2()
                C1()
                if pc:
                    pc["P_q"](pt)
                G2()
                C3()
                if pc:
                    pc["P_k"](pt)
                C4()
                if pc:
                    pc["P_v"](pt)
                G3()
                if pc:
                    gd = pc.setdefault("gdone", set())
                    for t2 in (pt, pt + 1):
                        if t2 < 5 and t2 not in gd:
                            pc["P_g"](t2)
                            gd.add(t2)
                if tt == 3:
                    load_units_upto(ctx["u0"] + 7)
                prev = cur
            ctx = nxt

        if STAGE < 2:
            continue
        pg.phase_begin(CONVA_NAMES)
        for cc in range(4):
            dve(lambda e, cc=cc: e.memset(glub_p[:, cc, 0:30], 0.0), [], [("glub_p", cc, "z")])
        for rt in range(4):
            dma("sp", ctok[0:120, :], cache[l, rt * 120:(rt + 1) * 120, :], (), ["ctok"], "c4")
            for cc in range(4):
                tr(psb[6][:, cc * 120:(cc + 1) * 120], ctok[0:120, cc * 128:(cc + 1) * 128], ident[0:120, 0:120], ["ctok", "cst"], PK6)
            act(glub_s[:, :, rt * 4:rt * 4 + 4, 0:30], psb[6][:, 0:480].rearrange("p (c s i) -> p c s i", c=4, s=4),
                AF.Copy, PK6, [("glub_s", "c", rt)])
        out_dma(ncs_d[l, :, 0:26, :], cache[l, :, :].rearrange("(b r) c -> b r c", r=30)[:, 4:30, :], [])

        for cc in range(4):
            proj(hb_rhs, hb_keys, lambda tt, t0, n, b: act(sig[:, t0:t0 + n], psb[b][:, 0:n], AF.Sigmoid, [PK(b)], [("sig", tt)]))

            def glu_evac(tt, t0, n, b, cc=cc):
                if tt < 4:
                    dve(lambda e: e.tensor_tensor(out=glub_p[:, cc, 30 + t0:30 + t0 + n], in0=psb[b][:, 0:n], in1=sig[:, t0:t0 + n], op=ALU.mult),
                        [PK(b), ("sig", tt)], [("glub_p", cc, tt)])
                    if tt == 3:
                        dve(lambda e: e.tensor_tensor(out=g32p[:, cc, 0:30], in0=psb[b][:, 482:512], in1=sig[:, TP - 30:TP], op=ALU.mult),
                            [PK(b), ("sig", tt)], [("g32p", cc)])
                else:
                    dve(lambda e: e.tensor_tensor(out=g32s[:, cc, :], in0=psb[b][:, 0:64], in1=sig[:, TP:T], op=ALU.mult),
                        [PK(b), ("sig", tt)], [("g32s", cc)])
                    act(glub_s[:, cc, :, 30:34], g32s[:, cc, :].rearrange("p (s i) -> p s i", i=4), AF.Copy, [("g32s", cc)], [("glub_s", "n", cc)])
            proj(hb_rhs, hb_keys, glu_evac)

        for cc in range(4):
            tr(psb[6][0:30, cc * 128:(cc + 1) * 128], g32p[:, cc, 0:30], ident, [("g32p", cc), "cst"], PK6)
        act(otok[0:30, :], psb[6][0:30, :], AF.Copy, PK6, ["otok"])
        out_dma(ncp_d[l, :, :], otok[0:30, :], ["otok"])
        for cc in range(4):
            tr(psb[6][0:64, cc * 128:(cc + 1) * 128], g32s[:, cc, :], ident, [("g32s", cc), "cst"], PK6)
        act(otok[0:64, :], psb[6][0:64, :], AF.Copy, PK6, ["otok"])
        for bq in range(NSEQ):
            out_dma(ncs_d[l, bq, 26:30, :], otok[bq * 4:bq * 4 + 4, :], ["otok"])

        if STAGE < 3:
            continue
        pg.phase_begin(CONVB_NAMES)
        for cc in range(4):
            c0, _ = vcols[f"cw{l}"]
            in0 = bass.AP(identb, 0, [[128, 128], [0, CW], [1, 128]])
            w_ap = vec[:, c0 + cc * CW:c0 + (cc + 1) * CW]
            in1 = bass.AP(w_ap.tensor, w_ap.offset, [[w_ap.ap[0][0], 128], [1, CW], [0, 128]])
            dve(lambda e, cc=cc, in0=in0, in1=in1: e.tensor_tensor(out=dg[:, cc * CW:(cc + 1) * CW, :], in0=in0, in1=in1, op=ALU.mult),
                ["identb", "vec"], [("dg", cc)])

        for tt, (t0, n) in enumerate(TTS):
            for cc in range(4):
                b = next_bank((0, 1, 2))
                for j in range(CW):
                    if tt < 4:
                        rhs = glub_p[:, cc, t0 + j:t0 + j + 512]
                        outp = psb[b][:, 0:512]
                        R = [("glub_p", cc, tt), ("glub_p", cc, "z" if tt == 0 else tt - 1)]
                    else:
                        rhs = glub_s[:, cc, :, j:j + 4]
                        outp = psb[b][:, 0:64].rearrange("p (s i) -> p s i", i=4)
                        R = [("glub_s", "n", cc)] + [("glub_s", "c", rt) for rt in range(4)]
                    mm(outp, dg[:, cc * CW + j, :], rhs, j == 0, j == CW - 1, R + [("dg", cc)], [PK(b)])
                act(c32[:, cc, 0:n], psb[b][:, 0:n], AF.Identity, [PK(b), "vec"], [("c32", cc)], bias=vc(f"cb{l}", cc))
                act(csq[:, cc, 0:n], psb[b][:, 0:n], AF.Square, [PK(b), "vec"], [("csq", cc)], bias=vc(f"cb{l}", cc))
                dve(lambda e, cc=cc, n=n: e.tensor_copy(out=cbb[:, cc, 0:n], in_=c32[:, cc, 0:n]), [("c32", cc)], [("cbb", cc)])
            for cc in range(4):
                mm(psb[4][:, 0:n], onesb[:, :], cbb[:, cc, 0:n], cc == 0, cc == 3, [("cbb", cc), "onesb"], [PK(4)])
            for cc in range(4):
                mm(psb[5][:, 0:n], onesb[:, :], csq[:, cc, 0:n], cc == 0, cc == 3, [("csq", cc), "onesb"], [PK(5)])
            dve(lambda e, n=n: e.tensor_scalar(out=mean[:, 0:n], in0=psb[4][:, 0:n], scalar1=1.0 / CD, scalar2=0.0, op0=ALU.mult, op1=ALU.add), [PK(4)], ["mean"])
            dve(lambda e, n=n: e.tensor_tensor(out=msq[:, 0:n], in0=mean[:, 0:n], in1=mean[:, 0:n], op=ALU.mult), ["mean"], ["msq"])
            dve(lambda e, n=n: e.scalar_tensor_tensor(out=crs[:, 0:n], in0=psb[5][:, 0:n], scalar=1.0 / CD, in1=msq[:, 0:n],
                                                      op0=ALU.mult, op1=ALU.subtract), [PK(5), "msq"], ["crs"])
            dve(lambda e, n=n: e.tensor_scalar(out=crs[:, 0:n], in0=crs[:, 0:n], scalar1=EPS, scalar2=1.0, op0=ALU.add, op1=ALU.mult), ["crs"], ["crs"])
            act(crs[:, 0:n], crs[:, 0:n], AF.Sqrt, ["crs"], ["crs"])
            dve(lambda e, n=n: e.reciprocal(out=crs[:, 0:n], in_=crs[:, 0:n]), ["crs"], ["crs"])
            for cc in range(4):
                dve(lambda e, cc=cc, n=n: e.tensor_tensor(out=c32[:, cc, 0:n], in0=c32[:, cc, 0:n], in1=mean[:, 0:n], op=ALU.subtract),
                    [("c32", cc), "mean"], [("c32", cc)])
                dve(lambda e, cc=cc, n=n: e.tensor_tensor(out=c32[:, cc, 0:n], in0=c32[:, cc, 0:n], in1=crs[:, 0:n], op=ALU.mult),
                    [("c32", cc), "crs"], [("c32", cc)])
                act(convout[:, cc, t0:t0 + n], c32[:, cc, 0:n], AF.Silu, [("c32", cc), "vec"], tkeys("convout", cc, t0, n),
                    bias=vc(f"lnb{l}", cc), scale=vc(f"lng{l}", cc))

        if STAGE < 4:
            continue
        def mix_rhs(kc, t0, n):
            return convout[:, kc, t0:t0 + n] if kc < 4 else retout[:, kc - 4, t0:t0 + n]

        def mix_keys(kc, t0, n):
            return tkeys("convout", kc, t0, n) if kc < 4 else tkeys("retout", kc - 4, t0, n)

        for m in range(8):
            def res_evac(tt, t0, n, b, m=m):
                dve(lambda e: e.tensor_tensor(out=xT[:, m, t0:t0 + n], in0=psb[b][:, 0:n], in1=xT[:, m, t0:t0 + n], op=ALU.add),
                    [PK(b)] + tkeys("xT", m, t0, n), tkeys("xT", m, t0, n))
            proj(mix_rhs, mix_keys, res_evac)

        if STAGE < 5:
            continue
        rmsnorm(f"n2g{l}", True)
        pg.phase_begin(FFN_NAMES)
        for g in range(4):
            hs = g % 2
            for f in range(8):
                def up_evac(tt, t0, n, b, f=f, hs=hs):
                    rs = (f + tt) % 2
                    act(rtmp[rs][:, 0:n], psb[b][:, 0:n], AF.Relu, [PK(b)], [(f"rtmp{rs}",)])
                    dve(lambda e: e.tensor_tensor(out=hid[hs][:, f, t0:t0 + n], in0=rtmp[rs][:, 0:n], in1=rtmp[rs][:, 0:n], op=ALU.mult),
                        [(f"rtmp{rs}",)], tkeys(f"hid{hs}", f, t0, n))
                proj(hb_rhs, hb_keys, up_evac)
            if TAIL and l == NL - 1 and g == 3:
                break
            for m in range(8):
                def dn_evac(tt, t0, n, b, m=m):
                    dve(lambda e: e.tensor_tensor(out=xT[:, m, t0:t0 + n], in0=psb[b][:, 0:n], in1=xT[:, m, t0:t0 + n], op=ALU.add),
                        [PK(b)] + tkeys("xT", m, t0, n), tkeys("xT", m, t0, n))
                proj(lambda kc, t0, n, hs=hs: hid[hs][:, kc, t0:t0 + n], lambda kc, t0, n, hs=hs: tkeys(f"hid{hs}", kc, t0, n), dn_evac)

    def store_chunk(i):
        r0 = i * 128
        rows = 128 if i < 16 else 64
        s = i % 2
        for half in range(2):
            b = next_bank((0, 1, 2, 4))
            for cq in range(4):
                c = half * 4 + cq
                tr(psb[b][0:rows, cq * 128:(cq + 1) * 128], xT[:, c, r0:r0 + rows], ident, [("xT", c, i), "cst"], [PK(b)])
            dst = xtok[s][0:rows, half * 512:(half + 1) * 512]
            if half == 0:
                act(dst, psb[b][0:rows, :], AF.Copy, [PK(b)], [(f"xtok{s}", half)])
            else:
                dve(lambda e, dst=dst, b=b, rows=rows: e.tensor_copy(out=dst, in_=psb[b][0:rows, :]), [PK(b)], [(f"xtok{s}", half)])
        out_dma(y_d[r0:r0 + rows, :], xtok[s][0:rows, :], [(f"xtok{s}", 0), (f"xtok{s}", 1)])

    def store_tile(tt):
        for i in (range(4 * tt, 4 * tt + 4) if tt < 4 else [16]):
            store_chunk(i)

    if TAIL:
        l = NL - 1
        pg.phase_begin([f"ringB{i}" for i in range(8)])
        for m in range(8):
            ap, kcn = tail_units[m]
            dma("pool", ring2[m][:, :, :], ap.rearrange("(kc p) n -> p kc n", p=128), (), [(f"ringB{m}",)], f"wB{m}")
        pg.phase_begin(NORM_NAMES + XTOK_NAMES[0:2])
        for step in range(7):
            if step < 5:
                t0, n = TTS[step]
                for m in range(8):
                    b = next_bank((0, 1, 2))
                    for kc in range(KC):
                        mm(psb[b][:, 0:n], ring2[m][:, kc, :], hid[1][:, kc, t0:t0 + n], kc == 0, kc == KC - 1,
                           [(f"ringB{m}",)] + tkeys("hid1", kc, t0, n), [PK(b)])
                    dve(lambda e, m=m, b=b, t0=t0, n=n: e.tensor_tensor(out=xT[:, m, t0:t0 + n], in0=psb[b][:, 0:n], in1=xT[:, m, t0:t0 + n], op=ALU.add),
                        [PK(b)] + tkeys("xT", m, t0, n), tkeys("xT", m, t0, n))
            if 1 <= step <= 5:
                rmsnorm_tile("fng", False, step - 1)
            if 2 <= step <= 6:
                store_tile(step - 2)
    else:
        rmsnorm("fng", False)
        pg.phase_begin(XTOK_NAMES[0:2])
        for tt in range(5):
            store_tile(tt)

    pg.emit(nc, es, [f"o{i}" for i in range(8)])
    es.close()
    print("A0", A0, "arena", SB_END - A0, "ops", pg.stats, flush=True)
    return nc, pg


_CACHE = {}


def kernel(**inputs):
    inp = {k: np.asarray(v) for k, v in inputs.items()}
    cstn, ccols, rot = host_consts()
    vecs, vcols = host_vecs(inp)
    key = "prog"
    if key not in _CACHE:
        _CACHE[key] = build_program(ccols, vcols, cstn.shape[1], vecs.shape[1])
    nc, pg = _CACHE[key]
    xp = inp["x_prompt"].astype(np.float32, copy=False)
    xs = inp["x_sample"].astype(np.float32, copy=False)
    in_maps = []
    for c in range(NCORES):
        xin = np.concatenate([xp[c], xs[c * NSEQ:(c + 1) * NSEQ].reshape(TS, D)], axis=0)
        in_maps.append({
            "xin": np.ascontiguousarray(xin),
            "cache": np.ascontiguousarray(inp["cache_conv"][:, c * NSEQ:(c + 1) * NSEQ].reshape(DEPTH, NSEQ * 30, CD)),
            "state": np.ascontiguousarray(inp["state_ret"][:, c * NSEQ:(c + 1) * NSEQ]),
            "w_in": inp["w_in"], "w_out": inp["w_out"], "w_up": inp["w_up"], "w_down": inp["w_down"],
            "cst": cstn, "vecs": vecs, "rot": rot,
        })
    res = run_bass_kernel_spmd(nc, in_maps, core_ids=list(range(NCORES)))
    R = res.results
    y = np.stack([r["y"] for r in R])
    y_prompt = np.ascontiguousarray(y[:, :TP])
    y_sample = np.ascontiguousarray(y[:, TP:].reshape(NCORES * NSEQ, 4, D))
    ncp = np.ascontiguousarray(np.stack([r["ncp"] for r in R], axis=1))
    nrp = np.ascontiguousarray(np.stack([r["nrp"] for r in R], axis=1))
    ncs = np.ascontiguousarray(np.concatenate([r["ncs"] for r in R], axis=1))
    nrs = np.ascontiguousarray(np.concatenate([r["nrs"] for r in R], axis=1))
    return (y_prompt, y_sample, ncp, nrp, ncs, nrs)
```

```python
import numpy as np
from contextlib import ExitStack

import concourse.bass as bass
import concourse.mybir as mybir
from concourse.bass_utils import run_bass_kernel_spmd

F32 = mybir.dt.float32
BF16 = mybir.dt.bfloat16
ALU = mybir.AluOpType
AF = mybir.ActivationFunctionType

NCORES = 8
P = 128
D = 1024
KC = 8
DEPTH = 2
TP = 2048
NSEQ = 16
TS = 64
T = TP + TS
CW = 31
CD = 512
HD = 128
NH = 4
DFF = 4096
EPS = 1e-6
PAST = 16384
TTS = [(0, 512), (512, 512), (1024, 512), (1536, 512), (2048, 64)]
NCHK = 17
NSLOT = 4
GAM = [1.0 - 2.0 ** (-5.0 - h) for h in range(NH)]
SAME_ENGINE_WAR = True


def chunks_of(t0, n):
    return range(t0 // 128, (t0 + n + 127) // 128)


class Op:
    __slots__ = ("eng", "fn", "deps", "dma", "sem", "val", "has_dep", "idx")


class Prog:
    ENGS = ("pe", "dve", "act", "pool", "sp")

    def __init__(self):
        self.ops = []
        self.last_w = {}
        self.readers = {}
        self.dma_last = {}
        self.ranges = {}
        self.tops = {}
        self.pending = {}
        self.pdone = {}

    def register(self, name, lo, hi):
        self.ranges[name] = (lo, hi)
        self.tops[name] = {}

    def phase_begin(self, names):
        for nm in names:
            lo, hi = self.ranges[nm]
            pend = {}
            for other, (l2, h2) in self.ranges.items():
                if other in names or other == nm:
                    continue
                if l2 < hi and lo < h2:
                    for o in self.tops[other].values():
                        pend[o.idx] = o
            self.pending[nm] = list(pend.values())
            self.pdone[nm] = set()

    def add(self, eng, fn, R=(), W=(), dma=None):
        o = Op()
        o.eng = eng
        o.fn = fn
        o.dma = dma
        o.sem = None
        o.val = 0
        o.has_dep = False
        o.idx = len(self.ops)
        deps = {}
        me = "dma" if dma else eng

        def consider(d, kind):
            if d is None:
                return
            de = "dma" if d.dma else d.eng
            if de == me and me != "dma":
                if me == "pe":
                    return
                if kind == "war" and not SAME_ENGINE_WAR:
                    return
            deps[d.idx] = d

        for k in tuple(R) + tuple(W):
            nm = k[0] if isinstance(k, tuple) else k
            if nm in self.pending and me not in self.pdone[nm]:
                if me != "dma":
                    self.pdone[nm].add(me)
                for d in self.pending[nm]:
                    consider(d, "raw")
            if nm in self.tops:
                self.tops[nm][dma if dma else eng] = o
        for k in R:
            consider(self.last_w.get(k), "raw")
        for k in W:
            consider(self.last_w.get(k), "waw")
            for r in self.readers.get(k, ()):
                consider(r, "war")
        if dma:
            consider(self.dma_last.get(dma), "waw")
            self.dma_last[dma] = o
        for k in R:
            self.readers.setdefault(k, []).append(o)
        for k in W:
            self.last_w[k] = o
            self.readers[k] = []
        o.deps = list(deps.values())
        for d in o.deps:
            d.has_dep = True
        if dma:
            o.has_dep = True
        self.ops.append(o)
        return o

    def emit(self, nc, es, final_waits_groups):
        eng_sem = {e: es.enter_context(nc.semaphore("sem_" + e)) for e in ("pe", "dve", "act", "pool")}
        cnt = {e: 0 for e in eng_sem}
        dma_sem = {}
        dma_cnt = {}
        for o in self.ops:
            if o.dma:
                if o.dma not in dma_sem:
                    dma_sem[o.dma] = es.enter_context(nc.semaphore("dsem_" + o.dma))
                    dma_cnt[o.dma] = 0
                dma_cnt[o.dma] += 16
                o.sem = dma_sem[o.dma]
                o.val = dma_cnt[o.dma]
            elif o.has_dep:
                cnt[o.eng] += 1
                o.sem = eng_sem[o.eng]
                o.val = cnt[o.eng]
        block = es.enter_context(nc.Block())
        handles = {"pe": block.tensor, "dve": block.vector, "act": block.scalar, "pool": block.gpsimd, "sp": block.sync}
        for e in self.ENGS:
            ops = [o for o in self.ops if o.eng == e]

            def body(eh, ops=ops, e=e):
                waited = {}
                for o in ops:
                    for d in o.deps:
                        key = d.sem.num
                        if waited.get(key, 0) < d.val:
                            eh.wait_ge(d.sem, d.val)
                            waited[key] = d.val
                    ins = o.fn(eh)
                    if o.sem is not None:
                        ins.then_inc(o.sem, 16 if o.dma else 1)
                if e == "sp":
                    for g in final_waits_groups:
                        if g in dma_sem:
                            eh.wait_ge(dma_sem[g], dma_cnt[g])

            handles[e](body)
        self.stats = {e: sum(1 for o in self.ops if o.eng == e) for e in self.ENGS}


def host_consts():
    f32 = np.float32
    i = np.arange(128)
    cols = {}
    cst = []

    def put(name, arr):
        arr = np.asarray(arr, dtype=f32)
        assert arr.shape[0] == 128
        cols[name] = (sum(a.shape[1] for a in cst), arr.shape[1])
        cst.append(arr)

    put("ident", np.eye(128))
    sc = f32(HD ** -0.5)
    for h in range(NH):
        g = np.float64(f32(np.log(f32(1.0) - f32(2.0) ** f32(-5.0 - h))))
        diff = i[None, :] - i[:, None]
        decT = np.where(diff >= 0, np.exp(g * np.maximum(diff, 0)), 0.0) * sc
        put(f"decT{h}", decT)
        i64 = np.arange(64)
        same = (i64[None, :] // 4) == (i64[:, None] // 4)
        dd = (i64[None, :] % 4) - (i64[:, None] % 4)
        decS = np.where(same & (dd >= 0), np.exp(g * np.maximum(dd, 0)), 0.0) * sc
        a = np.zeros((128, 64)); a[:64] = decS
        put(f"decS{h}", a)
        put(f"gq{h}", np.broadcast_to(np.exp(g * (i + 1.0))[None, :], (128, 128)))
        put(f"gqs{h}", np.broadcast_to(np.exp(g * ((i64 % 4) + 1.0))[None, :], (128, 64)))
        put(f"gdec{h}", (np.exp(g * (127.0 - i)) * sc)[:, None])
        md = np.zeros((128, 16))
        for j in range(64):
            md[j, j // 4] = np.exp(g * (3.0 - (j % 4))) * sc
        put(f"maskdec{h}", md)
    cstn = np.concatenate(cst, axis=1).astype(f32)
    half = HD // 2
    inv = (np.float32(10000.0) ** (-(np.arange(half, dtype=f32)) / f32(half))).astype(f32)
    pos = np.concatenate([np.arange(TP, dtype=f32), np.tile(f32(PAST) + np.arange(4, dtype=f32), NSEQ)]).astype(f32)
    ang = (pos[None, :] * inv[:, None]).astype(f32)
    cos = np.cos(ang.astype(np.float64)).astype(f32)
    sin = np.sin(ang.astype(np.float64)).astype(f32)
    rot = np.zeros((128, 2, T), f32)
    rot[:64, 0] = cos; rot[64:, 0] = cos
    rot[:64, 1] = sin; rot[64:, 1] = -sin
    return cstn, cols, rot


def host_vecs(inp):
    cols = {}
    parts = []

    def put(name, arr):
        cols[name] = (sum(a.shape[1] for a in parts), arr.shape[1])
        parts.append(np.ascontiguousarray(arr, dtype=np.float32))

    for l in range(DEPTH):
        put(f"n1g{l}", inp["norm1_g"][l].reshape(KC, 128).T)
        put(f"n2g{l}", inp["norm2_g"][l].reshape(KC, 128).T)
        cw = inp["conv_w"][l].reshape(CW, 4, 128).transpose(2, 1, 0).reshape(128, 4 * CW)
        put(f"cw{l}", cw)
        put(f"cb{l}", inp["conv_b"][l].reshape(4, 128).T)
        put(f"lng{l}", inp["conv_ln_g"][l].reshape(4, 128).T)
        put(f"lnb{l}", inp["conv_ln_b"][l].reshape(4, 128).T)
        put(f"gng{l}", inp["ret_gn_g"][l].reshape(4, 128).T)
    put("fng", inp["final_norm_g"].reshape(KC, 128).T)
    return np.concatenate(parts, axis=1), cols


def build_program(ccols, vcols, NCST, NVEC, dbg=False):
    import os
    STAGE = int(os.environ.get("KSTAGE", "9"))
    NL = int(os.environ.get("KLAYERS", str(DEPTH)))
    KSUB = int(os.environ.get("KSUB", "9"))
    KNH = int(os.environ.get("KHEADS", str(NH)))
    KPART = int(os.environ.get("KPART", "9"))
    KNC = int(os.environ.get("KNC", "99"))
    nc = bass.Bass("TRN2", target_bir_lowering=False)
    pg = Prog()
    es = ExitStack()

    def dram(name, shape, kind):
        return nc.dram_tensor(name, list(shape), F32, kind=kind).ap()

    xin = dram("xin", [T, D], "ExternalInput")
    cache = dram("cache", [DEPTH, NSEQ * 30, CD], "ExternalInput")
    state = dram("state", [DEPTH, NSEQ, NH, HD, HD], "ExternalInput")
    w_in = dram("w_in", [DEPTH, D, 3072], "ExternalInput")
    w_out = dram("w_out", [DEPTH, D, D], "ExternalInput")
    w_up = dram("w_up", [DEPTH, D, DFF], "ExternalInput")
    w_down = dram("w_down", [DEPTH, DFF, D], "ExternalInput")
    cst_d = dram("cst", [128, NCST], "ExternalInput")
    vec_d = dram("vecs", [128, NVEC], "ExternalInput")
    rot_d = dram("rot", [128, 2, T], "ExternalInput")
    y_d = dram("y", [T, D], "ExternalOutput")
    ncp_d = dram("ncp", [DEPTH, 30, CD], "ExternalOutput")
    nrp_d = dram("nrp", [DEPTH, NH, HD, HD], "ExternalOutput")
    ncs_d = dram("ncs", [DEPTH, NSEQ, 30, CD], "ExternalOutput")
    nrs_d = dram("nrs", [DEPTH, NSEQ, NH, HD, HD], "ExternalOutput")

    SB_END = 229376
    off = [24576]

    def sb(name, shape, dt, at=None):
        nbytes = int(np.prod(shape[1:])) * (4 if dt == F32 else 2)
        nbytes = (nbytes + 31) // 32 * 32
        if at is None:
            o = off[0]
            off[0] += nbytes
        else:
            o = at
        assert o + nbytes <= SB_END, (name, o, nbytes, o + nbytes - SB_END)
        t = nc.alloc_sbuf_tensor_at(name, list(shape), dt, offset=o)
        pg.register(name, o, o + nbytes)
        return t, o + nbytes

    xT, _ = sb("xT", [128, KC, T], F32)
    hb_off = off[0]
    hb, _ = sb("hb", [128, KC, T], BF16)
    ring = [sb(f"ring{i}", [128, KC, 128], BF16)[0] for i in range(NSLOT)]
    cst, _ = sb("cst", [128, NCST], F32)
    vec, _ = sb("vec", [128, NVEC], F32)
    identb, _ = sb("identb", [128, 128], BF16)
    onesb, _ = sb("onesb", [128, 128], BF16)
    epsc, _ = sb("epsc", [128, 8], F32)
    A0 = off[0]

    sq = [sb(f"sq{i}", [128, KC, 512], BF16, at=A0 + i * 8192)[0] for i in range(2)]
    rstd, _ = sb("rstd", [128, T], F32, at=A0 + 16384)
    NXT = 8
    xtok = [sb(f"xtok{i}", [128, D], F32, at=A0 + 25600 + i * 4096)[0] for i in range(NXT)]
    XTOK_NAMES = [f"xtok{i}" for i in range(NXT)]
    a = A0
    qT, a = sb("qT", [128, T], BF16, at=a)
    kT, a = sb("kT", [128, T], BF16, at=a)
    vT, a = sb("vT", [128, T], BF16, at=a)
    gs, a = sb("gs", [128, T], BF16, at=a)
    retout, a = sb("retout", [128, 4, T], BF16, at=a)
    R_END = a
    qdT, a = sb("qdT", [128, T], BF16, at=a)
    rott, a = sb("rott", [128, 2, T], F32, at=a)
    st32, a = sb("st32", [128, NSEQ, 128], F32, at=a)
    stsb, a = sb("stsb", [128, NSEQ, 128], BF16, at=a)
    qmask, a = sb("qmask", [128, NSEQ, 64], BF16, at=a)
    kmask, a = sb("kmask", [128, NSEQ, 128], BF16, at=a)
    k32a, _ = sb("k32a", [128, 512], F32, at=a - 4096)
    tmpKB, _ = sb("tmpKB", [128, 512], F32, at=a - 2048)
    k32b, a = sb("k32b", [128, 512], F32, at=a)
    k32 = [k32a, k32b]
    tmpA, a = sb("tmpA", [128, 512], F32, at=a)
    tmpB, a = sb("tmpB", [128, 512], F32, at=a)
    s32, a = sb("s32", [128, 128], F32, at=a)
    stb, a = sb("stb", [128, 512], BF16, at=a)
    ktok, a = sb("ktok", [128, 512], BF16, at=a)
    vtok, a = sb("vtok", [128, 512], BF16, at=a)
    vdec, a = sb("vdec", [128, 512], BF16, at=a)
    STb, a = sb("STb", [128, 512], BF16, at=a)
    onb, a = sb("onb", [128, 512], BF16, at=a)
    bnst, a = sb("bnst", [128, 32], F32, at=a)
    mv, a = sb("mv", [128, 16], F32, at=a)
    RET_NAMES = ["qT", "kT", "vT", "gs", "retout", "rott", "st32", "stsb", "qmask", "kmask", "tmpA", "tmpB", "s32", "stb",
                 "qdT", "ktok", "vtok", "vdec", "STb", "onb", "bnst", "mv", "k32a", "k32b", "tmpKB"]
    a = R_END
    glub_p, a = sb("glub_p", [128, 4, TP + 32], BF16, at=a)
    glub_s, a = sb("glub_s", [128, 4, NSEQ, 34], BF16, at=a)
    sig_off = a
    sig, a = sb("sig", [128, T], F32, at=a)
    g32p, a = sb("g32p", [128, 4, 32], F32, at=a)
    g32s, a = sb("g32s", [128, 4, 64], F32, at=a)
    ctok_off = a
    ctok, a = sb("ctok", [128, CD], F32, at=a)
    otok, a = sb("otok", [128, CD], F32, at=a)
    cbb, a = sb("cbb", [128, 4, 512], BF16, at=a)
    mean, a = sb("mean", [128, 512], F32, at=a)
    msq, a = sb("msq", [128, 512], F32, at=a)
    crs, a = sb("crs", [128, 512], F32, at=a)
    CONVA_NAMES = ["glub_p", "glub_s", "sig", "g32p", "g32s", "ctok", "otok"]
    convout, _ = sb("convout", [128, 4, T], BF16, at=A0)
    c32, _ = sb("c32", [128, 4, 512], F32, at=sig_off)
    csq, _ = sb("csq", [128, 4, 512], BF16, at=ctok_off)
    dg, _ = sb("dg", [128, 4 * CW, 128], BF16, at=hb_off)
    CONVB_NAMES = ["convout", "c32", "csq", "dg", "cbb", "mean", "msq", "crs"]
    hid = [sb(f"hid{i}", [128, 8, T], BF16, at=A0 + i * 8 * T * 2)[0] for i in range(2)]
    rtmp = [sb(f"rtmp{i}", [128, 512], BF16, at=A0 + 2 * 8 * T * 2 + i * 1024)[0] for i in range(2)]
    FFN_NAMES = ["hid0", "hid1", "rtmp0", "rtmp1"]
    ring2 = [sb(f"ringB{i}", [128, KC, 128], BF16, at=A0 + 2 * 8 * T * 2 + 2048 + i * 2048)[0] for i in range(8)]
    NORM_NAMES = ["sq0", "sq1", "rstd"]

    def cs(name, r0=0, r1=128):
        c0, n = ccols[name]
        return cst[r0:r1, c0:c0 + n]

    def vc(name, j=None):
        c0, n = vcols[name]
        if j is None:
            return vec[:, c0:c0 + n]
        return vec[:, c0 + j:c0 + j + 1]

    ident = cs("ident")

    psb = [es.enter_context(nc.psum_tensor(f"ps{i}", [128, 512], F32)) for i in range(8)]
    NB = [0, 1, 2, 4, 5]

    def PK(b):
        return ("ps", b)

    PK6 = [("ps6", i) for i in range(4)]

    def act(out, in_, func, R, W, bias=None, scale=None):
        kw = {}
        if bias is not None:
            kw["bias"] = bias
        if scale is not None:
            kw["scale"] = scale
        return pg.add("act", lambda e: e.activation(out=out, in_=in_, func=func, **kw), R, W)

    def dve(fn, R, W):
        return pg.add("dve", fn, R, W)

    def mm(out, lhsT, rhs, start, stop, R, W):
        return pg.add("pe", lambda e: e.matmul(out, lhsT, rhs, start=start, stop=stop), R, W)

    def tr(out, in_, idn, R, W):
        return pg.add("pe", lambda e: e.transpose(out, in_, idn), R, W)

    def dma(q, out, in_, R, W, group):
        return pg.add(q, lambda e: e.dma_start(out=out, in_=in_), R, W, dma=group)

    ogrp = [0]

    def out_dma(out, in_, R):
        g = f"o{ogrp[0] % 8}"
        ogrp[0] += 1
        return dma("sp", out, in_, R, (), g)

    mmb = [0]

    def next_bank(banks=(0, 1, 2)):
        b = banks[mmb[0] % len(banks)]
        mmb[0] += 1
        return b

    units = []
    for l in range(NL):
        if STAGE >= 1:
            for h in range(KNH):
                for grp in range(4):
                    c0 = 1024 + grp * 512 + h * 128
                    units.append((w_in[l, :, c0:c0 + 128], 8))
        if STAGE >= 2:
            for cc in range(4):
                units.append((w_in[l, :, 512 + cc * 128:512 + (cc + 1) * 128], 8))
                units.append((w_in[l, :, cc * 128:(cc + 1) * 128], 8))
        if STAGE >= 4:
            for m in range(8):
                units.append((w_out[l, :, m * 128:(m + 1) * 128], 8))
        for g in range(4 if STAGE >= 5 else 0):
            for f in range(8):
                c0 = (g * 8 + f) * 128
                units.append((w_up[l, :, c0:c0 + 128], 8))
            for m in range(8):
                units.append((w_down[l, g * 1024:(g + 1) * 1024, m * 128:(m + 1) * 128], 8))
    TAIL = STAGE >= 5 and NL == DEPTH
    if TAIL:
        tail_units = units[-8:]
        units = units[:-8]
    ucur = [0]
    uload = [0]

    def load_units_upto(i):
        while uload[0] <= i and uload[0] < len(units):
            u = uload[0]
            ap, kcn = units[u]
            s = u % NSLOT
            dma("pool", ring[s][:, 0:kcn, :], ap.rearrange("(kc p) n -> p kc n", p=128), (), [(f"ring{s}",)], f"w{s}")
            uload[0] += 1

    def next_unit(look=NSLOT - 1):
        u = ucur[0]
        ucur[0] += 1
        load_units_upto(u + look)
        return u % NSLOT, units[u][1]

    def proj(rhs_fn, rhs_keys, evac, banks=(0, 1, 2)):
        s, kcn = next_unit()
        for tt, (t0, n) in enumerate(TTS):
            b = next_bank(banks)
            for kc in range(kcn):
                mm(psb[b][:, 0:n], ring[s][:, kc, :], rhs_fn(kc, t0, n), kc == 0, kc == kcn - 1,
                   [(f"ring{s}",)] + rhs_keys(kc, t0, n), [PK(b)])
            evac(tt, t0, n, b)

    def proj2(rhs_fn, rhs_keys, evac_a, evac_b, banks=(0, 1, 2)):
        sa, kcn = next_unit(NSLOT - 1)
        sb_, _ = next_unit(NSLOT - 2)
        for tt, (t0, n) in enumerate(TTS):
            for s_, ev in ((sa, evac_a), (sb_, evac_b)):
                b = next_bank(banks)
                for kc in range(kcn):
                    mm(psb[b][:, 0:n], ring[s_][:, kc, :], rhs_fn(kc, t0, n), kc == 0, kc == kcn - 1,
                       [(f"ring{s_}",)] + rhs_keys(kc, t0, n), [PK(b)])
                ev(tt, t0, n, b)

    def tkeys(name, idx, t0, n):
        return [(name, idx, c) for c in chunks_of(t0, n)]

    dma("sp", cst[:, :], cst_d, (), ["cst"], "c0")
    dma("sp", vec[:, :], vec_d, (), ["vec"], "c1")
    dve(lambda e: e.tensor_copy(out=identb[:, :], in_=ident), ["cst"], ["identb"])
    dve(lambda e: e.memset(onesb[:, :], 1.0), [], ["onesb"])
    dve(lambda e: e.memset(epsc[:, :], EPS), [], ["epsc"])

    pg.phase_begin(XTOK_NAMES)
    for i in range(NCHK):
        r0 = i * 128
        rows = 128 if i < 16 else 64
        s = i % NXT
        dma("sp", xtok[s][0:rows, :], xin[r0:r0 + rows, :], (), [(f"xtok{s}",)], f"x{s}")
        for half in range(2):
            b = next_bank((0, 1, 2, 4))
            for cq in range(4):
                c = half * 4 + cq
                tr(psb[b][:, cq * 128:cq * 128 + rows], xtok[s][0:rows, c * 128:(c + 1) * 128], ident[0:rows, 0:rows],
                   [(f"xtok{s}",), "cst"], [PK(b)])
            src = psb[b][:, :].rearrange("p (c t) -> p c t", c=4)[:, :, 0:rows]
            dst = xT[:, half * 4:half * 4 + 4, r0:r0 + rows]
            W = [("xT", c, i) for c in range(half * 4, half * 4 + 4)]
            if (i + half) % 2 == 0:
                act(dst, src, AF.Copy, [PK(b)], W)
            else:
                dve(lambda e, dst=dst, src=src: e.tensor_copy(out=dst, in_=src), [PK(b)], W)

    def rmsnorm_tile(gname, to_hb, tt):
        t0, n = TTS[tt]
        s = tt % 2
        act(sq[s][:, :, 0:n], xT[:, :, t0:t0 + n], AF.Square, [k for c in range(KC) for k in tkeys("xT", c, t0, n)], [(f"sq{s}",)])
        for c in range(KC):
            mm(psb[NB[tt]][:, 0:n], onesb[:, :], sq[s][:, c, 0:n], c == 0, c == KC - 1, [(f"sq{s}",), "onesb"], [PK(NB[tt])])
        r = rstd[:, t0:t0 + n]
        act(r, psb[NB[tt]][:, 0:n], AF.Sqrt, [PK(NB[tt]), "epsc"], [("rstd", tt)], bias=epsc[:, 0:1], scale=1.0 / D)
        dve(lambda e: e.reciprocal(out=r, in_=r), [("rstd", tt)], [("rstd", tt)])
        for c in range(KC):
            src = xT[:, c, t0:t0 + n]
            if to_hb:
                dst = hb[:, c, t0:t0 + n]
                W = tkeys("hb", c, t0, n)
            else:
                dst = src
                W = tkeys("xT", c, t0, n)
            dve(lambda e, dst=dst, src=src, c=c: e.scalar_tensor_tensor(
                out=dst, in0=src, scalar=vc(gname, c), in1=rstd[:, t0:t0 + n], op0=ALU.mult, op1=ALU.mult),
                tkeys("xT", c, t0, n) + [("rstd", tt), "vec"], W)

    def rmsnorm(gname, to_hb):
        pg.phase_begin(NORM_NAMES + (["hb"] if to_hb else []))
        for tt in range(len(TTS)):
            rmsnorm_tile(gname, to_hb, tt)

    def hb_rhs(kc, t0, n):
        return hb[:, kc, t0:t0 + n]

    def hb_keys(kc, t0, n):
        return tkeys("hb", kc, t0, n)

    for l in range(NL):
        if STAGE < 1:
            break
        rmsnorm(f"n1g{l}", True)

        pg.phase_begin(RET_NAMES)
        dma("sp", rott[:, :, :], rot_d, (), ["rott"], "c2")
        dve(lambda e: e.memset(qmask[:, :, :], 0.0), [], ["qmask"])

        def make_head(h):
            g128 = float(np.float32(GAM[h]) ** 128)
            g4 = float(np.float32(GAM[h]) ** 4)
            dma("sp", st32[:, :, :], state[l, :, h, :, :].rearrange("b d v -> d b v"), (), ["st32"] + [("st32", i) for i in range(4)], "c3")
            u0 = ucur[0]
            ucur[0] += 4
            load_units_upto(u0 + 3)
            slots = [(u0 + i) % NSLOT for i in range(4)]
            pool = lambda fn, R, W: pg.add("pool", fn, R, W)

            def p_mm(ui, tt):
                t0, n = TTS[tt]
                b = next_bank((0, 1))
                s_ = slots[ui]
                for kc in range(KC):
                    mm(psb[b][:, 0:n], ring[s_][:, kc, :], hb[:, kc, t0:t0 + n], kc == 0, kc == KC - 1,
                       [(f"ring{s_}",)] + tkeys("hb", kc, t0, n), [PK(b)])
                return b

            def P_q(tt, h=h):
                t0, n = TTS[tt]
                b = p_mm(0, tt)
                ps = psb[b]
                dve(lambda e: e.tensor_tensor(out=tmpA[:, 0:n], in0=ps[:, 0:n], in1=rott[:, 0, t0:t0 + n], op=ALU.mult),
                    [PK(b), "rott"], ["tmpA"])
                dve(lambda e: e.tensor_tensor(out=tmpB[0:64, 0:n], in0=ps[64:128, 0:n], in1=rott[64:128, 1, t0:t0 + n], op=ALU.mult),
                    [PK(b), "rott"], [("tmpB", 0)])
                dve(lambda e: e.tensor_tensor(out=tmpB[64:128, 0:n], in0=ps[0:64, 0:n], in1=rott[0:64, 1, t0:t0 + n], op=ALU.mult),
                    [PK(b), "rott"], [("tmpB", 1)])
                dve(lambda e: e.tensor_tensor(out=tmpA[:, 0:n], in0=tmpA[:, 0:n], in1=tmpB[:, 0:n], op=ALU.add),
                    ["tmpA", ("tmpB", 0), ("tmpB", 1)], ["tmpA"])
                act(qT[:, t0:t0 + n], tmpA[:, 0:n], AF.Copy, ["tmpA"], tkeys("qT", 0, t0, n))
                if tt < 4:
                    g_ap = cs(f"gq{h}")
                    in1 = bass.AP(g_ap.tensor, g_ap.offset, [[g_ap.ap[0][0], 128], [0, 4], [1, 128]])
                    dve(lambda e: e.tensor_tensor(out=qdT[:, t0:t0 + n].rearrange("p (c i) -> p c i", c=4),
                                                  in0=tmpA[:, 0:n].rearrange("p (c i) -> p c i", c=4), in1=in1, op=ALU.mult),
                        ["tmpA", "cst"], tkeys("qdT", 0, t0, n))
                else:
                    dve(lambda e: e.tensor_tensor(out=qdT[:, t0:t0 + n], in0=tmpA[:, 0:n], in1=cs(f"gqs{h}"), op=ALU.mult),
                        ["tmpA", "cst"], tkeys("qdT", 0, t0, n))

            def P_k(tt):
                t0, n = TTS[tt]
                b = p_mm(1, tt)
                ps = psb[b]
                dve(lambda e: e.tensor_tensor(out=k32b[:, 0:n], in0=ps[:, 0:n], in1=rott[:, 0, t0:t0 + n], op=ALU.mult),
                    [PK(b), "rott"], [("k32b",)])
                dve(lambda e: e.tensor_tensor(out=tmpKB[0:64, 0:n], in0=ps[64:128, 0:n], in1=rott[64:128, 1, t0:t0 + n], op=ALU.mult),
                    [PK(b), "rott"], [("tmpKB", 0)])
                dve(lambda e: e.tensor_tensor(out=tmpKB[64:128, 0:n], in0=ps[0:64, 0:n], in1=rott[0:64, 1, t0:t0 + n], op=ALU.mult),
                    [PK(b), "rott"], [("tmpKB", 1)])
                dve(lambda e: e.tensor_tensor(out=kT[:, t0:t0 + n], in0=k32b[:, 0:n], in1=tmpKB[:, 0:n], op=ALU.add),
                    [("k32b",), ("tmpKB", 0), ("tmpKB", 1)], tkeys("kT", 0, t0, n))

            def P_v(tt):
                t0, n = TTS[tt]
                b = p_mm(2, tt)
                act(vT[:, t0:t0 + n], psb[b][:, 0:n], AF.Copy, [PK(b)], tkeys("vT", 0, t0, n))

            def P_g(tt):
                t0, n = TTS[tt]
                b = p_mm(3, tt)
                act(gs[:, t0:t0 + n], psb[b][:, 0:n], AF.Silu, [PK(b)], tkeys("gs", 0, t0, n))

            def chunk_steps(tt, h=h, g128=g128, g4=g4):
                t0, n = TTS[tt]
                smp = tt == 4
                nck = 1 if smp else 4
                w = 64 if smp else 128
                cl = [4 * tt + ci for ci in range(nck)]
                kq = [("qT", 0, c) for c in cl]
                kk = [("kT", 0, c) for c in cl]
                kv = [("vT", 0, c) for c in cl]
                nn = nck * 128
                ob = 5 if tt % 2 == 0 else 2

                def C1():
                    for ci in range(nck):
                        cols = slice(t0 + ci * 128, t0 + ci * 128 + w)
                        mm(psb[3][0:w, ci * 128:(ci + 1) * 128], kT[:, cols], identb[:, :], True, True, kk + ["identb"], [PK(3)])
                    for ci in range(nck):
                        cols = slice(t0 + ci * 128, t0 + ci * 128 + w)
                        mm(psb[7][0:w, ci * 128:(ci + 1) * 128], vT[:, cols], identb[:, :], True, True, kv + ["identb"], [PK(7)])
                    act(ktok[0:w, 0:nn], psb[3][0:w, 0:nn], AF.Copy, [PK(3)], ["ktok"])
                    if not smp:
                        act(vdec[:, 0:nn], psb[7][:, 0:nn], AF.Identity, [PK(7), "cst"], ["vdec"], scale=cs(f"gdec{h}"))
                    act(vtok[0:w, 0:nn], psb[7][0:w, 0:nn], AF.Copy, [PK(7)], ["vtok"])

                def C2():
                    for ci in range(nck):
                        cols = slice(t0 + ci * 128, t0 + ci * 128 + w)
                        mm(psb[4][0:w, ci * 128:ci * 128 + w], kT[:, cols], qT[:, cols], True, True, kk + kq, [PK(4)])
                    if not smp:
                        d_ap = cs(f"decT{h}")
                        dec4 = bass.AP(d_ap.tensor, d_ap.offset, [[d_ap.ap[0][0], 128], [0, 4], [1, 128]])
                        dve(lambda e: e.tensor_tensor(out=STb[:, 0:512].rearrange("p (c i) -> p c i", c=4),
                                                      in0=psb[4][:, 0:512].rearrange("p (c i) -> p c i", c=4), in1=dec4, op=ALU.mult),
                            [PK(4), "cst"], ["STb"])
                    else:
                        dve(lambda e: e.tensor_tensor(out=STb[0:64, 0:64], in0=psb[4][0:64, 0:64], in1=cs(f"decS{h}", 0, 64), op=ALU.mult),
                            [PK(4), "cst"], ["STb"])

                def o_mms(ci):
                    c = cl[ci]
                    cols = slice(t0 + ci * 128, t0 + ci * 128 + 128)
                    oreg = psb[ob][:, ci * 128:(ci + 1) * 128]
                    mm(oreg, STb[:, ci * 128:(ci + 1) * 128], vtok[:, ci * 128:(ci + 1) * 128], True, c == 0, ["STb", "vtok"], [PK(ob)])
                    if c > 0:
                        pv = (ci - 1) % 4
                        mm(oreg, qdT[:, cols], stb[:, pv * 128:(pv + 1) * 128], False, True, [("qdT", 0, c), ("stb", pv)], [PK(ob)])

                def C3():
                    if not smp:
                        for ci in range(nck):
                            mm(psb[6][:, ci * 128:(ci + 1) * 128], ktok[:, ci * 128:(ci + 1) * 128], vdec[:, ci * 128:(ci + 1) * 128], True, True,
                               ["ktok", "vdec"], PK6)
                        o_mms(0)
                        for ci in range(nck):
                            c = cl[ci]
                            ureg = psb[6][:, ci * 128:(ci + 1) * 128]
                            sreg = stb[:, ci * 128:(ci + 1) * 128]
                            if c == 0:
                                dve(lambda e, ureg=ureg, sreg=sreg: e.tensor_copy(out=sreg, in_=ureg), PK6, [("stb", ci)])
                                dve(lambda e, ureg=ureg: e.tensor_copy(out=s32[:, :], in_=ureg), PK6, ["s32"])
                            else:
                                if c < 15:
                                    dve(lambda e, ureg=ureg, sreg=sreg: e.scalar_tensor_tensor(
                                        out=sreg, in0=s32[:, :], scalar=g128, in1=ureg, op0=ALU.mult, op1=ALU.add), PK6 + ["s32"], [("stb", ci)])
                                dve(lambda e, ureg=ureg: e.scalar_tensor_tensor(out=s32[:, :], in0=s32[:, :], scalar=g128, in1=ureg,
                                                                                op0=ALU.mult, op1=ALU.add), PK6 + ["s32"], ["s32"])
                            if c == 15:
                                out_dma(nrp_d[l, h, :, :], s32[:, :], ["s32"])
                    else:
                        mm(psb[ob][0:64, 0:128], STb[0:64, 0:64], vtok[0:64, 0:128], True, False, ["STb", "vtok"], [PK(ob)])
                        qm_dst = bass.AP(qmask, 0, [[NSEQ * 64, 128], [68, NSEQ], [1, 4]])
                        dve(lambda e: e.tensor_copy(out=qm_dst, in_=qdT[:, t0:t0 + 64].rearrange("p (b i) -> p b i", i=4)),
                            [("qdT", 0, 16), "qmask"], ["qmask"])
                        act(stsb[:, :, :], st32[:, :, :], AF.Copy, ["st32"] + [("st32", i) for i in range(4)], ["stsb"])
                        for bq in range(NSEQ):
                            mm(psb[ob][0:64, 0:128], qmask[:, bq, :], stsb[:, bq, :], False, bq == NSEQ - 1, ["qmask", "stsb"], [PK(ob)])

                def C4():
                    if not smp:
                        for ci in range(1, nck):
                            o_mms(ci)
                    else:
                        pg.phase_begin(["kmask"])
                        km_in0 = bass.AP(ktok, 0, [[512, 64], [0, NSEQ], [1, 128]])
                        md = cs(f"maskdec{h}", 0, 64)
                        km_in1 = bass.AP(md.tensor, md.offset, [[md.ap[0][0], 64], [1, NSEQ], [0, 128]])
                        dve(lambda e: e.tensor_tensor(out=kmask[0:64, :, :], in0=km_in0, in1=km_in1, op=ALU.mult), ["ktok", "cst"], ["kmask"])
                        for bq4 in range(4):
                            ub = 6 if bq4 % 2 == 0 else 3
                            ukeys = PK6 if ub == 6 else [PK(3)]
                            for bi in range(4):
                                bq = bq4 * 4 + bi
                                mm(psb[ub][:, bi * 128:(bi + 1) * 128], kmask[0:64, bq, :], vtok[0:64, 0:128], True, True, ["kmask", "vtok"], ukeys)
                            dve(lambda e, bq4=bq4, ub=ub: e.scalar_tensor_tensor(
                                out=st32[:, bq4 * 4:bq4 * 4 + 4, :], in0=st32[:, bq4 * 4:bq4 * 4 + 4, :], scalar=g4,
                                in1=psb[ub][:, :].rearrange("p (b v) -> p b v", b=4), op0=ALU.mult, op1=ALU.add),
                                ukeys + [("st32", bq4)], [("st32", bq4)])
                        out_dma(nrs_d[l, :, h, :, :].rearrange("b d v -> d b v"), st32[:, :, :], ["st32"] + [("st32", i) for i in range(4)])

                def G1():
                    for ci in range(nck):
                        dve(lambda e, ci=ci: e.bn_stats(out=bnst[0:w, ci * 6:ci * 6 + 6], in_=psb[ob][0:w, ci * 128:(ci + 1) * 128]), [PK(ob)], [("bnst", ci)])
                    for ci in range(nck):
                        dve(lambda e, ci=ci: e.bn_aggr(out=mv[0:w, ci * 2:ci * 2 + 2], in_=bnst[0:w, ci * 6:ci * 6 + 6]), [("bnst", ci)], [("mv", ci)])
                    mvk = [("mv", ci) for ci in range(nck)]
                    var_ap = bass.AP(mv, 1, [[16, w], [2, nck]])
                    act(mv[0:w, 8:8 + nck], var_ap, AF.Sqrt, mvk + ["epsc"], ["rs"], bias=epsc[0:w, 0:1])

                def G2():
                    dve(lambda e: e.reciprocal(out=mv[0:w, 12:12 + nck], in_=mv[0:w, 8:8 + nck]), ["rs"], ["rs2"])
                    for ci in range(nck):
                        dve(lambda e, ci=ci: e.tensor_scalar(out=onb[0:w, ci * 128:(ci + 1) * 128], in0=psb[ob][0:w, ci * 128:(ci + 1) * 128],
                                                             scalar1=mv[0:w, 2 * ci:2 * ci + 1], scalar2=mv[0:w, 12 + ci:13 + ci],
                                                             op0=ALU.subtract, op1=ALU.mult), [PK(ob), ("mv", ci), "rs2"], [("onb", ci)])

                def C5():
                    for ci in range(nck):
                        mm(psb[4][:, ci * 128:ci * 128 + w], onb[0:w, ci * 128:(ci + 1) * 128], identb[0:w, 0:w], True, True, [("onb", ci), "identb"], [PK(4)])
                    dve(lambda e, gcol=vc(f"gng{l}", h): e.scalar_tensor_tensor(out=retout[:, h, t0:t0 + n], in0=psb[4][:, 0:n], scalar=gcol,
                                                                                in1=gs[:, t0:t0 + n], op0=ALU.mult, op1=ALU.mult),
                        [PK(4), "vec"] + [("gs", 0, c) for c in cl], [("retout", h, c) for c in cl])

                return [C1, C2, C3, C4, G1, G2, C5]

            return dict(P_q=P_q, P_k=P_k, P_v=P_v, P_g=P_g, chunk_steps=chunk_steps, u0=u0)

        nothing = lambda: None
        ctx = None
        for h in range(KNH):
            if ctx is None:
                ctx = make_head(h)
                pg.phase_begin(["k32a", "tmpKB"])
                for fn in ("P_q", "P_k", "P_v", "P_g"):
                    ctx[fn](0)
                ctx["gdone"] = {0}
            nxt = None
            prev = None
            for tt in range(6):
                cur = ctx["chunk_steps"](tt) if tt < 5 else None
                C1, C2, C3, C4 = cur[0:4] if cur else [nothing] * 4
                G1, G2, G3 = prev[4:7] if prev else [nothing] * 3
                if tt + 1 < 5:
                    pc, pt = ctx, tt + 1
                elif tt == 5 and h + 1 < KNH:
                    G1()
                    G1 = nothing
                    nxt = make_head(h + 1)
                    pg.phase_begin(["k32a", "tmpKB"])
                    pc, pt = nxt, 0
                else:
                    pc = None
                G1()
                C2()
                C1()
                if pc:
                    pc["P_q"](pt)
                G2()
                C3()
                if pc:
                    pc["P_k"](pt)
                C4()
                if pc:
                    pc["P_v"](pt)
                G3()
                if pc:
                    gd = pc.setdefault("gdone", set())
                    for t2 in (pt, pt + 1):
                        if t2 < 5 and t2 not in gd:
                            pc["P_g"](t2)
                            gd.add(t2)
                if tt == 3:
                    load_units_upto(ctx["u0"] + 7)
                prev = cur
            ctx = nxt

        if STAGE < 2:
            continue
        pg.phase_begin(CONVA_NAMES)
        for cc in range(4):
            dve(lambda e, cc=cc: e.memset(glub_p[:, cc, 0:30], 0.0), [], [("glub_p", cc, "z")])
        for rt in range(4):
            dma("sp", ctok[0:120, :], cache[l, rt * 120:(rt + 1) * 120, :], (), ["ctok"], "c4")
            for cc in range(4):
                tr(psb[6][:, cc * 120:(cc + 1) * 120], ctok[0:120, cc * 128:(cc + 1) * 128], ident[0:120, 0:120], ["ctok", "cst"], PK6)
            act(glub_s[:, :, rt * 4:rt * 4 + 4, 0:30], psb[6][:, 0:480].rearrange("p (c s i) -> p c s i", c=4, s=4),
                AF.Copy, PK6, [("glub_s", "c", rt)])
        out_dma(ncs_d[l, :, 0:26, :], cache[l, :, :].rearrange("(b r) c -> b r c", r=30)[:, 4:30, :], [])

        for cc in range(4):
            proj(hb_rhs, hb_keys, lambda tt, t0, n, b: act(sig[:, t0:t0 + n], psb[b][:, 0:n], AF.Sigmoid, [PK(b)], [("sig", tt)]))

            def glu_evac(tt, t0, n, b, cc=cc):
                if tt < 4:
                    dve(lambda e: e.tensor_tensor(out=glub_p[:, cc, 30 + t0:30 + t0 + n], in0=psb[b][:, 0:n], in1=sig[:, t0:t0 + n], op=ALU.mult),
                        [PK(b), ("sig", tt)], [("glub_p", cc, tt)])
                    if tt == 3:
                        dve(lambda e: e.tensor_tensor(out=g32p[:, cc, 0:30], in0=psb[b][:, 482:512], in1=sig[:, TP - 30:TP], op=ALU.mult),
                            [PK(b), ("sig", tt)], [("g32p", cc)])
                else:
                    dve(lambda e: e.tensor_tensor(out=g32s[:, cc, :], in0=psb[b][:, 0:64], in1=sig[:, TP:T], op=ALU.mult),
                        [PK(b), ("sig", tt)], [("g32s", cc)])
                    act(glub_s[:, cc, :, 30:34], g32s[:, cc, :].rearrange("p (s i) -> p s i", i=4), AF.Copy, [("g32s", cc)], [("glub_s", "n", cc)])
            proj(hb_rhs, hb_keys, glu_evac)

        for cc in range(4):
            tr(psb[6][0:30, cc * 128:(cc + 1) * 128], g32p[:, cc, 0:30], ident, [("g32p", cc), "cst"], PK6)
        act(otok[0:30, :], psb[6][0:30, :], AF.Copy, PK6, ["otok"])
        out_dma(ncp_d[l, :, :], otok[0:30, :], ["otok"])
        for cc in range(4):
            tr(psb[6][0:64, cc * 128:(cc + 1) * 128], g32s[:, cc, :], ident, [("g32s", cc), "cst"], PK6)
        act(otok[0:64, :], psb[6][0:64, :], AF.Copy, PK6, ["otok"])
        for bq in range(NSEQ):
            out_dma(ncs_d[l, bq, 26:30, :], otok[bq * 4:bq * 4 + 4, :], ["otok"])

        if STAGE < 3:
            continue
        pg.phase_begin(CONVB_NAMES)
        for cc in range(4):
            c0, _ = vcols[f"cw{l}"]
            in0 = bass.AP(identb, 0, [[128, 128], [0, CW], [1, 128]])
            w_ap = vec[:, c0 + cc * CW:c0 + (cc + 1) * CW]
            in1 = bass.AP(w_ap.tensor, w_ap.offset, [[w_ap.ap[0][0], 128], [1, CW], [0, 128]])
            dve(lambda e, cc=cc, in0=in0, in1=in1: e.tensor_tensor(out=dg[:, cc * CW:(cc + 1) * CW, :], in0=in0, in1=in1, op=ALU.mult),
                ["identb", "vec"], [("dg", cc)])

        for tt, (t0, n) in enumerate(TTS):
            for cc in range(4):
                b = next_bank((0, 1, 2))
                for j in range(CW):
                    if tt < 4:
                        rhs = glub_p[:, cc, t0 + j:t0 + j + 512]
                        outp = psb[b][:, 0:512]
                        R = [("glub_p", cc, tt), ("glub_p", cc, "z" if tt == 0 else tt - 1)]
                    else:
                        rhs = glub_s[:, cc, :, j:j + 4]
                        outp = psb[b][:, 0:64].rearrange("p (s i) -> p s i", i=4)
                        R = [("glub_s", "n", cc)] + [("glub_s", "c", rt) for rt in range(4)]
                    mm(outp, dg[:, cc * CW + j, :], rhs, j == 0, j == CW - 1, R + [("dg", cc)], [PK(b)])
                act(c32[:, cc, 0:n], psb[b][:, 0:n], AF.Identity, [PK(b), "vec"], [("c32", cc)], bias=vc(f"cb{l}", cc))
                act(csq[:, cc, 0:n], psb[b][:, 0:n], AF.Square, [PK(b), "vec"], [("csq", cc)], bias=vc(f"cb{l}", cc))
                dve(lambda e, cc=cc, n=n: e.tensor_copy(out=cbb[:, cc, 0:n], in_=c32[:, cc, 0:n]), [("c32", cc)], [("cbb", cc)])
            for cc in range(4):
                mm(psb[4][:, 0:n], onesb[:, :], cbb[:, cc, 0:n], cc == 0, cc == 3, [("cbb", cc), "onesb"], [PK(4)])
            for cc in range(4):
                mm(psb[5][:, 0:n], onesb[:, :], csq[:, cc, 0:n], cc == 0, cc == 3, [("csq", cc), "onesb"], [PK(5)])
            dve(lambda e, n=n: e.tensor_scalar(out=mean[:, 0:n], in0=psb[4][:, 0:n], scalar1=1.0 / CD, scalar2=0.0, op0=ALU.mult, op1=ALU.add), [PK(4)], ["mean"])
            dve(lambda e, n=n: e.tensor_tensor(out=msq[:, 0:n], in0=mean[:, 0:n], in1=mean[:, 0:n], op=ALU.mult), ["mean"], ["msq"])
            dve(lambda e, n=n: e.scalar_tensor_tensor(out=crs[:, 0:n], in0=psb[5][:, 0:n], scalar=1.0 / CD, in1=msq[:, 0:n],
                                                      op0=ALU.mult, op1=ALU.subtract), [PK(5), "msq"], ["crs"])
            act(crs[:, 0:n], crs[:, 0:n], AF.Sqrt, ["crs", "epsc"], ["crs"], bias=epsc[:, 0:1])
            dve(lambda e, n=n: e.reciprocal(out=crs[:, 0:n], in_=crs[:, 0:n]), ["crs"], ["crs"])
            for cc in range(4):
                dve(lambda e, cc=cc, n=n: e.tensor_tensor(out=c32[:, cc, 0:n], in0=c32[:, cc, 0:n], in1=mean[:, 0:n], op=ALU.subtract),
                    [("c32", cc), "mean"], [("c32", cc)])
                dve(lambda e, cc=cc, n=n: e.tensor_tensor(out=c32[:, cc, 0:n], in0=c32[:, cc, 0:n], in1=crs[:, 0:n], op=ALU.mult),
                    [("c32", cc), "crs"], [("c32", cc)])
                act(convout[:, cc, t0:t0 + n], c32[:, cc, 0:n], AF.Silu, [("c32", cc), "vec"], tkeys("convout", cc, t0, n),
                    bias=vc(f"lnb{l}", cc), scale=vc(f"lng{l}", cc))

        if STAGE < 4:
            continue
        def mix_rhs(kc, t0, n):
            return convout[:, kc, t0:t0 + n] if kc < 4 else retout[:, kc - 4, t0:t0 + n]

        def mix_keys(kc, t0, n):
            return tkeys("convout", kc, t0, n) if kc < 4 else tkeys("retout", kc - 4, t0, n)

        for m in range(8):
            def res_evac(tt, t0, n, b, m=m):
                dve(lambda e: e.tensor_tensor(out=xT[:, m, t0:t0 + n], in0=psb[b][:, 0:n], in1=xT[:, m, t0:t0 + n], op=ALU.add),
                    [PK(b)] + tkeys("xT", m, t0, n), tkeys("xT", m, t0, n))
            proj(mix_rhs, mix_keys, res_evac)

        if STAGE < 5:
            continue
        rmsnorm(f"n2g{l}", True)
        pg.phase_begin(FFN_NAMES)
        for g in range(4):
            hs = g % 2
            for f in range(8):
                def up_evac(tt, t0, n, b, f=f, hs=hs):
                    rs = (f + tt) % 2
                    act(rtmp[rs][:, 0:n], psb[b][:, 0:n], AF.Relu, [PK(b)], [(f"rtmp{rs}",)])
                    dve(lambda e: e.tensor_tensor(out=hid[hs][:, f, t0:t0 + n], in0=rtmp[rs][:, 0:n], in1=rtmp[rs][:, 0:n], op=ALU.mult),
                        [(f"rtmp{rs}",)], tkeys(f"hid{hs}", f, t0, n))
                proj(hb_rhs, hb_keys, up_evac)
            if TAIL and l == NL - 1 and g == 3:
                break
            for m in range(8):
                def dn_evac(tt, t0, n, b, m=m):
                    dve(lambda e: e.tensor_tensor(out=xT[:, m, t0:t0 + n], in0=psb[b][:, 0:n], in1=xT[:, m, t0:t0 + n], op=ALU.add),
                        [PK(b)] + tkeys("xT", m, t0, n), tkeys("xT", m, t0, n))
                proj(lambda kc, t0, n, hs=hs: hid[hs][:, kc, t0:t0 + n], lambda kc, t0, n, hs=hs: tkeys(f"hid{hs}", kc, t0, n), dn_evac)

    def store_chunk(i):
        r0 = i * 128
        rows = 128 if i < 16 else 64
        s = i % 2
        for half in range(2):
            b = next_bank((0, 1, 2, 4))
            for cq in range(4):
                c = half * 4 + cq
                tr(psb[b][0:rows, cq * 128:(cq + 1) * 128], xT[:, c, r0:r0 + rows], ident, [("xT", c, i), "cst"], [PK(b)])
            dst = xtok[s][0:rows, half * 512:(half + 1) * 512]
            if half == 0:
                act(dst, psb[b][0:rows, :], AF.Copy, [PK(b)], [(f"xtok{s}", half)])
            else:
                dve(lambda e, dst=dst, b=b, rows=rows: e.tensor_copy(out=dst, in_=psb[b][0:rows, :]), [PK(b)], [(f"xtok{s}", half)])
        out_dma(y_d[r0:r0 + rows, :], xtok[s][0:rows, :], [(f"xtok{s}", 0), (f"xtok{s}", 1)])

    def store_tile(tt):
        for i in (range(4 * tt, 4 * tt + 4) if tt < 4 else [16]):
            store_chunk(i)

    if TAIL:
        l = NL - 1
        pg.phase_begin([f"ringB{i}" for i in range(8)])
        for m in range(8):
            ap, kcn = tail_units[m]
            dma("pool", ring2[m][:, :, :], ap.rearrange("(kc p) n -> p kc n", p=128), (), [(f"ringB{m}",)], f"wB{m}")
        pg.phase_begin(NORM_NAMES + XTOK_NAMES[0:2])
        for step in range(7):
            if step < 5:
                t0, n = TTS[step]
                for m in range(8):
                    b = next_bank((0, 1, 2))
                    for kc in range(KC):
                        mm(psb[b][:, 0:n], ring2[m][:, kc, :], hid[1][:, kc, t0:t0 + n], kc == 0, kc == KC - 1,
                           [(f"ringB{m}",)] + tkeys("hid1", kc, t0, n), [PK(b)])
                    dve(lambda e, m=m, b=b, t0=t0, n=n: e.tensor_tensor(out=xT[:, m, t0:t0 + n], in0=psb[b][:, 0:n], in1=xT[:, m, t0:t0 + n], op=ALU.add),
                        [PK(b)] + tkeys("xT", m, t0, n), tkeys("xT", m, t0, n))
            if 1 <= step <= 5:
                rmsnorm_tile("fng", False, step - 1)
            if 2 <= step <= 6:
                store_tile(step - 2)
    else:
        rmsnorm("fng", False)
        pg.phase_begin(XTOK_NAMES[0:2])
        for tt in range(5):
            store_tile(tt)

    pg.emit(nc, es, [f"o{i}" for i in range(8)])
    es.close()
    print("A0", A0, "arena", SB_END - A0, "ops", pg.stats, flush=True)
    return nc, pg


_CACHE = {}


def kernel(**inputs):
    inp = {k: np.asarray(v) for k, v in inputs.items()}
    cstn, ccols, rot = host_consts()
    vecs, vcols = host_vecs(inp)
    key = "prog"
    if key not in _CACHE:
        _CACHE[key] = build_program(ccols, vcols, cstn.shape[1], vecs.shape[1])
    nc, pg = _CACHE[key]
    xp = inp["x_prompt"].astype(np.float32, copy=False)
    xs = inp["x_sample"].astype(np.float32, copy=False)
    in_maps = []
    for c in range(NCORES):
        xin = np.concatenate([xp[c], xs[c * NSEQ:(c + 1) * NSEQ].reshape(TS, D)], axis=0)
        in_maps.append({
            "xin": np.ascontiguousarray(xin),
            "cache": np.ascontiguousarray(inp["cache_conv"][:, c * NSEQ:(c + 1) * NSEQ].reshape(DEPTH, NSEQ * 30, CD)),
            "state": np.ascontiguousarray(inp["state_ret"][:, c * NSEQ:(c + 1) * NSEQ]),
            "w_in": inp["w_in"], "w_out": inp["w_out"], "w_up": inp["w_up"], "w_down": inp["w_down"],
            "cst": cstn, "vecs": vecs, "rot": rot,
        })
    res = run_bass_kernel_spmd(nc, in_maps, core_ids=list(range(NCORES)))
    R = res.results
    y = np.stack([r["y"] for r in R])
    y_prompt = np.ascontiguousarray(y[:, :TP])
    y_sample = np.ascontiguousarray(y[:, TP:].reshape(NCORES * NSEQ, 4, D))
    ncp = np.ascontiguousarray(np.stack([r["ncp"] for r in R], axis=1))
    nrp = np.ascontiguousarray(np.stack([r["nrp"] for r in R], axis=1))
    ncs = np.ascontiguousarray(np.concatenate([r["ncs"] for r in R], axis=1))
    nrs = np.ascontiguousarray(np.concatenate([r["nrs"] for r in R], axis=1))
    return (y_prompt, y_sample, ncp, nrp, ncs, nrs)
```

```python
import numpy as np
from contextlib import ExitStack

import concourse.bass as bass
import concourse.mybir as mybir
from concourse.bass_utils import run_bass_kernel_spmd

F32 = mybir.dt.float32
BF16 = mybir.dt.bfloat16
ALU = mybir.AluOpType
AF = mybir.ActivationFunctionType

NCORES = 8
P = 128
D = 1024
KC = 8
DEPTH = 2
TP = 2048
NSEQ = 16
TS = 64
T = TP + TS
CW = 31
CD = 512
HD = 128
NH = 4
DFF = 4096
EPS = 1e-6
PAST = 16384
TTS = [(0, 512), (512, 512), (1024, 512), (1536, 512), (2048, 64)]
NCHK = 17
NSLOT = 4
GAM = [1.0 - 2.0 ** (-5.0 - h) for h in range(NH)]
SAME_ENGINE_WAR = True


def chunks_of(t0, n):
    return range(t0 // 128, (t0 + n + 127) // 128)


class Op:
    __slots__ = ("eng", "fn", "deps", "dma", "sem", "val", "has_dep", "idx")


class Prog:
    ENGS = ("pe", "dve", "act", "pool", "sp")

    def __init__(self):
        self.ops = []
        self.last_w = {}
        self.readers = {}
        self.dma_last = {}
        self.ranges = {}
        self.tops = {}
        self.pending = {}
        self.pdone = {}

    def register(self, name, lo, hi):
        self.ranges[name] = (lo, hi)
        self.tops[name] = {}

    def phase_begin(self, names):
        for nm in names:
            lo, hi = self.ranges[nm]
            pend = {}
            for other, (l2, h2) in self.ranges.items():
                if other in names or other == nm:
                    continue
                if l2 < hi and lo < h2:
                    for o in self.tops[other].values():
                        pend[o.idx] = o
            self.pending[nm] = list(pend.values())
            self.pdone[nm] = set()

    def add(self, eng, fn, R=(), W=(), dma=None):
        o = Op()
        o.eng = eng
        o.fn = fn
        o.dma = dma
        o.sem = None
        o.val = 0
        o.has_dep = False
        o.idx = len(self.ops)
        deps = {}
        me = "dma" if dma else eng

        def consider(d, kind):
            if d is None:
                return
            de = "dma" if d.dma else d.eng
            if de == me and me != "dma":
                if me == "pe":
                    return
                if kind == "war" and not SAME_ENGINE_WAR:
                    return
            deps[d.idx] = d

        for k in tuple(R) + tuple(W):
            nm = k[0] if isinstance(k, tuple) else k
            if nm in self.pending and me not in self.pdone[nm]:
                if me != "dma":
                    self.pdone[nm].add(me)
                for d in self.pending[nm]:
                    consider(d, "raw")
            if nm in self.tops:
                self.tops[nm][dma if dma else eng] = o
        for k in R:
            consider(self.last_w.get(k), "raw")
        for k in W:
            consider(self.last_w.get(k), "waw")
            for r in self.readers.get(k, ()):
                consider(r, "war")
        if dma:
            consider(self.dma_last.get(dma), "waw")
            self.dma_last[dma] = o
        for k in R:
            self.readers.setdefault(k, []).append(o)
        for k in W:
            self.last_w[k] = o
            self.readers[k] = []
        o.deps = list(deps.values())
        for d in o.deps:
            d.has_dep = True
        if dma:
            o.has_dep = True
        self.ops.append(o)
        return o

    def emit(self, nc, es, final_waits_groups):
        eng_sem = {e: es.enter_context(nc.semaphore("sem_" + e)) for e in ("pe", "dve", "act", "pool")}
        cnt = {e: 0 for e in eng_sem}
        dma_sem = {}
        dma_cnt = {}
        for o in self.ops:
            if o.dma:
                if o.dma not in dma_sem:
                    dma_sem[o.dma] = es.enter_context(nc.semaphore("dsem_" + o.dma))
                    dma_cnt[o.dma] = 0
                dma_cnt[o.dma] += 16
                o.sem = dma_sem[o.dma]
                o.val = dma_cnt[o.dma]
            elif o.has_dep:
                cnt[o.eng] += 1
                o.sem = eng_sem[o.eng]
                o.val = cnt[o.eng]
        block = es.enter_context(nc.Block())
        handles = {"pe": block.tensor, "dve": block.vector, "act": block.scalar, "pool": block.gpsimd, "sp": block.sync}
        for e in self.ENGS:
            ops = [o for o in self.ops if o.eng == e]

            def body(eh, ops=ops, e=e):
                waited = {}
                for o in ops:
                    for d in o.deps:
                        key = d.sem.num
                        if waited.get(key, 0) < d.val:
                            eh.wait_ge(d.sem, d.val)
                            waited[key] = d.val
                    ins = o.fn(eh)
                    if o.sem is not None:
                        ins.then_inc(o.sem, 16 if o.dma else 1)
                if e == "sp":
                    for g in final_waits_groups:
                        if g in dma_sem:
                            eh.wait_ge(dma_sem[g], dma_cnt[g])

            handles[e](body)
        self.stats = {e: sum(1 for o in self.ops if o.eng == e) for e in self.ENGS}


def host_consts():
    f32 = np.float32
    i = np.arange(128)
    cols = {}
    cst = []

    def put(name, arr):
        arr = np.asarray(arr, dtype=f32)
        assert arr.shape[0] == 128
        cols[name] = (sum(a.shape[1] for a in cst), arr.shape[1])
        cst.append(arr)

    put("ident", np.eye(128))
    sc = f32(HD ** -0.5)
    for h in range(NH):
        g = np.float64(f32(np.log(f32(1.0) - f32(2.0) ** f32(-5.0 - h))))
        diff = i[None, :] - i[:, None]
        decT = np.where(diff >= 0, np.exp(g * np.maximum(diff, 0)), 0.0) * sc
        put(f"decT{h}", decT)
        i64 = np.arange(64)
        same = (i64[None, :] // 4) == (i64[:, None] // 4)
        dd = (i64[None, :] % 4) - (i64[:, None] % 4)
        decS = np.where(same & (dd >= 0), np.exp(g * np.maximum(dd, 0)), 0.0) * sc
        a = np.zeros((128, 64)); a[:64] = decS
        put(f"decS{h}", a)
        put(f"gq{h}", np.broadcast_to(np.exp(g * (i + 1.0))[None, :], (128, 128)))
        put(f"gqs{h}", np.broadcast_to(np.exp(g * ((i64 % 4) + 1.0))[None, :], (128, 64)))
        put(f"gdec{h}", (np.exp(g * (127.0 - i)) * sc)[:, None])
        md = np.zeros((128, 16))
        for j in range(64):
            md[j, j // 4] = np.exp(g * (3.0 - (j % 4))) * sc
        put(f"maskdec{h}", md)
    cstn = np.concatenate(cst, axis=1).astype(f32)
    half = HD // 2
    inv = (np.float32(10000.0) ** (-(np.arange(half, dtype=f32)) / f32(half))).astype(f32)
    pos = np.concatenate([np.arange(TP, dtype=f32), np.tile(f32(PAST) + np.arange(4, dtype=f32), NSEQ)]).astype(f32)
    ang = (pos[None, :] * inv[:, None]).astype(f32)
    cos = np.cos(ang.astype(np.float64)).astype(f32)
    sin = np.sin(ang.astype(np.float64)).astype(f32)
    rot = np.zeros((128, 2, T), f32)
    rot[:64, 0] = cos; rot[64:, 0] = cos
    rot[:64, 1] = sin; rot[64:, 1] = -sin
    return cstn, cols, rot


def host_vecs(inp):
    cols = {}
    parts = []

    def put(name, arr):
        cols[name] = (sum(a.shape[1] for a in parts), arr.shape[1])
        parts.append(np.ascontiguousarray(arr, dtype=np.float32))

    for l in range(DEPTH):
        put(f"n1g{l}", inp["norm1_g"][l].reshape(KC, 128).T)
        put(f"n2g{l}", inp["norm2_g"][l].reshape(KC, 128).T)
        cw = inp["conv_w"][l].reshape(CW, 4, 128).transpose(2, 1, 0).reshape(128, 4 * CW)
        put(f"cw{l}", cw)
        put(f"cb{l}", inp["conv_b"][l].reshape(4, 128).T)
        put(f"lng{l}", inp["conv_ln_g"][l].reshape(4, 128).T)
        put(f"lnb{l}", inp["conv_ln_b"][l].reshape(4, 128).T)
        put(f"gng{l}", inp["ret_gn_g"][l].reshape(4, 128).T)
    put("fng", inp["final_norm_g"].reshape(KC, 128).T)
    return np.concatenate(parts, axis=1), cols


def build_program(ccols, vcols, NCST, NVEC, dbg=False):
    import os
    STAGE = int(os.environ.get("KSTAGE", "9"))
    NL = int(os.environ.get("KLAYERS", str(DEPTH)))
    KSUB = int(os.environ.get("KSUB", "9"))
    KNH = int(os.environ.get("KHEADS", str(NH)))
    KPART = int(os.environ.get("KPART", "9"))
    KNC = int(os.environ.get("KNC", "99"))
    nc = bass.Bass("TRN2", target_bir_lowering=False)
    pg = Prog()
    es = ExitStack()

    def dram(name, shape, kind):
        return nc.dram_tensor(name, list(shape), F32, kind=kind).ap()

    xin = dram("xin", [T, D], "ExternalInput")
    cache = dram("cache", [DEPTH, NSEQ * 30, CD], "ExternalInput")
    state = dram("state", [DEPTH, NSEQ, NH, HD, HD], "ExternalInput")
    w_in = dram("w_in", [DEPTH, D, 3072], "ExternalInput")
    w_out = dram("w_out", [DEPTH, D, D], "ExternalInput")
    w_up = dram("w_up", [DEPTH, D, DFF], "ExternalInput")
    w_down = dram("w_down", [DEPTH, DFF, D], "ExternalInput")
    cst_d = dram("cst", [128, NCST], "ExternalInput")
    vec_d = dram("vecs", [128, NVEC], "ExternalInput")
    rot_d = dram("rot", [128, 2, T], "ExternalInput")
    y_d = dram("y", [T, D], "ExternalOutput")
    ncp_d = dram("ncp", [DEPTH, 30, CD], "ExternalOutput")
    nrp_d = dram("nrp", [DEPTH, NH, HD, HD], "ExternalOutput")
    ncs_d = dram("ncs", [DEPTH, NSEQ, 30, CD], "ExternalOutput")
    nrs_d = dram("nrs", [DEPTH, NSEQ, NH, HD, HD], "ExternalOutput")

    SB_END = 229376
    off = [24576]

    def sb(name, shape, dt, at=None):
        nbytes = int(np.prod(shape[1:])) * (4 if dt == F32 else 2)
        nbytes = (nbytes + 31) // 32 * 32
        if at is None:
            o = off[0]
            off[0] += nbytes
        else:
            o = at
        assert o + nbytes <= SB_END, (name, o, nbytes, o + nbytes - SB_END)
        t = nc.alloc_sbuf_tensor_at(name, list(shape), dt, offset=o)
        pg.register(name, o, o + nbytes)
        return t, o + nbytes

    xT, _ = sb("xT", [128, KC, T], F32)
    hb_off = off[0]
    hb, _ = sb("hb", [128, KC, T], BF16)
    ring = [sb(f"ring{i}", [128, KC, 128], BF16)[0] for i in range(NSLOT)]
    cst, _ = sb("cst", [128, NCST], F32)
    vec, _ = sb("vec", [128, NVEC], F32)
    identb, _ = sb("identb", [128, 128], BF16)
    onesb, _ = sb("onesb", [128, 128], BF16)
    epsc, _ = sb("epsc", [128, 8], F32)
    A0 = off[0]

    sq = [sb(f"sq{i}", [128, KC, 512], BF16, at=A0 + i * 8192)[0] for i in range(2)]
    rstd, _ = sb("rstd", [128, T], F32, at=A0 + 16384)
    NXT = 8
    xtok = [sb(f"xtok{i}", [128, D], F32, at=A0 + 25600 + i * 4096)[0] for i in range(NXT)]
    XTOK_NAMES = [f"xtok{i}" for i in range(NXT)]
    a = A0
    qT, a = sb("qT", [128, T], BF16, at=a)
    kT, a = sb("kT", [128, T], BF16, at=a)
    vT, a = sb("vT", [128, T], BF16, at=a)
    gs, a = sb("gs", [128, T], BF16, at=a)
    retout, a = sb("retout", [128, 4, T], BF16, at=a)
    R_END = a
    qdT, a = sb("qdT", [128, T], BF16, at=a)
    rott, a = sb("rott", [128, 2, T], F32, at=a)
    st32, a = sb("st32", [128, NSEQ, 128], F32, at=a)
    stsb, a = sb("stsb", [128, NSEQ, 128], BF16, at=a)
    qmask, a = sb("qmask", [128, NSEQ, 64], BF16, at=a)
    kmask, a = sb("kmask", [128, NSEQ, 128], BF16, at=a)
    k32a, _ = sb("k32a", [128, 512], F32, at=a - 4096)
    tmpKB, _ = sb("tmpKB", [128, 512], F32, at=a - 2048)
    k32b, a = sb("k32b", [128, 512], F32, at=a)
    k32 = [k32a, k32b]
    tmpA, a = sb("tmpA", [128, 512], F32, at=a)
    tmpB, a = sb("tmpB", [128, 512], F32, at=a)
    s32, a = sb("s32", [128, 128], F32, at=a)
    stb, a = sb("stb", [128, 512], BF16, at=a)
    ktok, a = sb("ktok", [128, 512], BF16, at=a)
    vtok, a = sb("vtok", [128, 512], BF16, at=a)
    vdec, a = sb("vdec", [128, 512], BF16, at=a)
    STb, a = sb("STb", [128, 512], BF16, at=a)
    onb, a = sb("onb", [128, 512], BF16, at=a)
    bnst, a = sb("bnst", [128, 32], F32, at=a)
    mv, a = sb("mv", [128, 16], F32, at=a)
    RET_NAMES = ["qT", "kT", "vT", "gs", "retout", "rott", "st32", "stsb", "qmask", "kmask", "tmpA", "tmpB", "s32", "stb",
                 "qdT", "ktok", "vtok", "vdec", "STb", "onb", "bnst", "mv", "k32a", "k32b", "tmpKB"]
    a = R_END
    glub_p, a = sb("glub_p", [128, 4, TP + 32], BF16, at=a)
    glub_s, a = sb("glub_s", [128, 4, NSEQ, 34], BF16, at=a)
    sig_off = a
    sig, a = sb("sig", [128, T], F32, at=a)
    g32p, a = sb("g32p", [128, 4, 32], F32, at=a)
    g32s, a = sb("g32s", [128, 4, 64], F32, at=a)
    ctok_off = a
    ctok, a = sb("ctok", [128, CD], F32, at=a)
    otok, a = sb("otok", [128, CD], F32, at=a)
    cbb, a = sb("cbb", [128, 4, 512], BF16, at=a)
    mean, a = sb("mean", [128, 512], F32, at=a)
    msq, a = sb("msq", [128, 512], F32, at=a)
    crs, a = sb("crs", [128, 512], F32, at=a)
    CONVA_NAMES = ["glub_p", "glub_s", "sig", "g32p", "g32s", "ctok", "otok"]
    convout, _ = sb("convout", [128, 4, T], BF16, at=A0)
    c32, _ = sb("c32", [128, 4, 512], F32, at=sig_off)
    csq, _ = sb("csq", [128, 4, 512], BF16, at=ctok_off)
    dg, _ = sb("dg", [128, 4 * CW, 128], BF16, at=hb_off)
    CONVB_NAMES = ["convout", "c32", "csq", "dg", "cbb", "mean", "msq", "crs"]
    hid = [sb(f"hid{i}", [128, 8, T], BF16, at=A0 + i * 8 * T * 2)[0] for i in range(2)]
    rtmp = [sb(f"rtmp{i}", [128, 512], BF16, at=A0 + 2 * 8 * T * 2 + i * 1024)[0] for i in range(2)]
    FFN_NAMES = ["hid0", "hid1", "rtmp0", "rtmp1"]
    ring2 = [sb(f"ringB{i}", [128, KC, 128], BF16, at=A0 + 2 * 8 * T * 2 + 2048 + i * 2048)[0] for i in range(8)]
    NORM_NAMES = ["sq0", "sq1", "rstd"]

    def cs(name, r0=0, r1=128):
        c0, n = ccols[name]
        return cst[r0:r1, c0:c0 + n]

    def vc(name, j=None):
        c0, n = vcols[name]
        if j is None:
            return vec[:, c0:c0 + n]
        return vec[:, c0 + j:c0 + j + 1]

    ident = cs("ident")

    psb = [es.enter_context(nc.psum_tensor(f"ps{i}", [128, 512], F32)) for i in range(8)]
    NB = [0, 1, 2, 4, 5]

    def PK(b):
        return ("ps", b)

    PK6 = [("ps6", i) for i in range(4)]

    def act(out, in_, func, R, W, bias=None, scale=None):
        kw = {}
        if bias is not None:
            kw["bias"] = bias
        if scale is not None:
            kw["scale"] = scale
        return pg.add("act", lambda e: e.activation(out=out, in_=in_, func=func, **kw), R, W)

    def dve(fn, R, W):
        return pg.add("dve", fn, R, W)

    def mm(out, lhsT, rhs, start, stop, R, W):
        return pg.add("pe", lambda e: e.matmul(out, lhsT, rhs, start=start, stop=stop), R, W)

    def tr(out, in_, idn, R, W):
        return pg.add("pe", lambda e: e.transpose(out, in_, idn), R, W)

    def dma(q, out, in_, R, W, group):
        return pg.add(q, lambda e: e.dma_start(out=out, in_=in_), R, W, dma=group)

    ogrp = [0]

    def out_dma(out, in_, R):
        g = f"o{ogrp[0] % 8}"
        ogrp[0] += 1
        return dma("sp", out, in_, R, (), g)

    mmb = [0]

    def next_bank(banks=(0, 1, 2)):
        b = banks[mmb[0] % len(banks)]
        mmb[0] += 1
        return b

    units = []
    for l in range(NL):
        if STAGE >= 1:
            for h in range(KNH):
                for grp in range(4):
                    c0 = 1024 + grp * 512 + h * 128
                    units.append((w_in[l, :, c0:c0 + 128], 8))
        if STAGE >= 2:
            for cc in range(4):
                units.append((w_in[l, :, 512 + cc * 128:512 + (cc + 1) * 128], 8))
                units.append((w_in[l, :, cc * 128:(cc + 1) * 128], 8))
        if STAGE >= 4:
            for m in range(8):
                units.append((w_out[l, :, m * 128:(m + 1) * 128], 8))
        for g in range(4 if STAGE >= 5 else 0):
            for f in range(8):
                c0 = (g * 8 + f) * 128
                units.append((w_up[l, :, c0:c0 + 128], 8))
            for m in range(8):
                units.append((w_down[l, g * 1024:(g + 1) * 1024, m * 128:(m + 1) * 128], 8))
    TAIL = STAGE >= 5 and NL == DEPTH
    if TAIL:
        tail_units = units[-8:]
        units = units[:-8]
    ucur = [0]
    uload = [0]

    def load_units_upto(i):
        while uload[0] <= i and uload[0] < len(units):
            u = uload[0]
            ap, kcn = units[u]
            s = u % NSLOT
            dma("pool", ring[s][:, 0:kcn, :], ap.rearrange("(kc p) n -> p kc n", p=128), (), [(f"ring{s}",)], f"w{s}")
            uload[0] += 1

    def next_unit(look=NSLOT - 1):
        u = ucur[0]
        ucur[0] += 1
        load_units_upto(u + look)
        return u % NSLOT, units[u][1]

    def proj(rhs_fn, rhs_keys, evac, banks=(0, 1, 2)):
        s, kcn = next_unit()
        for tt, (t0, n) in enumerate(TTS):
            b = next_bank(banks)
            for kc in range(kcn):
                mm(psb[b][:, 0:n], ring[s][:, kc, :], rhs_fn(kc, t0, n), kc == 0, kc == kcn - 1,
                   [(f"ring{s}",)] + rhs_keys(kc, t0, n), [PK(b)])
            evac(tt, t0, n, b)

    def proj2(rhs_fn, rhs_keys, evac_a, evac_b, banks=(0, 1, 2)):
        sa, kcn = next_unit(NSLOT - 1)
        sb_, _ = next_unit(NSLOT - 2)
        for tt, (t0, n) in enumerate(TTS):
            for s_, ev in ((sa, evac_a), (sb_, evac_b)):
                b = next_bank(banks)
                for kc in range(kcn):
                    mm(psb[b][:, 0:n], ring[s_][:, kc, :], rhs_fn(kc, t0, n), kc == 0, kc == kcn - 1,
                       [(f"ring{s_}",)] + rhs_keys(kc, t0, n), [PK(b)])
                ev(tt, t0, n, b)

    def tkeys(name, idx, t0, n):
        return [(name, idx, c) for c in chunks_of(t0, n)]

    dma("sp", cst[:, :], cst_d, (), ["cst"], "c0")
    dma("sp", vec[:, :], vec_d, (), ["vec"], "c1")
    dve(lambda e: e.tensor_copy(out=identb[:, :], in_=ident), ["cst"], ["identb"])
    dve(lambda e: e.memset(onesb[:, :], 1.0), [], ["onesb"])
    dve(lambda e: e.memset(epsc[:, :], EPS), [], ["epsc"])

    pg.phase_begin(XTOK_NAMES)
    for i in range(NCHK):
        r0 = i * 128
        rows = 128 if i < 16 else 64
        s = i % NXT
        dma("sp", xtok[s][0:rows, :], xin[r0:r0 + rows, :], (), [(f"xtok{s}",)], f"x{s}")
        for half in range(2):
            b = next_bank((0, 1, 2, 4))
            for cq in range(4):
                c = half * 4 + cq
                tr(psb[b][:, cq * 128:cq * 128 + rows], xtok[s][0:rows, c * 128:(c + 1) * 128], ident[0:rows, 0:rows],
                   [(f"xtok{s}",), "cst"], [PK(b)])
            src = psb[b][:, :].rearrange("p (c t) -> p c t", c=4)[:, :, 0:rows]
            dst = xT[:, half * 4:half * 4 + 4, r0:r0 + rows]
            W = [("xT", c, i) for c in range(half * 4, half * 4 + 4)]
            if (i + half) % 2 == 0:
                act(dst, src, AF.Copy, [PK(b)], W)
            else:
                dve(lambda e, dst=dst, src=src: e.tensor_copy(out=dst, in_=src), [PK(b)], W)

    def rmsnorm_tile(gname, to_hb, tt):
        t0, n = TTS[tt]
        s = tt % 2
        act(sq[s][:, :, 0:n], xT[:, :, t0:t0 + n], AF.Square, [k for c in range(KC) for k in tkeys("xT", c, t0, n)], [(f"sq{s}",)])
        for c in range(KC):
            mm(psb[NB[tt]][:, 0:n], onesb[:, :], sq[s][:, c, 0:n], c == 0, c == KC - 1, [(f"sq{s}",), "onesb"], [PK(NB[tt])])
        r = rstd[:, t0:t0 + n]
        act(r, psb[NB[tt]][:, 0:n], AF.Ln, [PK(NB[tt]), "epsc"], [("rstd", tt)], bias=epsc[:, 0:1], scale=1.0 / D)
        act(r, r, AF.Exp, [("rstd", tt)], [("rstd", tt)], scale=-0.5)
        for c in range(KC):
            src = xT[:, c, t0:t0 + n]
            if to_hb:
                dst = hb[:, c, t0:t0 + n]
                W = tkeys("hb", c, t0, n)
            else:
                dst = src
                W = tkeys("xT", c, t0, n)
            dve(lambda e, dst=dst, src=src, c=c: e.scalar_tensor_tensor(
                out=dst, in0=src, scalar=vc(gname, c), in1=rstd[:, t0:t0 + n], op0=ALU.mult, op1=ALU.mult),
                tkeys("xT", c, t0, n) + [("rstd", tt), "vec"], W)

    def rmsnorm(gname, to_hb):
        pg.phase_begin(NORM_NAMES + (["hb"] if to_hb else []))
        for tt in range(len(TTS)):
            rmsnorm_tile(gname, to_hb, tt)

    def hb_rhs(kc, t0, n):
        return hb[:, kc, t0:t0 + n]

    def hb_keys(kc, t0, n):
        return tkeys("hb", kc, t0, n)

    for l in range(NL):
        if STAGE < 1:
            break
        rmsnorm(f"n1g{l}", True)

        pg.phase_begin(RET_NAMES)
        dma("sp", rott[:, :, :], rot_d, (), ["rott"], "c2")
        dve(lambda e: e.memset(qmask[:, :, :], 0.0), [], ["qmask"])

        def make_head(h):
            g128 = float(np.float32(GAM[h]) ** 128)
            g4 = float(np.float32(GAM[h]) ** 4)
            dma("sp", st32[:, :, :], state[l, :, h, :, :].rearrange("b d v -> d b v"), (), ["st32"] + [("st32", i) for i in range(4)], "c3")
            u0 = ucur[0]
            ucur[0] += 4
            load_units_upto(u0 + 3)
            slots = [(u0 + i) % NSLOT for i in range(4)]
            pool = lambda fn, R, W: pg.add("pool", fn, R, W)

            def p_mm(ui, tt):
                t0, n = TTS[tt]
                b = next_bank((0, 1))
                s_ = slots[ui]
                for kc in range(KC):
                    mm(psb[b][:, 0:n], ring[s_][:, kc, :], hb[:, kc, t0:t0 + n], kc == 0, kc == KC - 1,
                       [(f"ring{s_}",)] + tkeys("hb", kc, t0, n), [PK(b)])
                return b

            def P_q(tt, h=h):
                t0, n = TTS[tt]
                b = p_mm(0, tt)
                ps = psb[b]
                dve(lambda e: e.tensor_tensor(out=tmpA[:, 0:n], in0=ps[:, 0:n], in1=rott[:, 0, t0:t0 + n], op=ALU.mult),
                    [PK(b), "rott"], ["tmpA"])
                dve(lambda e: e.tensor_tensor(out=tmpB[0:64, 0:n], in0=ps[64:128, 0:n], in1=rott[64:128, 1, t0:t0 + n], op=ALU.mult),
                    [PK(b), "rott"], [("tmpB", 0)])
                dve(lambda e: e.tensor_tensor(out=tmpB[64:128, 0:n], in0=ps[0:64, 0:n], in1=rott[0:64, 1, t0:t0 + n], op=ALU.mult),
                    [PK(b), "rott"], [("tmpB", 1)])
                dve(lambda e: e.tensor_tensor(out=tmpA[:, 0:n], in0=tmpA[:, 0:n], in1=tmpB[:, 0:n], op=ALU.add),
                    ["tmpA", ("tmpB", 0), ("tmpB", 1)], ["tmpA"])
                act(qT[:, t0:t0 + n], tmpA[:, 0:n], AF.Copy, ["tmpA"], tkeys("qT", 0, t0, n))
                if tt < 4:
                    g_ap = cs(f"gq{h}")
                    in1 = bass.AP(g_ap.tensor, g_ap.offset, [[g_ap.ap[0][0], 128], [0, 4], [1, 128]])
                    dve(lambda e: e.tensor_tensor(out=qdT[:, t0:t0 + n].rearrange("p (c i) -> p c i", c=4),
                                                  in0=tmpA[:, 0:n].rearrange("p (c i) -> p c i", c=4), in1=in1, op=ALU.mult),
                        ["tmpA", "cst"], tkeys("qdT", 0, t0, n))
                else:
                    dve(lambda e: e.tensor_tensor(out=qdT[:, t0:t0 + n], in0=tmpA[:, 0:n], in1=cs(f"gqs{h}"), op=ALU.mult),
                        ["tmpA", "cst"], tkeys("qdT", 0, t0, n))

            def P_k(tt):
                t0, n = TTS[tt]
                b = p_mm(1, tt)
                ps = psb[b]
                dve(lambda e: e.tensor_tensor(out=k32b[:, 0:n], in0=ps[:, 0:n], in1=rott[:, 0, t0:t0 + n], op=ALU.mult),
                    [PK(b), "rott"], [("k32b",)])
                dve(lambda e: e.tensor_tensor(out=tmpKB[0:64, 0:n], in0=ps[64:128, 0:n], in1=rott[64:128, 1, t0:t0 + n], op=ALU.mult),
                    [PK(b), "rott"], [("tmpKB", 0)])
                dve(lambda e: e.tensor_tensor(out=tmpKB[64:128, 0:n], in0=ps[0:64, 0:n], in1=rott[0:64, 1, t0:t0 + n], op=ALU.mult),
                    [PK(b), "rott"], [("tmpKB", 1)])
                dve(lambda e: e.tensor_tensor(out=kT[:, t0:t0 + n], in0=k32b[:, 0:n], in1=tmpKB[:, 0:n], op=ALU.add),
                    [("k32b",), ("tmpKB", 0), ("tmpKB", 1)], tkeys("kT", 0, t0, n))

            def P_v(tt):
                t0, n = TTS[tt]
                b = p_mm(2, tt)
                act(vT[:, t0:t0 + n], psb[b][:, 0:n], AF.Copy, [PK(b)], tkeys("vT", 0, t0, n))

            def P_g(tt):
                t0, n = TTS[tt]
                b = p_mm(3, tt)
                act(gs[:, t0:t0 + n], psb[b][:, 0:n], AF.Silu, [PK(b)], tkeys("gs", 0, t0, n))

            def chunk_steps(tt, h=h, g128=g128, g4=g4):
                t0, n = TTS[tt]
                smp = tt == 4
                nck = 1 if smp else 4
                w = 64 if smp else 128
                cl = [4 * tt + ci for ci in range(nck)]
                kq = [("qT", 0, c) for c in cl]
                kk = [("kT", 0, c) for c in cl]
                kv = [("vT", 0, c) for c in cl]
                nn = nck * 128
                ob = 5 if tt % 2 == 0 else 2

                def C1():
                    for ci in range(nck):
                        cols = slice(t0 + ci * 128, t0 + ci * 128 + w)
                        mm(psb[3][0:w, ci * 128:(ci + 1) * 128], kT[:, cols], identb[:, :], True, True, kk + ["identb"], [PK(3)])
                    for ci in range(nck):
                        cols = slice(t0 + ci * 128, t0 + ci * 128 + w)
                        mm(psb[7][0:w, ci * 128:(ci + 1) * 128], vT[:, cols], identb[:, :], True, True, kv + ["identb"], [PK(7)])
                    act(ktok[0:w, 0:nn], psb[3][0:w, 0:nn], AF.Copy, [PK(3)], ["ktok"])
                    if not smp:
                        act(vdec[:, 0:nn], psb[7][:, 0:nn], AF.Identity, [PK(7), "cst"], ["vdec"], scale=cs(f"gdec{h}"))
                    act(vtok[0:w, 0:nn], psb[7][0:w, 0:nn], AF.Copy, [PK(7)], ["vtok"])

                def C2():
                    for ci in range(nck):
                        cols = slice(t0 + ci * 128, t0 + ci * 128 + w)
                        mm(psb[4][0:w, ci * 128:ci * 128 + w], kT[:, cols], qT[:, cols], True, True, kk + kq, [PK(4)])
                    if not smp:
                        d_ap = cs(f"decT{h}")
                        dec4 = bass.AP(d_ap.tensor, d_ap.offset, [[d_ap.ap[0][0], 128], [0, 4], [1, 128]])
                        dve(lambda e: e.tensor_tensor(out=STb[:, 0:512].rearrange("p (c i) -> p c i", c=4),
                                                      in0=psb[4][:, 0:512].rearrange("p (c i) -> p c i", c=4), in1=dec4, op=ALU.mult),
                            [PK(4), "cst"], ["STb"])
                    else:
                        dve(lambda e: e.tensor_tensor(out=STb[0:64, 0:64], in0=psb[4][0:64, 0:64], in1=cs(f"decS{h}", 0, 64), op=ALU.mult),
                            [PK(4), "cst"], ["STb"])

                def o_mms(ci):
                    c = cl[ci]
                    cols = slice(t0 + ci * 128, t0 + ci * 128 + 128)
                    oreg = psb[ob][:, ci * 128:(ci + 1) * 128]
                    mm(oreg, STb[:, ci * 128:(ci + 1) * 128], vtok[:, ci * 128:(ci + 1) * 128], True, c == 0, ["STb", "vtok"], [PK(ob)])
                    if c > 0:
                        pv = (ci - 1) % 4
                        mm(oreg, qdT[:, cols], stb[:, pv * 128:(pv + 1) * 128], False, True, [("qdT", 0, c), ("stb", pv)], [PK(ob)])

                def C3():
                    if not smp:
                        for ci in range(nck):
                            mm(psb[6][:, ci * 128:(ci + 1) * 128], ktok[:, ci * 128:(ci + 1) * 128], vdec[:, ci * 128:(ci + 1) * 128], True, True,
                               ["ktok", "vdec"], PK6)
                        o_mms(0)
                        for ci in range(nck):
                            c = cl[ci]
                            ureg = psb[6][:, ci * 128:(ci + 1) * 128]
                            sreg = stb[:, ci * 128:(ci + 1) * 128]
                            if c == 0:
                                dve(lambda e, ureg=ureg, sreg=sreg: e.tensor_copy(out=sreg, in_=ureg), PK6, [("stb", ci)])
                                dve(lambda e, ureg=ureg: e.tensor_copy(out=s32[:, :], in_=ureg), PK6, ["s32"])
                            else:
                                if c < 15:
                                    dve(lambda e, ureg=ureg, sreg=sreg: e.scalar_tensor_tensor(
                                        out=sreg, in0=s32[:, :], scalar=g128, in1=ureg, op0=ALU.mult, op1=ALU.add), PK6 + ["s32"], [("stb", ci)])
                                dve(lambda e, ureg=ureg: e.scalar_tensor_tensor(out=s32[:, :], in0=s32[:, :], scalar=g128, in1=ureg,
                                                                                op0=ALU.mult, op1=ALU.add), PK6 + ["s32"], ["s32"])
                            if c == 15:
                                out_dma(nrp_d[l, h, :, :], s32[:, :], ["s32"])
                    else:
                        mm(psb[ob][0:64, 0:128], STb[0:64, 0:64], vtok[0:64, 0:128], True, False, ["STb", "vtok"], [PK(ob)])
                        qm_dst = bass.AP(qmask, 0, [[NSEQ * 64, 128], [68, NSEQ], [1, 4]])
                        dve(lambda e: e.tensor_copy(out=qm_dst, in_=qdT[:, t0:t0 + 64].rearrange("p (b i) -> p b i", i=4)),
                            [("qdT", 0, 16), "qmask"], ["qmask"])
                        act(stsb[:, :, :], st32[:, :, :], AF.Copy, ["st32"] + [("st32", i) for i in range(4)], ["stsb"])
                        for bq in range(NSEQ):
                            mm(psb[ob][0:64, 0:128], qmask[:, bq, :], stsb[:, bq, :], False, bq == NSEQ - 1, ["qmask", "stsb"], [PK(ob)])

                def C4():
                    if not smp:
                        for ci in range(1, nck):
                            o_mms(ci)
                    else:
                        pg.phase_begin(["kmask"])
                        km_in0 = bass.AP(ktok, 0, [[512, 64], [0, NSEQ], [1, 128]])
                        md = cs(f"maskdec{h}", 0, 64)
                        km_in1 = bass.AP(md.tensor, md.offset, [[md.ap[0][0], 64], [1, NSEQ], [0, 128]])
                        dve(lambda e: e.tensor_tensor(out=kmask[0:64, :, :], in0=km_in0, in1=km_in1, op=ALU.mult), ["ktok", "cst"], ["kmask"])
                        for bq4 in range(4):
                            ub = 6 if bq4 % 2 == 0 else 3
                            ukeys = PK6 if ub == 6 else [PK(3)]
                            for bi in range(4):
                                bq = bq4 * 4 + bi
                                mm(psb[ub][:, bi * 128:(bi + 1) * 128], kmask[0:64, bq, :], vtok[0:64, 0:128], True, True, ["kmask", "vtok"], ukeys)
                            dve(lambda e, bq4=bq4, ub=ub: e.scalar_tensor_tensor(
                                out=st32[:, bq4 * 4:bq4 * 4 + 4, :], in0=st32[:, bq4 * 4:bq4 * 4 + 4, :], scalar=g4,
                                in1=psb[ub][:, :].rearrange("p (b v) -> p b v", b=4), op0=ALU.mult, op1=ALU.add),
                                ukeys + [("st32", bq4)], [("st32", bq4)])
                        out_dma(nrs_d[l, :, h, :, :].rearrange("b d v -> d b v"), st32[:, :, :], ["st32"] + [("st32", i) for i in range(4)])

                def G1():
                    for ci in range(nck):
                        dve(lambda e, ci=ci: e.bn_stats(out=bnst[0:w, ci * 6:ci * 6 + 6], in_=psb[ob][0:w, ci * 128:(ci + 1) * 128]), [PK(ob)], [("bnst", ci)])
                    for ci in range(nck):
                        dve(lambda e, ci=ci: e.bn_aggr(out=mv[0:w, ci * 2:ci * 2 + 2], in_=bnst[0:w, ci * 6:ci * 6 + 6]), [("bnst", ci)], [("mv", ci)])
                    mvk = [("mv", ci) for ci in range(nck)]
                    var_ap = bass.AP(mv, 1, [[16, w], [2, nck]])
                    act(mv[0:w, 8:8 + nck], var_ap, AF.Sqrt, mvk + ["epsc"], ["rs"], bias=epsc[0:w, 0:1])

                def G2():
                    dve(lambda e: e.reciprocal(out=mv[0:w, 12:12 + nck], in_=mv[0:w, 8:8 + nck]), ["rs"], ["rs2"])
                    for ci in range(nck):
                        dve(lambda e, ci=ci: e.tensor_scalar(out=onb[0:w, ci * 128:(ci + 1) * 128], in0=psb[ob][0:w, ci * 128:(ci + 1) * 128],
                                                             scalar1=mv[0:w, 2 * ci:2 * ci + 1], scalar2=mv[0:w, 12 + ci:13 + ci],
                                                             op0=ALU.subtract, op1=ALU.mult), [PK(ob), ("mv", ci), "rs2"], [("onb", ci)])

                def C5():
                    for ci in range(nck):
                        mm(psb[4][:, ci * 128:ci * 128 + w], onb[0:w, ci * 128:(ci + 1) * 128], identb[0:w, 0:w], True, True, [("onb", ci), "identb"], [PK(4)])
                    dve(lambda e, gcol=vc(f"gng{l}", h): e.scalar_tensor_tensor(out=retout[:, h, t0:t0 + n], in0=psb[4][:, 0:n], scalar=gcol,
                                                                                in1=gs[:, t0:t0 + n], op0=ALU.mult, op1=ALU.mult),
                        [PK(4), "vec"] + [("gs", 0, c) for c in cl], [("retout", h, c) for c in cl])

                return [C1, C2, C3, C4, G1, G2, C5]

            return dict(P_q=P_q, P_k=P_k, P_v=P_v, P_g=P_g, chunk_steps=chunk_steps, u0=u0)

        nothing = lambda: None
        ctx = None
        for h in range(KNH):
            if ctx is None:
                ctx = make_head(h)
                pg.phase_begin(["k32a", "tmpKB"])
                for fn in ("P_q", "P_k", "P_v", "P_g"):
                    ctx[fn](0)
                ctx["gdone"] = {0}
            nxt = None
            prev = None
            for tt in range(6):
                cur = ctx["chunk_steps"](tt) if tt < 5 else None
                C1, C2, C3, C4 = cur[0:4] if cur else [nothing] * 4
                G1, G2, G3 = prev[4:7] if prev else [nothing] * 3
                if tt + 1 < 5:
                    pc, pt = ctx, tt + 1
                elif tt == 5 and h + 1 < KNH:
                    G1()
                    G1 = nothing
                    nxt = make_head(h + 1)
                    pg.phase_begin(["k32a", "tmpKB"])
                    pc, pt = nxt, 0
                else:
                    pc = None
                G1()
                C2()
                C1()
                if pc:
                    pc["P_q"](pt)
                G2()
                C3()
                if pc:
                    pc["P_k"](pt)
                C4()
                if pc:
                    pc["P_v"](pt)
                G3()
                if pc:
                    gd = pc.setdefault("gdone", set())
                    for t2 in (pt, pt + 1):
                        if t2 < 5 and t2 not in gd:
                            pc["P_g"](t2)
                            gd.add(t2)
                if tt == 3:
                    load_units_upto(ctx["u0"] + 7)
                prev = cur
            ctx = nxt

        if STAGE < 2:
            continue
        pg.phase_begin(CONVA_NAMES)
        for cc in range(4):
            dve(lambda e, cc=cc: e.memset(glub_p[:, cc, 0:30], 0.0), [], [("glub_p", cc, "z")])
        for rt in range(4):
            dma("sp", ctok[0:120, :], cache[l, rt * 120:(rt + 1) * 120, :], (), ["ctok"], "c4")
            for cc in range(4):
                tr(psb[6][:, cc * 120:(cc + 1) * 120], ctok[0:120, cc * 128:(cc + 1) * 128], ident[0:120, 0:120], ["ctok", "cst"], PK6)
            act(glub_s[:, :, rt * 4:rt * 4 + 4, 0:30], psb[6][:, 0:480].rearrange("p (c s i) -> p c s i", c=4, s=4),
                AF.Copy, PK6, [("glub_s", "c", rt)])
        out_dma(ncs_d[l, :, 0:26, :], cache[l, :, :].rearrange("(b r) c -> b r c", r=30)[:, 4:30, :], [])

        for cc in range(4):
            proj(hb_rhs, hb_keys, lambda tt, t0, n, b: act(sig[:, t0:t0 + n], psb[b][:, 0:n], AF.Sigmoid, [PK(b)], [("sig", tt)]))

            def glu_evac(tt, t0, n, b, cc=cc):
                if tt < 4:
                    dve(lambda e: e.tensor_tensor(out=glub_p[:, cc, 30 + t0:30 + t0 + n], in0=psb[b][:, 0:n], in1=sig[:, t0:t0 + n], op=ALU.mult),
                        [PK(b), ("sig", tt)], [("glub_p", cc, tt)])
                    if tt == 3:
                        dve(lambda e: e.tensor_tensor(out=g32p[:, cc, 0:30], in0=psb[b][:, 482:512], in1=sig[:, TP - 30:TP], op=ALU.mult),
                            [PK(b), ("sig", tt)], [("g32p", cc)])
                else:
                    dve(lambda e: e.tensor_tensor(out=g32s[:, cc, :], in0=psb[b][:, 0:64], in1=sig[:, TP:T], op=ALU.mult),
                        [PK(b), ("sig", tt)], [("g32s", cc)])
                    act(glub_s[:, cc, :, 30:34], g32s[:, cc, :].rearrange("p (s i) -> p s i", i=4), AF.Copy, [("g32s", cc)], [("glub_s", "n", cc)])
            proj(hb_rhs, hb_keys, glu_evac)

        for cc in range(4):
            tr(psb[6][0:30, cc * 128:(cc + 1) * 128], g32p[:, cc, 0:30], ident, [("g32p", cc), "cst"], PK6)
        act(otok[0:30, :], psb[6][0:30, :], AF.Copy, PK6, ["otok"])
        out_dma(ncp_d[l, :, :], otok[0:30, :], ["otok"])
        for cc in range(4):
            tr(psb[6][0:64, cc * 128:(cc + 1) * 128], g32s[:, cc, :], ident, [("g32s", cc), "cst"], PK6)
        act(otok[0:64, :], psb[6][0:64, :], AF.Copy, PK6, ["otok"])
        for bq in range(NSEQ):
            out_dma(ncs_d[l, bq, 26:30, :], otok[bq * 4:bq * 4 + 4, :], ["otok"])

        if STAGE < 3:
            continue
        pg.phase_begin(CONVB_NAMES)
        for cc in range(4):
            c0, _ = vcols[f"cw{l}"]
            in0 = bass.AP(identb, 0, [[128, 128], [0, CW], [1, 128]])
            w_ap = vec[:, c0 + cc * CW:c0 + (cc + 1) * CW]
            in1 = bass.AP(w_ap.tensor, w_ap.offset, [[w_ap.ap[0][0], 128], [1, CW], [0, 128]])
            dve(lambda e, cc=cc, in0=in0, in1=in1: e.tensor_tensor(out=dg[:, cc * CW:(cc + 1) * CW, :], in0=in0, in1=in1, op=ALU.mult),
                ["identb", "vec"], [("dg", cc)])

        for tt, (t0, n) in enumerate(TTS):
            for cc in range(4):
                b = next_bank((0, 1, 2))
                for j in range(CW):
                    if tt < 4:
                        rhs = glub_p[:, cc, t0 + j:t0 + j + 512]
                        outp = psb[b][:, 0:512]
                        R = [("glub_p", cc, tt), ("glub_p", cc, "z" if tt == 0 else tt - 1)]
                    else:
                        rhs = glub_s[:, cc, :, j:j + 4]
                        outp = psb[b][:, 0:64].rearrange("p (s i) -> p s i", i=4)
                        R = [("glub_s", "n", cc)] + [("glub_s", "c", rt) for rt in range(4)]
                    mm(outp, dg[:, cc * CW + j, :], rhs, j == 0, j == CW - 1, R + [("dg", cc)], [PK(b)])
                act(c32[:, cc, 0:n], psb[b][:, 0:n], AF.Identity, [PK(b), "vec"], [("c32", cc)], bias=vc(f"cb{l}", cc))
                act(csq[:, cc, 0:n], psb[b][:, 0:n], AF.Square, [PK(b), "vec"], [("csq", cc)], bias=vc(f"cb{l}", cc))
                dve(lambda e, cc=cc, n=n: e.tensor_copy(out=cbb[:, cc, 0:n], in_=c32[:, cc, 0:n]), [("c32", cc)], [("cbb", cc)])
            for cc in range(4):
                mm(psb[4][:, 0:n], onesb[:, :], cbb[:, cc, 0:n], cc == 0, cc == 3, [("cbb", cc), "onesb"], [PK(4)])
            for cc in range(4):
                mm(psb[5][:, 0:n], onesb[:, :], csq[:, cc, 0:n], cc == 0, cc == 3, [("csq", cc), "onesb"], [PK(5)])
            dve(lambda e, n=n: e.tensor_scalar(out=mean[:, 0:n], in0=psb[4][:, 0:n], scalar1=1.0 / CD, scalar2=0.0, op0=ALU.mult, op1=ALU.add), [PK(4)], ["mean"])
            dve(lambda e, n=n: e.tensor_tensor(out=msq[:, 0:n], in0=mean[:, 0:n], in1=mean[:, 0:n], op=ALU.mult), ["mean"], ["msq"])
            dve(lambda e, n=n: e.scalar_tensor_tensor(out=crs[:, 0:n], in0=psb[5][:, 0:n], scalar=1.0 / CD, in1=msq[:, 0:n],
                                                      op0=ALU.mult, op1=ALU.subtract), [PK(5), "msq"], ["crs"])
            act(crs[:, 0:n], crs[:, 0:n], AF.Sqrt, ["crs", "epsc"], ["crs"], bias=epsc[:, 0:1])
            dve(lambda e, n=n: e.reciprocal(out=crs[:, 0:n], in_=crs[:, 0:n]), ["crs"], ["crs"])
            for cc in range(4):
                dve(lambda e, cc=cc, n=n: e.tensor_tensor(out=c32[:, cc, 0:n], in0=c32[:, cc, 0:n], in1=mean[:, 0:n], op=ALU.subtract),
                    [("c32", cc), "mean"], [("c32", cc)])
                dve(lambda e, cc=cc, n=n: e.tensor_tensor(out=c32[:, cc, 0:n], in0=c32[:, cc, 0:n], in1=crs[:, 0:n], op=ALU.mult),
                    [("c32", cc), "crs"], [("c32", cc)])
                act(convout[:, cc, t0:t0 + n], c32[:, cc, 0:n], AF.Silu, [("c32", cc), "vec"], tkeys("convout", cc, t0, n),
                    bias=vc(f"lnb{l}", cc), scale=vc(f"lng{l}", cc))

        if STAGE < 4:
            continue
        def mix_rhs(kc, t0, n):
            return convout[:, kc, t0:t0 + n] if kc < 4 else retout[:, kc - 4, t0:t0 + n]

        def mix_keys(kc, t0, n):
            return tkeys("convout", kc, t0, n) if kc < 4 else tkeys("retout", kc - 4, t0, n)

        for m in range(8):
            def res_evac(tt, t0, n, b, m=m):
                dve(lambda e: e.tensor_tensor(out=xT[:, m, t0:t0 + n], in0=psb[b][:, 0:n], in1=xT[:, m, t0:t0 + n], op=ALU.add),
                    [PK(b)] + tkeys("xT", m, t0, n), tkeys("xT", m, t0, n))
            proj(mix_rhs, mix_keys, res_evac)

        if STAGE < 5:
            continue
        rmsnorm(f"n2g{l}", True)
        pg.phase_begin(FFN_NAMES)
        for g in range(4):
            hs = g % 2
            for f in range(8):
                def up_evac(tt, t0, n, b, f=f, hs=hs):
                    rs = (f + tt) % 2
                    act(rtmp[rs][:, 0:n], psb[b][:, 0:n], AF.Relu, [PK(b)], [(f"rtmp{rs}",)])
                    dve(lambda e: e.tensor_tensor(out=hid[hs][:, f, t0:t0 + n], in0=rtmp[rs][:, 0:n], in1=rtmp[rs][:, 0:n], op=ALU.mult),
                        [(f"rtmp{rs}",)], tkeys(f"hid{hs}", f, t0, n))
                proj(hb_rhs, hb_keys, up_evac)
            if TAIL and l == NL - 1 and g == 3:
                break
            for m in range(8):
                def dn_evac(tt, t0, n, b, m=m):
                    dve(lambda e: e.tensor_tensor(out=xT[:, m, t0:t0 + n], in0=psb[b][:, 0:n], in1=xT[:, m, t0:t0 + n], op=ALU.add),
                        [PK(b)] + tkeys("xT", m, t0, n), tkeys("xT", m, t0, n))
                proj(lambda kc, t0, n, hs=hs: hid[hs][:, kc, t0:t0 + n], lambda kc, t0, n, hs=hs: tkeys(f"hid{hs}", kc, t0, n), dn_evac)

    def store_chunk(i):
        r0 = i * 128
        rows = 128 if i < 16 else 64
        s = i % 2
        for half in range(2):
            b = next_bank((0, 1, 2, 4))
            for cq in range(4):
                c = half * 4 + cq
                tr(psb[b][0:rows, cq * 128:(cq + 1) * 128], xT[:, c, r0:r0 + rows], ident, [("xT", c, i), "cst"], [PK(b)])
            dst = xtok[s][0:rows, half * 512:(half + 1) * 512]
            if half == 0:
                act(dst, psb[b][0:rows, :], AF.Copy, [PK(b)], [(f"xtok{s}", half)])
            else:
                dve(lambda e, dst=dst, b=b, rows=rows: e.tensor_copy(out=dst, in_=psb[b][0:rows, :]), [PK(b)], [(f"xtok{s}", half)])
        out_dma(y_d[r0:r0 + rows, :], xtok[s][0:rows, :], [(f"xtok{s}", 0), (f"xtok{s}", 1)])

    def store_tile(tt):
        for i in (range(4 * tt, 4 * tt + 4) if tt < 4 else [16]):
            store_chunk(i)

    if TAIL:
        l = NL - 1
        pg.phase_begin([f"ringB{i}" for i in range(8)])
        for m in range(8):
            ap, kcn = tail_units[m]
            dma("pool", ring2[m][:, :, :], ap.rearrange("(kc p) n -> p kc n", p=128), (), [(f"ringB{m}",)], f"wB{m}")
        pg.phase_begin(NORM_NAMES + XTOK_NAMES[0:2])
        for step in range(7):
            if step < 5:
                t0, n = TTS[step]
                for m in range(8):
                    b = next_bank((0, 1, 2))
                    for kc in range(KC):
                        mm(psb[b][:, 0:n], ring2[m][:, kc, :], hid[1][:, kc, t0:t0 + n], kc == 0, kc == KC - 1,
                           [(f"ringB{m}",)] + tkeys("hid1", kc, t0, n), [PK(b)])
                    dve(lambda e, m=m, b=b, t0=t0, n=n: e.tensor_tensor(out=xT[:, m, t0:t0 + n], in0=psb[b][:, 0:n], in1=xT[:, m, t0:t0 + n], op=ALU.add),
                        [PK(b)] + tkeys("xT", m, t0, n), tkeys("xT", m, t0, n))
            if 1 <= step <= 5:
                rmsnorm_tile("fng", False, step - 1)
            if 2 <= step <= 6:
                store_tile(step - 2)
    else:
        rmsnorm("fng", False)
        pg.phase_begin(XTOK_NAMES[0:2])
        for tt in range(5):
            store_tile(tt)

    pg.emit(nc, es, [f"o{i}" for i in range(8)])
    es.close()
    print("A0", A0, "arena", SB_END - A0, "ops", pg.stats, flush=True)
    return nc, pg


_CACHE = {}


def kernel(**inputs):
    inp = {k: np.asarray(v) for k, v in inputs.items()}
    cstn, ccols, rot = host_consts()
    vecs, vcols = host_vecs(inp)
    key = "prog"
    if key not in _CACHE:
        _CACHE[key] = build_program(ccols, vcols, cstn.shape[1], vecs.shape[1])
    nc, pg = _CACHE[key]
    xp = inp["x_prompt"].astype(np.float32, copy=False)
    xs = inp["x_sample"].astype(np.float32, copy=False)
    in_maps = []
    for c in range(NCORES):
        xin = np.concatenate([xp[c], xs[c * NSEQ:(c + 1) * NSEQ].reshape(TS, D)], axis=0)
        in_maps.append({
            "xin": np.ascontiguousarray(xin),
            "cache": np.ascontiguousarray(inp["cache_conv"][:, c * NSEQ:(c + 1) * NSEQ].reshape(DEPTH, NSEQ * 30, CD)),
            "state": np.ascontiguousarray(inp["state_ret"][:, c * NSEQ:(c + 1) * NSEQ]),
            "w_in": inp["w_in"], "w_out": inp["w_out"], "w_up": inp["w_up"], "w_down": inp["w_down"],
            "cst": cstn, "vecs": vecs, "rot": rot,
        })
    res = run_bass_kernel_spmd(nc, in_maps, core_ids=list(range(NCORES)))
    R = res.results
    y = np.stack([r["y"] for r in R])
    y_prompt = np.ascontiguousarray(y[:, :TP])
    y_sample = np.ascontiguousarray(y[:, TP:].reshape(NCORES * NSEQ, 4, D))
    ncp = np.ascontiguousarray(np.stack([r["ncp"] for r in R], axis=1))
    nrp = np.ascontiguousarray(np.stack([r["nrp"] for r in R], axis=1))
    ncs = np.ascontiguousarray(np.concatenate([r["ncs"] for r in R], axis=1))
    nrs = np.ascontiguousarray(np.concatenate([r["nrs"] for r in R], axis=1))
    return (y_prompt, y_sample, ncp, nrp, ncs, nrs)
```

```python
import numpy as np
from contextlib import ExitStack

import concourse.bass as bass
import concourse.mybir as mybir
from concourse.bass_utils import run_bass_kernel_spmd

F32 = mybir.dt.float32
BF16 = mybir.dt.bfloat16
ALU = mybir.AluOpType
AF = mybir.ActivationFunctionType

NCORES = 8
P = 128
D = 1024
KC = 8
DEPTH = 2
TP = 2048
NSEQ = 16
TS = 64
T = TP + TS
CW = 31
CD = 512
HD = 128
NH = 4
DFF = 4096
EPS = 1e-6
PAST = 16384
TTS = [(0, 512), (512, 512), (1024, 512), (1536, 512), (2048, 64)]
NCHK = 17
NSLOT = 4
GAM = [1.0 - 2.0 ** (-5.0 - h) for h in range(NH)]
SAME_ENGINE_WAR = True


def chunks_of(t0, n):
    return range(t0 // 128, (t0 + n + 127) // 128)


class Op:
    __slots__ = ("eng", "fn", "deps", "dma", "sem", "val", "has_dep", "idx")


class Prog:
    ENGS = ("pe", "dve", "act", "pool", "sp")

    def __init__(self):
        self.ops = []
        self.last_w = {}
        self.readers = {}
        self.dma_last = {}
        self.ranges = {}
        self.tops = {}
        self.pending = {}
        self.pdone = {}

    def register(self, name, lo, hi):
        self.ranges[name] = (lo, hi)
        self.tops[name] = {}

    def phase_begin(self, names):
        for nm in names:
            lo, hi = self.ranges[nm]
            pend = {}
            for other, (l2, h2) in self.ranges.items():
                if other in names or other == nm:
                    continue
                if l2 < hi and lo < h2:
                    for o in self.tops[other].values():
                        pend[o.idx] = o
            self.pending[nm] = list(pend.values())
            self.pdone[nm] = set()

    def add(self, eng, fn, R=(), W=(), dma=None):
        o = Op()
        o.eng = eng
        o.fn = fn
        o.dma = dma
        o.sem = None
        o.val = 0
        o.has_dep = False
        o.idx = len(self.ops)
        deps = {}
        me = "dma" if dma else eng

        def consider(d, kind):
            if d is None:
                return
            de = "dma" if d.dma else d.eng
            if de == me and me != "dma":
                if me == "pe":
                    return
                if kind == "war" and not SAME_ENGINE_WAR:
                    return
            deps[d.idx] = d

        for k in tuple(R) + tuple(W):
            nm = k[0] if isinstance(k, tuple) else k
            if nm in self.pending and me not in self.pdone[nm]:
                if me != "dma":
                    self.pdone[nm].add(me)
                for d in self.pending[nm]:
                    consider(d, "raw")
            if nm in self.tops:
                self.tops[nm][dma if dma else eng] = o
        for k in R:
            consider(self.last_w.get(k), "raw")
        for k in W:
            consider(self.last_w.get(k), "waw")
            for r in self.readers.get(k, ()):
                consider(r, "war")
        if dma:
            consider(self.dma_last.get(dma), "waw")
            self.dma_last[dma] = o
        for k in R:
            self.readers.setdefault(k, []).append(o)
        for k in W:
            self.last_w[k] = o
            self.readers[k] = []
        o.deps = list(deps.values())
        for d in o.deps:
            d.has_dep = True
        if dma:
            o.has_dep = True
        self.ops.append(o)
        return o

    def emit(self, nc, es, final_waits_groups):
        eng_sem = {e: es.enter_context(nc.semaphore("sem_" + e)) for e in ("pe", "dve", "act", "pool")}
        cnt = {e: 0 for e in eng_sem}
        dma_sem = {}
        dma_cnt = {}
        for o in self.ops:
            if o.dma:
                if o.dma not in dma_sem:
                    dma_sem[o.dma] = es.enter_context(nc.semaphore("dsem_" + o.dma))
                    dma_cnt[o.dma] = 0
                dma_cnt[o.dma] += 16
                o.sem = dma_sem[o.dma]
                o.val = dma_cnt[o.dma]
            elif o.has_dep:
                cnt[o.eng] += 1
                o.sem = eng_sem[o.eng]
                o.val = cnt[o.eng]
        block = es.enter_context(nc.Block())
        handles = {"pe": block.tensor, "dve": block.vector, "act": block.scalar, "pool": block.gpsimd, "sp": block.sync}
        for e in self.ENGS:
            ops = [o for o in self.ops if o.eng == e]

            def body(eh, ops=ops, e=e):
                waited = {}
                for o in ops:
                    for d in o.deps:
                        key = d.sem.num
                        if waited.get(key, 0) < d.val:
                            eh.wait_ge(d.sem, d.val)
                            waited[key] = d.val
                    ins = o.fn(eh)
                    if o.sem is not None:
                        ins.then_inc(o.sem, 16 if o.dma else 1)
                if e == "sp":
                    for g in final_waits_groups:
                        if g in dma_sem:
                            eh.wait_ge(dma_sem[g], dma_cnt[g])

            handles[e](body)
        self.stats = {e: sum(1 for o in self.ops if o.eng == e) for e in self.ENGS}


def host_consts():
    f32 = np.float32
    i = np.arange(128)
    cols = {}
    cst = []

    def put(name, arr):
        arr = np.asarray(arr, dtype=f32)
        assert arr.shape[0] == 128
        cols[name] = (sum(a.shape[1] for a in cst), arr.shape[1])
        cst.append(arr)

    put("ident", np.eye(128))
    sc = f32(HD ** -0.5)
    for h in range(NH):
        g = np.float64(f32(np.log(f32(1.0) - f32(2.0) ** f32(-5.0 - h))))
        diff = i[None, :] - i[:, None]
        decT = np.where(diff >= 0, np.exp(g * np.maximum(diff, 0)), 0.0) * sc
        put(f"decT{h}", decT)
        i64 = np.arange(64)
        same = (i64[None, :] // 4) == (i64[:, None] // 4)
        dd = (i64[None, :] % 4) - (i64[:, None] % 4)
        decS = np.where(same & (dd >= 0), np.exp(g * np.maximum(dd, 0)), 0.0) * sc
        a = np.zeros((128, 64)); a[:64] = decS
        put(f"decS{h}", a)
        put(f"gq{h}", np.broadcast_to(np.exp(g * (i + 1.0))[None, :], (128, 128)))
        put(f"gqs{h}", np.broadcast_to(np.exp(g * ((i64 % 4) + 1.0))[None, :], (128, 64)))
        put(f"gdec{h}", (np.exp(g * (127.0 - i)) * sc)[:, None])
        md = np.zeros((128, 16))
        for j in range(64):
            md[j, j // 4] = np.exp(g * (3.0 - (j % 4))) * sc
        put(f"maskdec{h}", md)
    cstn = np.concatenate(cst, axis=1).astype(f32)
    half = HD // 2
    inv = (np.float32(10000.0) ** (-(np.arange(half, dtype=f32)) / f32(half))).astype(f32)
    pos = np.concatenate([np.arange(TP, dtype=f32), np.tile(f32(PAST) + np.arange(4, dtype=f32), NSEQ)]).astype(f32)
    ang = (pos[None, :] * inv[:, None]).astype(f32)
    cos = np.cos(ang.astype(np.float64)).astype(f32)
    sin = np.sin(ang.astype(np.float64)).astype(f32)
    rot = np.zeros((128, 2, T), f32)
    rot[:64, 0] = cos; rot[64:, 0] = cos
    rot[:64, 1] = sin; rot[64:, 1] = -sin
    return cstn, cols, rot


def host_vecs(inp):
    cols = {}
    parts = []

    def put(name, arr):
        cols[name] = (sum(a.shape[1] for a in parts), arr.shape[1])
        parts.append(np.ascontiguousarray(arr, dtype=np.float32))

    for l in range(DEPTH):
        put(f"n1g{l}", inp["norm1_g"][l].reshape(KC, 128).T)
        put(f"n2g{l}", inp["norm2_g"][l].reshape(KC, 128).T)
        cw = inp["conv_w"][l].reshape(CW, 4, 128).transpose(2, 1, 0).reshape(128, 4 * CW)
        put(f"cw{l}", cw)
        put(f"cb{l}", inp["conv_b"][l].reshape(4, 128).T)
        put(f"lng{l}", inp["conv_ln_g"][l].reshape(4, 128).T)
        put(f"lnb{l}", inp["conv_ln_b"][l].reshape(4, 128).T)
        put(f"gng{l}", inp["ret_gn_g"][l].reshape(4, 128).T)
    put("fng", inp["final_norm_g"].reshape(KC, 128).T)
    return np.concatenate(parts, axis=1), cols


def build_program(ccols, vcols, NCST, NVEC, dbg=False):
    import os
    STAGE = int(os.environ.get("KSTAGE", "9"))
    NL = int(os.environ.get("KLAYERS", str(DEPTH)))
    KSUB = int(os.environ.get("KSUB", "9"))
    KNH = int(os.environ.get("KHEADS", str(NH)))
    KPART = int(os.environ.get("KPART", "9"))
    KNC = int(os.environ.get("KNC", "99"))
    nc = bass.Bass("TRN2", target_bir_lowering=False)
    pg = Prog()
    es = ExitStack()

    def dram(name, shape, kind):
        return nc.dram_tensor(name, list(shape), F32, kind=kind).ap()

    xin = dram("xin", [T, D], "ExternalInput")
    cache = dram("cache", [DEPTH, NSEQ * 30, CD], "ExternalInput")
    state = dram("state", [DEPTH, NSEQ, NH, HD, HD], "ExternalInput")
    w_in = dram("w_in", [DEPTH, D, 3072], "ExternalInput")
    w_out = dram("w_out", [DEPTH, D, D], "ExternalInput")
    w_up = dram("w_up", [DEPTH, D, DFF], "ExternalInput")
    w_down = dram("w_down", [DEPTH, DFF, D], "ExternalInput")
    cst_d = dram("cst", [128, NCST], "ExternalInput")
    vec_d = dram("vecs", [128, NVEC], "ExternalInput")
    rot_d = dram("rot", [128, 2, T], "ExternalInput")
    y_d = dram("y", [T, D], "ExternalOutput")
    ncp_d = dram("ncp", [DEPTH, 30, CD], "ExternalOutput")
    nrp_d = dram("nrp", [DEPTH, NH, HD, HD], "ExternalOutput")
    ncs_d = dram("ncs", [DEPTH, NSEQ, 30, CD], "ExternalOutput")
    nrs_d = dram("nrs", [DEPTH, NSEQ, NH, HD, HD], "ExternalOutput")

    SB_END = 229376
    off = [24576]

    def sb(name, shape, dt, at=None):
        nbytes = int(np.prod(shape[1:])) * (4 if dt == F32 else 2)
        nbytes = (nbytes + 31) // 32 * 32
        if at is None:
            o = off[0]
            off[0] += nbytes
        else:
            o = at
        assert o + nbytes <= SB_END, (name, o, nbytes, o + nbytes - SB_END)
        t = nc.alloc_sbuf_tensor_at(name, list(shape), dt, offset=o)
        pg.register(name, o, o + nbytes)
        return t, o + nbytes

    xT, _ = sb("xT", [128, KC, T], F32)
    hb_off = off[0]
    hb, _ = sb("hb", [128, KC, T], BF16)
    ring = [sb(f"ring{i}", [128, KC, 128], BF16)[0] for i in range(NSLOT)]
    cst, _ = sb("cst", [128, NCST], F32)
    vec, _ = sb("vec", [128, NVEC], F32)
    identb, _ = sb("identb", [128, 128], BF16)
    onesb, _ = sb("onesb", [128, 128], BF16)
    epsc, _ = sb("epsc", [128, 8], F32)
    A0 = off[0]

    sq = [sb(f"sq{i}", [128, KC, 512], BF16, at=A0 + i * 8192)[0] for i in range(2)]
    rstd, _ = sb("rstd", [128, T], F32, at=A0 + 16384)
    NXT = 8
    xtok = [sb(f"xtok{i}", [128, D], F32, at=A0 + 25600 + i * 4096)[0] for i in range(NXT)]
    XTOK_NAMES = [f"xtok{i}" for i in range(NXT)]
    a = A0
    qT, a = sb("qT", [128, T], BF16, at=a)
    kT, a = sb("kT", [128, T], BF16, at=a)
    vT, a = sb("vT", [128, T], BF16, at=a)
    gs, a = sb("gs", [128, T], BF16, at=a)
    retout, a = sb("retout", [128, 4, T], BF16, at=a)
    R_END = a
    qdT, a = sb("qdT", [128, T], BF16, at=a)
    rott, a = sb("rott", [128, 2, T], F32, at=a)
    st32, a = sb("st32", [128, NSEQ, 128], F32, at=a)
    stsb, a = sb("stsb", [128, NSEQ, 128], BF16, at=a)
    qmask, a = sb("qmask", [128, NSEQ, 64], BF16, at=a)
    kmask, a = sb("kmask", [128, NSEQ, 128], BF16, at=a)
    k32a, _ = sb("k32a", [128, 512], F32, at=a - 4096)
    tmpKB, _ = sb("tmpKB", [128, 512], F32, at=a - 2048)
    k32b, a = sb("k32b", [128, 512], F32, at=a)
    k32 = [k32a, k32b]
    tmpA, a = sb("tmpA", [128, 512], F32, at=a)
    tmpB, a = sb("tmpB", [128, 512], F32, at=a)
    s32, a = sb("s32", [128, 128], F32, at=a)
    stb, a = sb("stb", [128, 512], BF16, at=a)
    ktok, a = sb("ktok", [128, 512], BF16, at=a)
    vtok, a = sb("vtok", [128, 512], BF16, at=a)
    vdec, a = sb("vdec", [128, 512], BF16, at=a)
    STb, a = sb("STb", [128, 512], BF16, at=a)
    onb, a = sb("onb", [128, 512], BF16, at=a)
    bnst, a = sb("bnst", [128, 32], F32, at=a)
    mv, a = sb("mv", [128, 16], F32, at=a)
    RET_NAMES = ["qT", "kT", "vT", "gs", "retout", "rott", "st32", "stsb", "qmask", "kmask", "tmpA", "tmpB", "s32", "stb",
                 "qdT", "ktok", "vtok", "vdec", "STb", "onb", "bnst", "mv", "k32a", "k32b", "tmpKB"]
    a = R_END
    glub_p, a = sb("glub_p", [128, 4, TP + 32], BF16, at=a)
    glub_s, a = sb("glub_s", [128, 4, NSEQ, 34], BF16, at=a)
    sig_off = a
    sig, a = sb("sig", [128, T], F32, at=a)
    g32p, a = sb("g32p", [128, 4, 32], F32, at=a)
    g32s, a = sb("g32s", [128, 4, 64], F32, at=a)
    ctok_off = a
    ctok, a = sb("ctok", [128, CD], F32, at=a)
    otok, a = sb("otok", [128, CD], F32, at=a)
    cbb, a = sb("cbb", [128, 4, 512], BF16, at=a)
    mean, a = sb("mean", [128, 512], F32, at=a)
    msq, a = sb("msq", [128, 512], F32, at=a)
    crs, a = sb("crs", [128, 512], F32, at=a)
    CONVA_NAMES = ["glub_p", "glub_s", "sig", "g32p", "g32s", "ctok", "otok"]
    convout, _ = sb("convout", [128, 4, T], BF16, at=A0)
    c32, _ = sb("c32", [128, 4, 512], F32, at=sig_off)
    csq, _ = sb("csq", [128, 4, 512], BF16, at=ctok_off)
    dg, _ = sb("dg", [128, 4 * CW, 128], BF16, at=hb_off)
    CONVB_NAMES = ["convout", "c32", "csq", "dg", "cbb", "mean", "msq", "crs"]
    hid = [sb(f"hid{i}", [128, 8, T], BF16, at=A0 + i * 8 * T * 2)[0] for i in range(2)]
    rtmp = [sb(f"rtmp{i}", [128, 512], BF16, at=A0 + 2 * 8 * T * 2 + i * 1024)[0] for i in range(2)]
    FFN_NAMES = ["hid0", "hid1", "rtmp0", "rtmp1"]
    ring2 = [sb(f"ringB{i}", [128, KC, 128], BF16, at=A0 + 2 * 8 * T * 2 + 2048 + i * 2048)[0] for i in range(8)]
    NORM_NAMES = ["sq0", "sq1", "rstd"]

    def cs(name, r0=0, r1=128):
        c0, n = ccols[name]
        return cst[r0:r1, c0:c0 + n]

    def vc(name, j=None):
        c0, n = vcols[name]
        if j is None:
            return vec[:, c0:c0 + n]
        return vec[:, c0 + j:c0 + j + 1]

    ident = cs("ident")

    psb = [es.enter_context(nc.psum_tensor(f"ps{i}", [128, 512], F32)) for i in range(8)]
    NB = [0, 1, 2, 4, 5]

    def PK(b):
        return ("ps", b)

    PK6 = [("ps6", i) for i in range(4)]

    def act(out, in_, func, R, W, bias=None, scale=None):
        kw = {}
        if bias is not None:
            kw["bias"] = bias
        if scale is not None:
            kw["scale"] = scale
        return pg.add("act", lambda e: e.activation(out=out, in_=in_, func=func, **kw), R, W)

    def dve(fn, R, W):
        return pg.add("dve", fn, R, W)

    def mm(out, lhsT, rhs, start, stop, R, W):
        return pg.add("pe", lambda e: e.matmul(out, lhsT, rhs, start=start, stop=stop), R, W)

    def tr(out, in_, idn, R, W):
        return pg.add("pe", lambda e: e.transpose(out, in_, idn), R, W)

    def dma(q, out, in_, R, W, group):
        return pg.add(q, lambda e: e.dma_start(out=out, in_=in_), R, W, dma=group)

    ogrp = [0]

    def out_dma(out, in_, R):
        g = f"o{ogrp[0] % 8}"
        ogrp[0] += 1
        return dma("sp", out, in_, R, (), g)

    mmb = [0]

    def next_bank(banks=(0, 1, 2)):
        b = banks[mmb[0] % len(banks)]
        mmb[0] += 1
        return b

    units = []
    for l in range(NL):
        if STAGE >= 1:
            for h in range(KNH):
                for grp in range(4):
                    c0 = 1024 + grp * 512 + h * 128
                    units.append((w_in[l, :, c0:c0 + 128], 8))
        if STAGE >= 2:
            for cc in range(4):
                units.append((w_in[l, :, 512 + cc * 128:512 + (cc + 1) * 128], 8))
                units.append((w_in[l, :, cc * 128:(cc + 1) * 128], 8))
        if STAGE >= 4:
            for m in range(8):
                units.append((w_out[l, :, m * 128:(m + 1) * 128], 8))
        for g in range(4 if STAGE >= 5 else 0):
            for f in range(8):
                c0 = (g * 8 + f) * 128
                units.append((w_up[l, :, c0:c0 + 128], 8))
            for m in range(8):
                units.append((w_down[l, g * 1024:(g + 1) * 1024, m * 128:(m + 1) * 128], 8))
    TAIL = STAGE >= 5 and NL == DEPTH
    if TAIL:
        tail_units = units[-8:]
        units = units[:-8]
    ucur = [0]
    uload = [0]

    def load_units_upto(i):
        while uload[0] <= i and uload[0] < len(units):
            u = uload[0]
            ap, kcn = units[u]
            s = u % NSLOT
            dma("pool", ring[s][:, 0:kcn, :], ap.rearrange("(kc p) n -> p kc n", p=128), (), [(f"ring{s}",)], f"w{s}")
            uload[0] += 1

    def next_unit(look=NSLOT - 1):
        u = ucur[0]
        ucur[0] += 1
        load_units_upto(u + look)
        return u % NSLOT, units[u][1]

    def proj(rhs_fn, rhs_keys, evac, banks=(0, 1, 2)):
        s, kcn = next_unit()
        for tt, (t0, n) in enumerate(TTS):
            b = next_bank(banks)
            for kc in range(kcn):
                mm(psb[b][:, 0:n], ring[s][:, kc, :], rhs_fn(kc, t0, n), kc == 0, kc == kcn - 1,
                   [(f"ring{s}",)] + rhs_keys(kc, t0, n), [PK(b)])
            evac(tt, t0, n, b)

    def proj2(rhs_fn, rhs_keys, evac_a, evac_b, banks=(0, 1, 2)):
        sa, kcn = next_unit(NSLOT - 1)
        sb_, _ = next_unit(NSLOT - 2)
        for tt, (t0, n) in enumerate(TTS):
            for s_, ev in ((sa, evac_a), (sb_, evac_b)):
                b = next_bank(banks)
                for kc in range(kcn):
                    mm(psb[b][:, 0:n], ring[s_][:, kc, :], rhs_fn(kc, t0, n), kc == 0, kc == kcn - 1,
                       [(f"ring{s_}",)] + rhs_keys(kc, t0, n), [PK(b)])
                ev(tt, t0, n, b)

    def tkeys(name, idx, t0, n):
        return [(name, idx, c) for c in chunks_of(t0, n)]

    dma("sp", cst[:, :], cst_d, (), ["cst"], "c0")
    dma("sp", vec[:, :], vec_d, (), ["vec"], "c1")
    dve(lambda e: e.tensor_copy(out=identb[:, :], in_=ident), ["cst"], ["identb"])
    dve(lambda e: e.memset(onesb[:, :], 1.0), [], ["onesb"])
    dve(lambda e: e.memset(epsc[:, :], EPS), [], ["epsc"])

    pg.phase_begin(XTOK_NAMES)
    for i in range(NCHK):
        r0 = i * 128
        rows = 128 if i < 16 else 64
        s = i % NXT
        dma("sp", xtok[s][0:rows, :], xin[r0:r0 + rows, :], (), [(f"xtok{s}",)], f"x{s}")
        for half in range(2):
            b = next_bank((0, 1, 2, 4))
            for cq in range(4):
                c = half * 4 + cq
                tr(psb[b][:, cq * 128:cq * 128 + rows], xtok[s][0:rows, c * 128:(c + 1) * 128], ident[0:rows, 0:rows],
                   [(f"xtok{s}",), "cst"], [PK(b)])
            src = psb[b][:, :].rearrange("p (c t) -> p c t", c=4)[:, :, 0:rows]
            dst = xT[:, half * 4:half * 4 + 4, r0:r0 + rows]
            W = [("xT", c, i) for c in range(half * 4, half * 4 + 4)]
            if (i + half) % 2 == 0:
                act(dst, src, AF.Copy, [PK(b)], W)
            else:
                dve(lambda e, dst=dst, src=src: e.tensor_copy(out=dst, in_=src), [PK(b)], W)

    def rmsnorm_tile(gname, to_hb, tt):
        t0, n = TTS[tt]
        s = tt % 2
        act(sq[s][:, :, 0:n], xT[:, :, t0:t0 + n], AF.Square, [k for c in range(KC) for k in tkeys("xT", c, t0, n)], [(f"sq{s}",)])
        for c in range(KC):
            mm(psb[NB[tt]][:, 0:n], onesb[:, :], sq[s][:, c, 0:n], c == 0, c == KC - 1, [(f"sq{s}",), "onesb"], [PK(NB[tt])])
        r = rstd[:, t0:t0 + n]
        act(r, psb[NB[tt]][:, 0:n], AF.Ln, [PK(NB[tt]), "epsc"], [("rstd", tt)], bias=epsc[:, 0:1], scale=1.0 / D)
        act(r, r, AF.Exp, [("rstd", tt)], [("rstd", tt)], scale=-0.5)
        for c in range(KC):
            src = xT[:, c, t0:t0 + n]
            if to_hb:
                dst = hb[:, c, t0:t0 + n]
                W = tkeys("hb", c, t0, n)
            else:
                dst = src
                W = tkeys("xT", c, t0, n)
            dve(lambda e, dst=dst, src=src, c=c: e.scalar_tensor_tensor(
                out=dst, in0=src, scalar=vc(gname, c), in1=rstd[:, t0:t0 + n], op0=ALU.mult, op1=ALU.mult),
                tkeys("xT", c, t0, n) + [("rstd", tt), "vec"], W)

    def rmsnorm(gname, to_hb):
        pg.phase_begin(NORM_NAMES + (["hb"] if to_hb else []))
        for tt in range(len(TTS)):
            rmsnorm_tile(gname, to_hb, tt)

    def hb_rhs(kc, t0, n):
        return hb[:, kc, t0:t0 + n]

    def hb_keys(kc, t0, n):
        return tkeys("hb", kc, t0, n)

    for l in range(NL):
        if STAGE < 1:
            break
        rmsnorm(f"n1g{l}", True)

        pg.phase_begin(RET_NAMES)
        dma("sp", rott[:, :, :], rot_d, (), ["rott"], "c2")
        dve(lambda e: e.memset(qmask[:, :, :], 0.0), [], ["qmask"])

        def make_head(h):
            g128 = float(np.float32(GAM[h]) ** 128)
            g4 = float(np.float32(GAM[h]) ** 4)
            dma("sp", st32[:, :, :], state[l, :, h, :, :].rearrange("b d v -> d b v"), (), ["st32"] + [("st32", i) for i in range(4)], "c3")
            u0 = ucur[0]
            ucur[0] += 4
            load_units_upto(u0 + 3)
            slots = [(u0 + i) % NSLOT for i in range(4)]
            pool = lambda fn, R, W: pg.add("pool", fn, R, W)

            def p_mm(ui, tt):
                t0, n = TTS[tt]
                b = next_bank((0, 1))
                s_ = slots[ui]
                for kc in range(KC):
                    mm(psb[b][:, 0:n], ring[s_][:, kc, :], hb[:, kc, t0:t0 + n], kc == 0, kc == KC - 1,
                       [(f"ring{s_}",)] + tkeys("hb", kc, t0, n), [PK(b)])
                return b

            def P_q(tt, h=h):
                t0, n = TTS[tt]
                b = p_mm(0, tt)
                ps = psb[b]
                dve(lambda e: e.tensor_tensor(out=tmpA[:, 0:n], in0=ps[:, 0:n], in1=rott[:, 0, t0:t0 + n], op=ALU.mult),
                    [PK(b), "rott"], ["tmpA"])
                dve(lambda e: e.tensor_tensor(out=tmpB[0:64, 0:n], in0=ps[64:128, 0:n], in1=rott[64:128, 1, t0:t0 + n], op=ALU.mult),
                    [PK(b), "rott"], [("tmpB", 0)])
                dve(lambda e: e.tensor_tensor(out=tmpB[64:128, 0:n], in0=ps[0:64, 0:n], in1=rott[0:64, 1, t0:t0 + n], op=ALU.mult),
                    [PK(b), "rott"], [("tmpB", 1)])
                dve(lambda e: e.tensor_tensor(out=tmpA[:, 0:n], in0=tmpA[:, 0:n], in1=tmpB[:, 0:n], op=ALU.add),
                    ["tmpA", ("tmpB", 0), ("tmpB", 1)], ["tmpA"])
                act(qT[:, t0:t0 + n], tmpA[:, 0:n], AF.Copy, ["tmpA"], tkeys("qT", 0, t0, n))
                if tt < 4:
                    g_ap = cs(f"gq{h}")
                    in1 = bass.AP(g_ap.tensor, g_ap.offset, [[g_ap.ap[0][0], 128], [0, 4], [1, 128]])
                    dve(lambda e: e.tensor_tensor(out=qdT[:, t0:t0 + n].rearrange("p (c i) -> p c i", c=4),
                                                  in0=tmpA[:, 0:n].rearrange("p (c i) -> p c i", c=4), in1=in1, op=ALU.mult),
                        ["tmpA", "cst"], tkeys("qdT", 0, t0, n))
                else:
                    dve(lambda e: e.tensor_tensor(out=qdT[:, t0:t0 + n], in0=tmpA[:, 0:n], in1=cs(f"gqs{h}"), op=ALU.mult),
                        ["tmpA", "cst"], tkeys("qdT", 0, t0, n))

            def P_k(tt):
                t0, n = TTS[tt]
                b = p_mm(1, tt)
                ps = psb[b]
                dve(lambda e: e.tensor_tensor(out=k32b[:, 0:n], in0=ps[:, 0:n], in1=rott[:, 0, t0:t0 + n], op=ALU.mult),
                    [PK(b), "rott"], [("k32b",)])
                dve(lambda e: e.tensor_tensor(out=tmpKB[0:64, 0:n], in0=ps[64:128, 0:n], in1=rott[64:128, 1, t0:t0 + n], op=ALU.mult),
                    [PK(b), "rott"], [("tmpKB", 0)])
                dve(lambda e: e.tensor_tensor(out=tmpKB[64:128, 0:n], in0=ps[0:64, 0:n], in1=rott[0:64, 1, t0:t0 + n], op=ALU.mult),
                    [PK(b), "rott"], [("tmpKB", 1)])
                dve(lambda e: e.tensor_tensor(out=kT[:, t0:t0 + n], in0=k32b[:, 0:n], in1=tmpKB[:, 0:n], op=ALU.add),
                    [("k32b",), ("tmpKB", 0), ("tmpKB", 1)], tkeys("kT", 0, t0, n))

            def P_v(tt):
                t0, n = TTS[tt]
                b = p_mm(2, tt)
                act(vT[:, t0:t0 + n], psb[b][:, 0:n], AF.Copy, [PK(b)], tkeys("vT", 0, t0, n))

            def P_g(tt):
                t0, n = TTS[tt]
                b = p_mm(3, tt)
                act(gs[:, t0:t0 + n], psb[b][:, 0:n], AF.Silu, [PK(b)], tkeys("gs", 0, t0, n))

            def chunk_steps(tt, h=h, g128=g128, g4=g4):
                t0, n = TTS[tt]
                smp = tt == 4
                nck = 1 if smp else 4
                w = 64 if smp else 128
                cl = [4 * tt + ci for ci in range(nck)]
                kq = [("qT", 0, c) for c in cl]
                kk = [("kT", 0, c) for c in cl]
                kv = [("vT", 0, c) for c in cl]
                nn = nck * 128
                ob = 5 if tt % 2 == 0 else 2

                def C1():
                    for ci in range(nck):
                        cols = slice(t0 + ci * 128, t0 + ci * 128 + w)
                        mm(psb[3][0:w, ci * 128:(ci + 1) * 128], kT[:, cols], identb[:, :], True, True, kk + ["identb"], [PK(3)])
                    for ci in range(nck):
                        cols = slice(t0 + ci * 128, t0 + ci * 128 + w)
                        mm(psb[7][0:w, ci * 128:(ci + 1) * 128], vT[:, cols], identb[:, :], True, True, kv + ["identb"], [PK(7)])
                    act(ktok[0:w, 0:nn], psb[3][0:w, 0:nn], AF.Copy, [PK(3)], ["ktok"])
                    if not smp:
                        act(vdec[:, 0:nn], psb[7][:, 0:nn], AF.Identity, [PK(7), "cst"], ["vdec"], scale=cs(f"gdec{h}"))
                    act(vtok[0:w, 0:nn], psb[7][0:w, 0:nn], AF.Copy, [PK(7)], ["vtok"])

                def C2():
                    for ci in range(nck):
                        cols = slice(t0 + ci * 128, t0 + ci * 128 + w)
                        mm(psb[4][0:w, ci * 128:ci * 128 + w], kT[:, cols], qT[:, cols], True, True, kk + kq, [PK(4)])
                    if not smp:
                        d_ap = cs(f"decT{h}")
                        dec4 = bass.AP(d_ap.tensor, d_ap.offset, [[d_ap.ap[0][0], 128], [0, 4], [1, 128]])
                        dve(lambda e: e.tensor_tensor(out=STb[:, 0:512].rearrange("p (c i) -> p c i", c=4),
                                                      in0=psb[4][:, 0:512].rearrange("p (c i) -> p c i", c=4), in1=dec4, op=ALU.mult),
                            [PK(4), "cst"], ["STb"])
                    else:
                        dve(lambda e: e.tensor_tensor(out=STb[0:64, 0:64], in0=psb[4][0:64, 0:64], in1=cs(f"decS{h}", 0, 64), op=ALU.mult),
                            [PK(4), "cst"], ["STb"])

                def o_mms(ci):
                    c = cl[ci]
                    cols = slice(t0 + ci * 128, t0 + ci * 128 + 128)
                    oreg = psb[ob][:, ci * 128:(ci + 1) * 128]
                    mm(oreg, STb[:, ci * 128:(ci + 1) * 128], vtok[:, ci * 128:(ci + 1) * 128], True, c == 0, ["STb", "vtok"], [PK(ob)])
                    if c > 0:
                        pv = (ci - 1) % 4
                        mm(oreg, qdT[:, cols], stb[:, pv * 128:(pv + 1) * 128], False, True, [("qdT", 0, c), ("stb", pv)], [PK(ob)])

                def C3():
                    if not smp:
                        for ci in range(nck):
                            mm(psb[6][:, ci * 128:(ci + 1) * 128], ktok[:, ci * 128:(ci + 1) * 128], vdec[:, ci * 128:(ci + 1) * 128], True, True,
                               ["ktok", "vdec"], PK6)
                        o_mms(0)
                        for ci in range(nck):
                            c = cl[ci]
                            ureg = psb[6][:, ci * 128:(ci + 1) * 128]
                            sreg = stb[:, ci * 128:(ci + 1) * 128]
                            if c == 0:
                                dve(lambda e, ureg=ureg, sreg=sreg: e.tensor_copy(out=sreg, in_=ureg), PK6, [("stb", ci)])
                                dve(lambda e, ureg=ureg: e.tensor_copy(out=s32[:, :], in_=ureg), PK6, ["s32"])
                            else:
                                if c < 15:
                                    dve(lambda e, ureg=ureg, sreg=sreg: e.scalar_tensor_tensor(
                                        out=sreg, in0=s32[:, :], scalar=g128, in1=ureg, op0=ALU.mult, op1=ALU.add), PK6 + ["s32"], [("stb", ci)])
                                dve(lambda e, ureg=ureg: e.scalar_tensor_tensor(out=s32[:, :], in0=s32[:, :], scalar=g128, in1=ureg,
                                                                                op0=ALU.mult, op1=ALU.add), PK6 + ["s32"], ["s32"])
                            if c == 15:
                                out_dma(nrp_d[l, h, :, :], s32[:, :], ["s32"])
                    else:
                        mm(psb[ob][0:64, 0:128], STb[0:64, 0:64], vtok[0:64, 0:128], True, False, ["STb", "vtok"], [PK(ob)])
                        qm_dst = bass.AP(qmask, 0, [[NSEQ * 64, 128], [68, NSEQ], [1, 4]])
                        dve(lambda e: e.tensor_copy(out=qm_dst, in_=qdT[:, t0:t0 + 64].rearrange("p (b i) -> p b i", i=4)),
                            [("qdT", 0, 16), "qmask"], ["qmask"])
                        act(stsb[:, :, :], st32[:, :, :], AF.Copy, ["st32"] + [("st32", i) for i in range(4)], ["stsb"])
                        for bq in range(NSEQ):
                            mm(psb[ob][0:64, 0:128], qmask[:, bq, :], stsb[:, bq, :], False, bq == NSEQ - 1, ["qmask", "stsb"], [PK(ob)])

                def C4():
                    if not smp:
                        for ci in range(1, nck):
                            o_mms(ci)
                    else:
                        pg.phase_begin(["kmask"])
                        km_in0 = bass.AP(ktok, 0, [[512, 64], [0, NSEQ], [1, 128]])
                        md = cs(f"maskdec{h}", 0, 64)
                        km_in1 = bass.AP(md.tensor, md.offset, [[md.ap[0][0], 64], [1, NSEQ], [0, 128]])
                        dve(lambda e: e.tensor_tensor(out=kmask[0:64, :, :], in0=km_in0, in1=km_in1, op=ALU.mult), ["ktok", "cst"], ["kmask"])
                        for bq4 in range(4):
                            ub = 6 if bq4 % 2 == 0 else 3
                            ukeys = PK6 if ub == 6 else [PK(3)]
                            for bi in range(4):
                                bq = bq4 * 4 + bi
                                mm(psb[ub][:, bi * 128:(bi + 1) * 128], kmask[0:64, bq, :], vtok[0:64, 0:128], True, True, ["kmask", "vtok"], ukeys)
                            dve(lambda e, bq4=bq4, ub=ub: e.scalar_tensor_tensor(
                                out=st32[:, bq4 * 4:bq4 * 4 + 4, :], in0=st32[:, bq4 * 4:bq4 * 4 + 4, :], scalar=g4,
                                in1=psb[ub][:, :].rearrange("p (b v) -> p b v", b=4), op0=ALU.mult, op1=ALU.add),
                                ukeys + [("st32", bq4)], [("st32", bq4)])
                        out_dma(nrs_d[l, :, h, :, :].rearrange("b d v -> d b v"), st32[:, :, :], ["st32"] + [("st32", i) for i in range(4)])

                def G1():
                    for ci in range(nck):
                        dve(lambda e, ci=ci: e.bn_stats(out=bnst[0:w, ci * 6:ci * 6 + 6], in_=psb[ob][0:w, ci * 128:(ci + 1) * 128]), [PK(ob)], [("bnst", ci)])
                    for ci in range(nck):
                        dve(lambda e, ci=ci: e.bn_aggr(out=mv[0:w, ci * 2:ci * 2 + 2], in_=bnst[0:w, ci * 6:ci * 6 + 6]), [("bnst", ci)], [("mv", ci)])
                    mvk = [("mv", ci) for ci in range(nck)]
                    var_ap = bass.AP(mv, 1, [[16, w], [2, nck]])
                    act(mv[0:w, 8:8 + nck], var_ap, AF.Sqrt, mvk + ["epsc"], ["rs"], bias=epsc[0:w, 0:1])

                def G2():
                    dve(lambda e: e.reciprocal(out=mv[0:w, 12:12 + nck], in_=mv[0:w, 8:8 + nck]), ["rs"], ["rs2"])
                    for ci in range(nck):
                        dve(lambda e, ci=ci: e.tensor_scalar(out=onb[0:w, ci * 128:(ci + 1) * 128], in0=psb[ob][0:w, ci * 128:(ci + 1) * 128],
                                                             scalar1=mv[0:w, 2 * ci:2 * ci + 1], scalar2=mv[0:w, 12 + ci:13 + ci],
                                                             op0=ALU.subtract, op1=ALU.mult), [PK(ob), ("mv", ci), "rs2"], [("onb", ci)])

                def C5():
                    for ci in range(nck):
                        mm(psb[4][:, ci * 128:ci * 128 + w], onb[0:w, ci * 128:(ci + 1) * 128], identb[0:w, 0:w], True, True, [("onb", ci), "identb"], [PK(4)])
                    dve(lambda e, gcol=vc(f"gng{l}", h): e.scalar_tensor_tensor(out=retout[:, h, t0:t0 + n], in0=psb[4][:, 0:n], scalar=gcol,
                                                                                in1=gs[:, t0:t0 + n], op0=ALU.mult, op1=ALU.mult),
                        [PK(4), "vec"] + [("gs", 0, c) for c in cl], [("retout", h, c) for c in cl])

                return [C1, C2, C3, C4, G1, G2, C5]

            return dict(P_q=P_q, P_k=P_k, P_v=P_v, P_g=P_g, chunk_steps=chunk_steps, u0=u0)

        nothing = lambda: None
        ctx = None
        for h in range(KNH):
            if ctx is None:
                ctx = make_head(h)
                pg.phase_begin(["k32a", "tmpKB"])
                for fn in ("P_q", "P_k", "P_v", "P_g"):
                    ctx[fn](0)
                ctx["gdone"] = {0}
            nxt = None
            prev = None
            for tt in range(6):
                cur = ctx["chunk_steps"](tt) if tt < 5 else None
                C1, C2, C3, C4 = cur[0:4] if cur else [nothing] * 4
                G1, G2, G3 = prev[4:7] if prev else [nothing] * 3
                if tt + 1 < 5:
                    pc, pt = ctx, tt + 1
                elif tt == 5 and h + 1 < KNH:
                    G1()
                    G1 = nothing
                    nxt = make_head(h + 1)
                    pg.phase_begin(["k32a", "tmpKB"])
                    pc, pt = nxt, 0
                else:
                    pc = None
                G1()
                C2()
                C1()
                if pc:
                    pc["P_q"](pt)
                G2()
                C3()
                if pc:
                    pc["P_k"](pt)
                C4()
                if pc:
                    pc["P_v"](pt)
                G3()
                if pc:
                    gd = pc.setdefault("gdone", set())
                    for t2 in (pt, pt + 1):
                        if t2 < 5 and t2 not in gd:
                            pc["P_g"](t2)
                            gd.add(t2)
                if tt == 3:
                    load_units_upto(ctx["u0"] + 7)
                prev = cur
            ctx = nxt

        if STAGE < 2:
            continue
        pg.phase_begin(CONVA_NAMES)
        for cc in range(4):
            dve(lambda e, cc=cc: e.memset(glub_p[:, cc, 0:30], 0.0), [], [("glub_p", cc, "z")])
        for rt in range(4):
            dma("sp", ctok[0:120, :], cache[l, rt * 120:(rt + 1) * 120, :], (), ["ctok"], "c4")
            for cc in range(4):
                tr(psb[6][:, cc * 120:(cc + 1) * 120], ctok[0:120, cc * 128:(cc + 1) * 128], ident[0:120, 0:120], ["ctok", "cst"], PK6)
            act(glub_s[:, :, rt * 4:rt * 4 + 4, 0:30], psb[6][:, 0:480].rearrange("p (c s i) -> p c s i", c=4, s=4),
                AF.Copy, PK6, [("glub_s", "c", rt)])
        out_dma(ncs_d[l, :, 0:26, :], cache[l, :, :].rearrange("(b r) c -> b r c", r=30)[:, 4:30, :], [])

        for cc in range(4):
            proj(hb_rhs, hb_keys, lambda tt, t0, n, b: act(sig[:, t0:t0 + n], psb[b][:, 0:n], AF.Sigmoid, [PK(b)], [("sig", tt)]))

            def glu_evac(tt, t0, n, b, cc=cc):
                if tt < 4:
                    dve(lambda e: e.tensor_tensor(out=glub_p[:, cc, 30 + t0:30 + t0 + n], in0=psb[b][:, 0:n], in1=sig[:, t0:t0 + n], op=ALU.mult),
                        [PK(b), ("sig", tt)], [("glub_p", cc, tt)])
                    if tt == 3:
                        dve(lambda e: e.tensor_tensor(out=g32p[:, cc, 0:30], in0=psb[b][:, 482:512], in1=sig[:, TP - 30:TP], op=ALU.mult),
                            [PK(b), ("sig", tt)], [("g32p", cc)])
                else:
                    dve(lambda e: e.tensor_tensor(out=g32s[:, cc, :], in0=psb[b][:, 0:64], in1=sig[:, TP:T], op=ALU.mult),
                        [PK(b), ("sig", tt)], [("g32s", cc)])
                    act(glub_s[:, cc, :, 30:34], g32s[:, cc, :].rearrange("p (s i) -> p s i", i=4), AF.Copy, [("g32s", cc)], [("glub_s", "n", cc)])
            proj(hb_rhs, hb_keys, glu_evac)

        for cc in range(4):
            tr(psb[6][0:30, cc * 128:(cc + 1) * 128], g32p[:, cc, 0:30], ident, [("g32p", cc), "cst"], PK6)
        act(otok[0:30, :], psb[6][0:30, :], AF.Copy, PK6, ["otok"])
        out_dma(ncp_d[l, :, :], otok[0:30, :], ["otok"])
        for cc in range(4):
            tr(psb[6][0:64, cc * 128:(cc + 1) * 128], g32s[:, cc, :], ident, [("g32s", cc), "cst"], PK6)
        act(otok[0:64, :], psb[6][0:64, :], AF.Copy, PK6, ["otok"])
        for bq in range(NSEQ):
            out_dma(ncs_d[l, bq, 26:30, :], otok[bq * 4:bq * 4 + 4, :], ["otok"])

        if STAGE < 3:
            continue
        pg.phase_begin(CONVB_NAMES)
        for cc in range(4):
            c0, _ = vcols[f"cw{l}"]
            in0 = bass.AP(identb, 0, [[128, 128], [0, CW], [1, 128]])
            w_ap = vec[:, c0 + cc * CW:c0 + (cc + 1) * CW]
            in1 = bass.AP(w_ap.tensor, w_ap.offset, [[w_ap.ap[0][0], 128], [1, CW], [0, 128]])
            dve(lambda e, cc=cc, in0=in0, in1=in1: e.tensor_tensor(out=dg[:, cc * CW:(cc + 1) * CW, :], in0=in0, in1=in1, op=ALU.mult),
                ["identb", "vec"], [("dg", cc)])

        for tt, (t0, n) in enumerate(TTS):
            for cc in range(4):
                b = next_bank((0, 1, 2))
                for j in range(CW):
                    if tt < 4:
                        rhs = glub_p[:, cc, t0 + j:t0 + j + 512]
                        outp = psb[b][:, 0:512]
                        R = [("glub_p", cc, tt), ("glub_p", cc, "z" if tt == 0 else tt - 1)]
                    else:
                        rhs = glub_s[:, cc, :, j:j + 4]
                        outp = psb[b][:, 0:64].rearrange("p (s i) -> p s i", i=4)
                        R = [("glub_s", "n", cc)] + [("glub_s", "c", rt) for rt in range(4)]
                    mm(outp, dg[:, cc * CW + j, :], rhs, j == 0, j == CW - 1, R + [("dg", cc)], [PK(b)])
                act(c32[:, cc, 0:n], psb[b][:, 0:n], AF.Identity, [PK(b), "vec"], [("c32", cc)], bias=vc(f"cb{l}", cc))
                act(csq[:, cc, 0:n], psb[b][:, 0:n], AF.Square, [PK(b), "vec"], [("csq", cc)], bias=vc(f"cb{l}", cc))
                dve(lambda e, cc=cc, n=n: e.tensor_copy(out=cbb[:, cc, 0:n], in_=c32[:, cc, 0:n]), [("c32", cc)], [("cbb", cc)])
            for cc in range(4):
                mm(psb[4][:, 0:n], onesb[:, :], cbb[:, cc, 0:n], cc == 0, cc == 3, [("cbb", cc), "onesb"], [PK(4)])
            for cc in range(4):
                mm(psb[5][:, 0:n], onesb[:, :], csq[:, cc, 0:n], cc == 0, cc == 3, [("csq", cc), "onesb"], [PK(5)])
            dve(lambda e, n=n: e.tensor_scalar(out=mean[:, 0:n], in0=psb[4][:, 0:n], scalar1=1.0 / CD, scalar2=0.0, op0=ALU.mult, op1=ALU.add), [PK(4)], ["mean"])
            dve(lambda e, n=n: e.tensor_tensor(out=msq[:, 0:n], in0=mean[:, 0:n], in1=mean[:, 0:n], op=ALU.mult), ["mean"], ["msq"])
            dve(lambda e, n=n: e.scalar_tensor_tensor(out=crs[:, 0:n], in0=psb[5][:, 0:n], scalar=1.0 / CD, in1=msq[:, 0:n],
                                                      op0=ALU.mult, op1=ALU.subtract), [PK(5), "msq"], ["crs"])
            act(crs[:, 0:n], crs[:, 0:n], AF.Ln, ["crs", "epsc"], ["crs"], bias=epsc[:, 0:1])
            act(crs[:, 0:n], crs[:, 0:n], AF.Exp, ["crs"], ["crs"], scale=-0.5)
            for cc in range(4):
                dve(lambda e, cc=cc, n=n: e.tensor_tensor(out=c32[:, cc, 0:n], in0=c32[:, cc, 0:n], in1=mean[:, 0:n], op=ALU.subtract),
                    [("c32", cc), "mean"], [("c32", cc)])
                dve(lambda e, cc=cc, n=n: e.tensor_tensor(out=c32[:, cc, 0:n], in0=c32[:, cc, 0:n], in1=crs[:, 0:n], op=ALU.mult),
                    [("c32", cc), "crs"], [("c32", cc)])
                act(convout[:, cc, t0:t0 + n], c32[:, cc, 0:n], AF.Silu, [("c32", cc), "vec"], tkeys("convout", cc, t0, n),
                    bias=vc(f"lnb{l}", cc), scale=vc(f"lng{l}", cc))

        if STAGE < 4:
            continue
        def mix_rhs(kc, t0, n):
            return convout[:, kc, t0:t0 + n] if kc < 4 else retout[:, kc - 4, t0:t0 + n]

        def mix_keys(kc, t0, n):
            return tkeys("convout", kc, t0, n) if kc < 4 else tkeys("retout", kc - 4, t0, n)

        for m in range(8):
            def res_evac(tt, t0, n, b, m=m):
                dve(lambda e: e.tensor_tensor(out=xT[:, m, t0:t0 + n], in0=psb[b][:, 0:n], in1=xT[:, m, t0:t0 + n], op=ALU.add),
                    [PK(b)] + tkeys("xT", m, t0, n), tkeys("xT", m, t0, n))
            proj(mix_rhs, mix_keys, res_evac)

        if STAGE < 5:
            continue
        rmsnorm(f"n2g{l}", True)
        pg.phase_begin(FFN_NAMES)
        for g in range(4):
            hs = g % 2
            for f in range(8):
                def up_evac(tt, t0, n, b, f=f, hs=hs):
                    rs = (f + tt) % 2
                    act(rtmp[rs][:, 0:n], psb[b][:, 0:n], AF.Relu, [PK(b)], [(f"rtmp{rs}",)])
                    dve(lambda e: e.tensor_tensor(out=hid[hs][:, f, t0:t0 + n], in0=rtmp[rs][:, 0:n], in1=rtmp[rs][:, 0:n], op=ALU.mult),
                        [(f"rtmp{rs}",)], tkeys(f"hid{hs}", f, t0, n))
                proj(hb_rhs, hb_keys, up_evac)
            if TAIL and l == NL - 1 and g == 3:
                break
            for m in range(8):
                def dn_evac(tt, t0, n, b, m=m):
                    dve(lambda e: e.tensor_tensor(out=xT[:, m, t0:t0 + n], in0=psb[b][:, 0:n], in1=xT[:, m, t0:t0 + n], op=ALU.add),
                        [PK(b)] + tkeys("xT", m, t0, n), tkeys("xT", m, t0, n))
                proj(lambda kc, t0, n, hs=hs: hid[hs][:, kc, t0:t0 + n], lambda kc, t0, n, hs=hs: tkeys(f"hid{hs}", kc, t0, n), dn_evac)

    def store_chunk(i):
        r0 = i * 128
        rows = 128 if i < 16 else 64
        s = i % 2
        for half in range(2):
            b = next_bank((0, 1, 2, 4))
            for cq in range(4):
                c = half * 4 + cq
                tr(psb[b][0:rows, cq * 128:(cq + 1) * 128], xT[:, c, r0:r0 + rows], ident, [("xT", c, i), "cst"], [PK(b)])
            dst = xtok[s][0:rows, half * 512:(half + 1) * 512]
            if half == 0:
                act(dst, psb[b][0:rows, :], AF.Copy, [PK(b)], [(f"xtok{s}", half)])
            else:
                dve(lambda e, dst=dst, b=b, rows=rows: e.tensor_copy(out=dst, in_=psb[b][0:rows, :]), [PK(b)], [(f"xtok{s}", half)])
        out_dma(y_d[r0:r0 + rows, :], xtok[s][0:rows, :], [(f"xtok{s}", 0), (f"xtok{s}", 1)])

    def store_tile(tt):
        for i in (range(4 * tt, 4 * tt + 4) if tt < 4 else [16]):
            store_chunk(i)

    if TAIL:
        l = NL - 1
        pg.phase_begin([f"ringB{i}" for i in range(8)])
        for m in range(8):
            ap, kcn = tail_units[m]
            dma("pool", ring2[m][:, :, :], ap.rearrange("(kc p) n -> p kc n", p=128), (), [(f"ringB{m}",)], f"wB{m}")
        pg.phase_begin(NORM_NAMES + XTOK_NAMES[0:2])
        for step in range(7):
            if step < 5:
                t0, n = TTS[step]
                for m in range(8):
                    b = next_bank((0, 1, 2))
                    for kc in range(KC):
                        mm(psb[b][:, 0:n], ring2[m][:, kc, :], hid[1][:, kc, t0:t0 + n], kc == 0, kc == KC - 1,
                           [(f"ringB{m}",)] + tkeys("hid1", kc, t0, n), [PK(b)])
                    dve(lambda e, m=m, b=b, t0=t0, n=n: e.tensor_tensor(out=xT[:, m, t0:t0 + n], in0=psb[b][:, 0:n], in1=xT[:, m, t0:t0 + n], op=ALU.add),
                        [PK(b)] + tkeys("xT", m, t0, n), tkeys("xT", m, t0, n))
            if 1 <= step <= 5:
                rmsnorm_tile("fng", False, step - 1)
            if 2 <= step <= 6:
                store_tile(step - 2)
    else:
        rmsnorm("fng", False)
        pg.phase_begin(XTOK_NAMES[0:2])
        for tt in range(5):
            store_tile(tt)

    pg.emit(nc, es, [f"o{i}" for i in range(8)])
    es.close()
    print("A0", A0, "arena", SB_END - A0, "ops", pg.stats, flush=True)
    return nc, pg


_CACHE = {}


def kernel(**inputs):
    inp = {k: np.asarray(v) for k, v in inputs.items()}
    cstn, ccols, rot = host_consts()
    vecs, vcols = host_vecs(inp)
    key = "prog"
    if key not in _CACHE:
        _CACHE[key] = build_program(ccols, vcols, cstn.shape[1], vecs.shape[1])
    nc, pg = _CACHE[key]
    xp = inp["x_prompt"].astype(np.float32, copy=False)
    xs = inp["x_sample"].astype(np.float32, copy=False)
    in_maps = []
    for c in range(NCORES):
        xin = np.concatenate([xp[c], xs[c * NSEQ:(c + 1) * NSEQ].reshape(TS, D)], axis=0)
        in_maps.append({
            "xin": np.ascontiguousarray(xin),
            "cache": np.ascontiguousarray(inp["cache_conv"][:, c * NSEQ:(c + 1) * NSEQ].reshape(DEPTH, NSEQ * 30, CD)),
            "state": np.ascontiguousarray(inp["state_ret"][:, c * NSEQ:(c + 1) * NSEQ]),
            "w_in": inp["w_in"], "w_out": inp["w_out"], "w_up": inp["w_up"], "w_down": inp["w_down"],
            "cst": cstn, "vecs": vecs, "rot": rot,
        })
    res = run_bass_kernel_spmd(nc, in_maps, core_ids=list(range(NCORES)))
    R = res.results
    y = np.stack([r["y"] for r in R])
    y_prompt = np.ascontiguousarray(y[:, :TP])
    y_sample = np.ascontiguousarray(y[:, TP:].reshape(NCORES * NSEQ, 4, D))
    ncp = np.ascontiguousarray(np.stack([r["ncp"] for r in R], axis=1))
    nrp = np.ascontiguousarray(np.stack([r["nrp"] for r in R], axis=1))
    ncs = np.ascontiguousarray(np.concatenate([r["ncs"] for r in R], axis=1))
    nrs = np.ascontiguousarray(np.concatenate([r["nrs"] for r in R], axis=1))
    return (y_prompt, y_sample, ncp, nrp, ncs, nrs)
```
